# Optimizing a Trainium2 kernel written in Bass

```python
import jax, jax.numpy as jnp
from jax import lax
import numpy as np

D_MODEL = 2048
BATCH = 1
SEQ = 8192
DEPTH = 2
DEC_BATCH = 32
DEC_SEQ = 4
PAST_LEN = 8192
PAGE_SIZE = 128

N_A_LAYERS = DEPTH // 2
N_B_LAYERS = DEPTH - N_A_LAYERS
HGRN_EXPAND = 128
HGRN_HEADS = D_MODEL // HGRN_EXPAND
HGRN_K = HGRN_EXPAND
HGRN_V = D_MODEL // HGRN_HEADS
HGRN_DK = HGRN_HEADS * HGRN_K
HGRN_DV = HGRN_HEADS * HGRN_V
GLA_CHUNK = 64
HEAD_DIM = 128
N_HEADS = D_MODEL // HEAD_DIM
N_KV_HEADS = 4
GROUP = N_HEADS // N_KV_HEADS
MOBA_BLOCK = 256
MOBA_TOPK = 3
QUERY_TOKENS = 64
SCALE = HEAD_DIM ** -0.5
D_FF = ((8 * D_MODEL + 3 * 256 - 1) // (3 * 256)) * 256
EPS = 1e-6

kernel_name = 'yoco_hgrn2_moba_decode_step'


def _rmsnorm(x, g):
    xf = x.astype(jnp.float32)
    r = lax.rsqrt(jnp.mean(xf * xf, axis=-1, keepdims=True) + EPS)
    return (xf * r * g.astype(jnp.float32)).astype(x.dtype)


def _divisor_at_most(n, cap):
    c = max(1, min(n, cap))
    while n % c:
        c -= 1
    return c


def _query_chunk(lq, b):
    cap = max(1, QUERY_TOKENS // b)
    c = 1
    while c * 2 <= cap and lq % (c * 2) == 0:
        c *= 2
    return c


def _swiglu(x, g, w_gu, w_down):
    a, u = jnp.split(_rmsnorm(x, g) @ w_gu, 2, axis=-1)
    return (jax.nn.silu(a) * u) @ w_down


def _gla_chunked(q, k, v, log_f, s0):
    b_, l_, h_, _ = q.shape
    dv = v.shape[-1]
    c = _divisor_at_most(l_, GLA_CHUNK)
    n = l_ // c

    def to_chunks(a):
        return a.reshape(b_, n, c, h_, a.shape[-1]).transpose(1, 0, 3, 2, 4)

    qc, kc, vc, gc = to_chunks(q), to_chunks(k), to_chunks(v), to_chunks(log_f)
    cum = jnp.cumsum(gc, axis=3)
    mid = (c - 1) // 2
    ref = cum[:, :, :, mid:mid + 1]
    last = cum[:, :, :, c - 1:c]
    att = jnp.einsum('nbhtk,nbhsk->nbhts', qc * jnp.exp(cum - ref), kc * jnp.exp(ref - cum))
    att = jnp.where(jnp.tril(jnp.ones((c, c), dtype=bool)), att, 0.0)
    o_intra = jnp.einsum('nbhts,nbhsv->nbhtv', att, vc)
    q_in = qc * jnp.exp(cum)
    k_out = kc * jnp.exp(last - cum)
    decay = jnp.exp(last[:, :, :, 0])

    def step(s, xs):
        q_i, k_i, v_i, d_i = xs
        o_i = jnp.einsum('bhtk,bhkv->bhtv', q_i, s)
        s = d_i[..., None] * s + jnp.einsum('bhtk,bhtv->bhkv', k_i, v_i)
        return s, o_i

    s_fin, o_inter = lax.scan(step, s0, (q_in, k_out, vc, decay))
    o = (o_intra + o_inter).transpose(1, 0, 3, 2, 4).reshape(b_, l_, h_, dv)
    return o, s_fin


def _hgrn2_mixer(x, s0, norm_g, w_in, lb, onorm_g, w_out):
    b_, l_, _ = x.shape
    f32 = jnp.float32
    zq, zf, zi, zg = jnp.split(_rmsnorm(x, norm_g) @ w_in, 4, axis=-1)
    hk = (b_, l_, HGRN_HEADS, HGRN_K)
    hv = (b_, l_, HGRN_HEADS, HGRN_V)
    lb = lb.reshape(HGRN_HEADS, HGRN_K)
    f = lb + (1.0 - lb) * jax.nn.sigmoid(zf.astype(f32).reshape(hk))
    q = jax.nn.silu(zq.astype(f32).reshape(hk))
    o, s_fin = _gla_chunked(q, 1.0 - f, zi.astype(f32).reshape(hv), jnp.log(f), s0.astype(f32))
    o = _rmsnorm(o, onorm_g) * jax.nn.silu(zg.astype(f32).reshape(hv))
    return o.reshape(b_, l_, HGRN_DV).astype(x.dtype) @ w_out, s_fin.astype(s0.dtype)


def _moba(q, k, v, q_pos):
    b_, lq, _, _ = q.shape
    t_ = k.shape[1]
    f32 = jnp.float32
    nb = -(-t_ // MOBA_BLOCK)
    pad = nb * MOBA_BLOCK - t_
    k = jnp.pad(k, ((0, 0), (0, pad), (0, 0), (0, 0)))
    v = jnp.pad(v, ((0, 0), (0, pad), (0, 0), (0, 0)))
    kb = k.reshape(b_, nb, MOBA_BLOCK, N_KV_HEADS, HEAD_DIM)
    vb = v.reshape(b_, nb, MOBA_BLOCK, N_KV_HEADS, HEAD_DIM)
    kmean = jnp.mean(kb.astype(f32), axis=2)
    kb_t = kb.transpose(0, 3, 1, 2, 4)
    vb_t = vb.transpose(0, 3, 1, 2, 4)
    k_sel = min(MOBA_TOPK, nb)
    n_q = _query_chunk(lq, b_)
    n_chunks = lq // n_q
    qg = q.reshape(b_, n_chunks, n_q, N_KV_HEADS, GROUP, HEAD_DIM).transpose(1, 0, 2, 3, 4, 5)
    pos = q_pos.reshape(n_chunks, n_q)
    b_ix = jnp.arange(b_)[:, None, None, None, None]
    kv_ix = jnp.arange(N_KV_HEADS)[None, None, :, None, None]

    def attend(args):
        q_c, p_c = args
        qf = q_c.astype(f32)
        qblk = p_c // MOBA_BLOCK
        gate = jnp.einsum('bqkgd,bnkd->bqkgn', qf, kmean)
        full_past = jnp.arange(nb)[None, :] < qblk[:, None]
        gate = jnp.where(full_past[None, :, None, None, :], gate, -jnp.inf)
        _, top = lax.top_k(gate, k_sel)
        own = jnp.broadcast_to(qblk[None, :, None, None, None], top.shape[:-1] + (1,))
        blk = jnp.concatenate([top, own.astype(top.dtype)], axis=-1)
        rank_ok = jnp.concatenate([jnp.arange(k_sel)[None, :] < qblk[:, None],
                                   jnp.ones((n_q, 1), dtype=bool)], axis=-1)
        kg = kb_t[b_ix, kv_ix, blk].astype(f32)
        vg = vb_t[b_ix, kv_ix, blk].astype(f32)
        s = jnp.einsum('bqkgd,bqkgrpd->bqkgrp', qf, kg) * SCALE
        kpos = blk[..., None] * MOBA_BLOCK + jnp.arange(MOBA_BLOCK)
        mask = rank_ok[None, :, None, None, :, None] & (kpos <= p_c[None, :, None, None, None, None])
        s = jnp.where(mask, s, -jnp.inf)
        p = jax.nn.softmax(s.reshape(s.shape[:-2] + (-1,)), axis=-1).reshape(s.shape)
        return jnp.einsum('bqkgrp,bqkgrpd->bqkgd', p, vg)

    out = lax.map(attend, (qg, pos))
    return out.transpose(1, 0, 2, 3, 4, 5).reshape(b_, lq, N_HEADS, HEAD_DIM).astype(q.dtype)


def _trunk(x, s0, k_past, v_past, norm_mix_a, w_in_a, lb_all, onorm_a, w_out_a, norm_kv, w_kv, k_norm,
           norm_mix_b, w_q_b, q_norm, w_o_b, norm_ffn, w_gate_up, w_down):
    b_, l_, _ = x.shape
    pos0 = 0 if k_past is None else k_past.shape[1]
    q_pos = pos0 + jnp.arange(l_, dtype=jnp.int32)
    h = x
    states = []
    k_new = v_new = k_all = v_all = None
    for layer in range(DEPTH):
        if layer < N_A_LAYERS:
            o, s = _hgrn2_mixer(h, s0[layer], norm_mix_a[layer], w_in_a[layer], lb_all[layer],
                                onorm_a[layer], w_out_a[layer])
            h = h + o
            states.append(s)
        else:
            if layer == N_A_LAYERS:
                k_new, v_new = jnp.split(_rmsnorm(h, norm_kv) @ w_kv, 2, axis=-1)
                k_new = _rmsnorm(k_new.reshape(b_, l_, N_KV_HEADS, HEAD_DIM), k_norm)
                v_new = v_new.reshape(b_, l_, N_KV_HEADS, HEAD_DIM)
                if k_past is None:
                    k_all, v_all = k_new, v_new
                else:
                    k_all = jnp.concatenate([k_past.astype(k_new.dtype), k_new], axis=1)
                    v_all = jnp.concatenate([v_past.astype(v_new.dtype), v_new], axis=1)
            j = layer - N_A_LAYERS
            q = (_rmsnorm(h, norm_mix_b[j]) @ w_q_b[j]).reshape(b_, l_, N_HEADS, HEAD_DIM)
            q = _rmsnorm(q, q_norm[j])
            h = h + _moba(q, k_all, v_all, q_pos).reshape(b_, l_, N_HEADS * HEAD_DIM) @ w_o_b[j]
        h = h + _swiglu(h, norm_ffn[layer], w_gate_up[layer], w_down[layer])
    return h, jnp.stack(states), k_new, v_new


def setup_inputs(seed: int = 0) -> dict:
    key = jax.random.key(seed)
    ks = jax.random.split(key, 24)
    f32 = jnp.float32
    n_pages = PAST_LEN // PAGE_SIZE
    n_phys = (5 * DEC_BATCH * n_pages + 3) // 4

    def nrm(k, shape, scale):
        return jax.random.normal(k, shape, f32) * scale

    def gain(k, shape):
        return 1.0 + 0.02 * jax.random.normal(k, shape, f32)

    perm = jax.random.permutation(ks[5], n_phys)
    page_table = perm[:DEC_BATCH * n_pages].reshape(DEC_BATCH, n_pages).astype(jnp.int32)
    return {
        'x_prompt': nrm(ks[0], (BATCH, SEQ, D_MODEL), 1.0),
        'x_sample': nrm(ks[1], (DEC_BATCH, DEC_SEQ, D_MODEL), 1.0),
        'state_hgrn': nrm(ks[2], (N_A_LAYERS, DEC_BATCH, HGRN_HEADS, HGRN_K, HGRN_V), 0.5),
        'cache_k': nrm(ks[3], (n_phys, PAGE_SIZE, N_KV_HEADS, HEAD_DIM), 1.0),
        'cache_v': nrm(ks[4], (n_phys, PAGE_SIZE, N_KV_HEADS, HEAD_DIM), 1.0),
        'page_table': page_table,
        'norm_mix_a': gain(ks[6], (N_A_LAYERS, D_MODEL)),
        'w_in_a': nrm(ks[7], (N_A_LAYERS, D_MODEL, 2 * HGRN_DK + 2 * HGRN_DV), D_MODEL ** -0.5),
        'lb_logits': nrm(ks[8], (N_A_LAYERS + 1, HGRN_DK), 0.5),
        'onorm_a': gain(ks[9], (N_A_LAYERS, HGRN_V)),
        'w_out_a': nrm(ks[10], (N_A_LAYERS, HGRN_DV, D_MODEL), HGRN_DV ** -0.5),
        'norm_kv': gain(ks[11], (D_MODEL,)),
        'w_kv': nrm(ks[12], (D_MODEL, 2 * N_KV_HEADS * HEAD_DIM), D_MODEL ** -0.5),
        'k_norm': gain(ks[13], (HEAD_DIM,)),
        'norm_mix_b': gain(ks[14], (N_B_LAYERS, D_MODEL)),
        'w_q_b': nrm(ks[15], (N_B_LAYERS, D_MODEL, N_HEADS * HEAD_DIM), D_MODEL ** -0.5),
        'q_norm': gain(ks[16], (N_B_LAYERS, HEAD_DIM)),
        'w_o_b': nrm(ks[17], (N_B_LAYERS, N_HEADS * HEAD_DIM, D_MODEL), (N_HEADS * HEAD_DIM) ** -0.5),
        'norm_ffn': gain(ks[18], (DEPTH, D_MODEL)),
        'w_gate_up': nrm(ks[19], (DEPTH, D_MODEL, 2 * D_FF), D_MODEL ** -0.5),
        'w_down': nrm(ks[20], (DEPTH, D_FF, D_MODEL), D_FF ** -0.5),
    }


def reference(x_prompt, x_sample, state_hgrn, cache_k, cache_v, page_table, norm_mix_a, w_in_a, lb_logits,
              onorm_a, w_out_a, norm_kv, w_kv, k_norm, norm_mix_b, w_q_b, q_norm, w_o_b, norm_ffn,
              w_gate_up, w_down):
    lb_all = jnp.cumsum(jax.nn.softmax(lb_logits.astype(jnp.float32), axis=0), axis=0)[:N_A_LAYERS]
    weights = (norm_mix_a, w_in_a, lb_all, onorm_a, w_out_a, norm_kv, w_kv, k_norm,
               norm_mix_b, w_q_b, q_norm, w_o_b, norm_ffn, w_gate_up, w_down)
    s0_prompt = jnp.zeros((N_A_LAYERS, x_prompt.shape[0], HGRN_HEADS, HGRN_K, HGRN_V), state_hgrn.dtype)
    y_prompt, s_prompt, k_prompt, v_prompt = _trunk(x_prompt, s0_prompt, None, None, *weights)
    dec_b, n_pages = page_table.shape
    past = n_pages * cache_k.shape[1]
    k_past = cache_k[page_table].reshape(dec_b, past, N_KV_HEADS, HEAD_DIM)
    v_past = cache_v[page_table].reshape(dec_b, past, N_KV_HEADS, HEAD_DIM)
    y_sample, s_sample, k_sample, v_sample = _trunk(x_sample, state_hgrn, k_past, v_past, *weights)
    return (y_prompt, y_sample, s_prompt, s_sample, k_prompt, v_prompt, k_sample, v_sample)
```

```python
import numpy as np
from contextlib import ExitStack
import concourse.bass as bass
import concourse.mybir as mybir
from concourse.bass_utils import run_bass_kernel_spmd

F32 = mybir.dt.float32
BF16 = mybir.dt.bfloat16
I32 = mybir.dt.int32
ALU = mybir.AluOpType
AF = mybir.ActivationFunctionType
AX = mybir.AxisListType

NCORES = 8
D = 2048
KC = 16
LP = 1024
NS = 4
ST = 4
NT = LP + NS * ST
H = 16
DFF = 5632
FC = DFF // 128
EPS = 1e-6
PARTS = [(0, 347), (347, 694), (694, 1040)]
ENGS = ("tensor", "vector", "scalar", "gpsimd", "sync")
GRP4 = [[0, 1, 2, 3], [4, 5, 6, 7]]
PAIRS = [[0, 4], [1, 5], [2, 6], [3, 7]]


class Buf:
    __slots__ = ("name", "w", "r")

    def __init__(self, name):
        self.name = name
        self.w = None
        self.r = {}


class Sched:
    def __init__(self, nc, stack):
        self.nc = nc
        self.stack = stack
        self.q = {e: [] for e in ENGS}
        self.cnt = {e: 0 for e in ENGS}
        self.sems = {}
        for e in ENGS:
            self.sems["E_" + e] = stack.enter_context(nc.semaphore("sem_" + e))
        self.seen = {e: {} for e in ENGS}
        self.dmacnt = {}

    def dma_sem(self, name):
        key = "D_" + name
        assert key not in self.sems
        self.sems[key] = self.stack.enter_context(self.nc.semaphore("dsem_" + name))
        self.dmacnt[key] = 0
        return key

    def op(self, eng, fn, reads=(), writes=(), dsem=None, inc=16):
        if getattr(self, "serial_buf", None) is not None:
            writes = list(writes) + [self.serial_buf]
        need = {}
        for b in reads:
            if b.w is not None:
                k, v = b.w
                if need.get(k, 0) < v:
                    need[k] = v
        for b in writes:
            if b.w is not None:
                k, v = b.w
                if need.get(k, 0) < v:
                    need[k] = v
            for k, v in b.r.items():
                if need.get(k, 0) < v:
                    need[k] = v
        if eng == "tensor":
            need.pop("E_tensor", None)
        for k in need:
            if k.startswith("D_"):
                need[k] = max(need[k], self.dmacnt[k])
        seen = self.seen[eng]
        waits = [(k, v) for k, v in need.items() if seen.get(k, 0) < v]
        for k, v in waits:
            seen[k] = v
        if dsem is None:
            self.cnt[eng] += 1
            me = ("E_" + eng, self.cnt[eng])
            incr = ("E_" + eng, 1)
        else:
            self.dmacnt[dsem] += inc
            me = (dsem, self.dmacnt[dsem])
            incr = (dsem, inc)
        self.q[eng].append((waits, fn, incr))
        for b in reads:
            if b.r.get(me[0], 0) < me[1]:
                b.r[me[0]] = me[1]
        for b in writes:
            b.w = me
            b.r = {}
        return me

    def final_wait(self, eng, bufs):
        need = {}
        for b in bufs:
            deps = list(b.r.items()) + ([b.w] if b.w else [])
            for k, v in deps:
                need[k] = max(need.get(k, 0), v)
        waits = [(k, v) for k, v in need.items() if self.seen[eng].get(k, 0) < v]
        for k, v in waits:
            self.seen[eng][k] = v
        self.q[eng].append((waits, None, None))

    def replay(self):
        nc, sems, q = self.nc, self.sems, self.q

        def run(name, e):
            for waits, fn, inc in q[name]:
                for k, v in waits:
                    e.wait_ge(sems[k], v)
                if fn is not None:
                    fn(e).then_inc(sems[inc[0]], inc[1])

        with nc.Block() as block:
            @block.tensor
            def _(e):
                run("tensor", e)

            @block.vector
            def _(e):
                run("vector", e)

            @block.scalar
            def _(e):
                run("scalar", e)

            @block.gpsimd
            def _(e):
                run("gpsimd", e)

            @block.sync
            def _(e):
                run("sync", e)


class KB:
    def __init__(self, nc, st):
        self.nc = nc
        self.st = st
        self.S = Sched(nc, st)
        self.nbuf = 0

    def sb(self, name, shape, dt):
        return self.st.enter_context(self.nc.sbuf_tensor("s_" + name, shape, dt))

    def ps(self, name, shape, dt):
        return self.st.enter_context(self.nc.psum_tensor("p_" + name, shape, dt))

    def buf(self, name=None):
        self.nbuf += 1
        return Buf(name or "b%d" % self.nbuf)

    def mm(self, out, lhsT, rhs, start, stop, r, w):
        self.S.op("tensor", lambda e: e.matmul(out, lhsT, rhs, start=start, stop=stop), r, w)

    def tr(self, out, in_, ident, r, w):
        self.S.op("tensor", lambda e: e.transpose(out, in_, ident), r, w)

    def act(self, out, in_, func, r, w, bias=None, scale=None, accum_out=None):
        kw = {}
        if bias is not None:
            kw["bias"] = bias
        if scale is not None:
            kw["scale"] = scale
        if accum_out is not None:
            kw["accum_out"] = accum_out
        self.S.op("scalar", lambda e: e.activation(out, in_, func, **kw), r, w)

    def ve(self, method, args, r, w, eng="vector", **kw):
        self.S.op(eng, lambda e: getattr(e, method)(*args, **kw), r, w)

    def dma(self, eng, out, in_, r, w, dsem, **kw):
        self.S.op(eng, lambda e: e.dma_start(out=out, in_=in_, **kw), r, w, dsem=dsem)

    def cc(self, groups, in_ap, out_ap, r, w, dsem):
        self.ncc = getattr(self, "ncc", 0) + 1
        own = self.S.dma_sem("cc%d" % self.ncc)
        if not hasattr(self, "cc_chain"):
            self.cc_chain = Buf("cc_chain")
        self.S.op("gpsimd", lambda e: e.collective_compute("AllGather", ALU.bypass, replica_groups=groups,
                                                           ins=[in_ap], outs=[out_ap]), r, list(w) + [self.cc_chain], dsem=own, inc=1)


def build(cfg=None):
    cfg = cfg or {}
    upto = cfg.get("upto", 99)
    nc = bass.Bass("TRN2", target_bir_lowering=False)

    def din(name, shape, dt=F32):
        return nc.dram_tensor(name, list(shape), dt, kind="ExternalInput")

    def dout(name, shape, dt=F32):
        return nc.dram_tensor(name, list(shape), dt, kind="ExternalOutput")

    xp_h = din("xp", [LP, D])
    xs_h = din("xs", [NS * ST, D])
    s0_h = din("s0", [NS, H, 128, 128])
    vecs_h = din("vecs", [128, 128])
    onehot_h = din("onehot", [128, 8])
    w_in_h = din("w_in", [D, 4 * D])
    w_out_h = din("w_out", [D, D])
    w_gu_h = din("w_gu", [2, D, 2 * DFF])
    w_dn_h = din("w_dn", [2, DFF, D])
    w_kv_h = din("w_kv", [D, 1024])
    w_q_h = din("w_q", [D, D])
    w_o_h = din("w_o", [D, D])
    pastmask_h = din("pastmask", [128, 8 * 32])
    with_cache = not cfg.get("nocache")
    if with_cache:
        npages = cfg.get("npages", 2560)
        ck_h = din("cache_k", [npages * 128, 512])
        cvv_h = din("cache_v", [npages * 128, 512])
        pt_h = din("pt", [NS, 64], I32)
    o_yp = dout("o_yp", [LP, D])
    o_ys = dout("o_ys", [NS * ST, D])
    o_kp = dout("o_kp", [LP, 512])
    o_vp = dout("o_vp", [LP, 512])
    o_ks = dout("o_ks", [NS * ST, 512])
    o_vs = dout("o_vs", [NS * ST, 512])
    o_sp = dout("o_sp", [H, 128, 128])
    o_ss = dout("o_ss", [NS, H, 128, 128])
    o_dbg = dout("o_dbg", [128, KC, NT]) if cfg.get("dbg") else None
    o_km = dout("o_km", [128, 128]) if cfg.get("dbg2") else None
    o_bias = dout("o_bias", [128, 8 * H * 32]) if cfg.get("dbg2") else None

    gs_in = [nc.dram_tensor("gs_in%d" % g, [128, 512], F32) for g in range(4)]
    gs_mid = [nc.dram_tensor("gs_mid%d" % g, [4 * 128, 512], F32) for g in range(4)]
    gs_out = [nc.dram_tensor("gs_out%d" % g, [8 * 128, 512], F32) for g in range(4)]
    gd_in = nc.dram_tensor("gd_in", [128, 512], F32)
    gd_mid = nc.dram_tensor("gd_mid", [4 * 128, 512], F32)
    gd_out = nc.dram_tensor("gd_out", [8 * 128, 512], F32)
    ck_in = [nc.dram_tensor("ck_in%d" % g, [128, LP // 2], F32) for g in range(4)]
    ck_mid = [nc.dram_tensor("ck_mid%d" % g, [4 * 128, LP // 2], F32) for g in range(4)]
    ck_out = [nc.dram_tensor("ck_out%d" % g, [8 * 128, LP // 2], F32) for g in range(4)]
    cv_in = [nc.dram_tensor("cv_in%d" % g, [128, 512], F32) for g in range(4)]
    cv_mid = [nc.dram_tensor("cv_mid%d" % g, [4 * 128, 512], F32) for g in range(4)]
    cv_out = [nc.dram_tensor("cv_out%d" % g, [8 * 128, 512], F32) for g in range(4)]
    cm_in = nc.dram_tensor("cm_in", [128, 512], F32)
    cm_mid = nc.dram_tensor("cm_mid", [4 * 128, 512], F32)
    cm_out = nc.dram_tensor("cm_out", [8 * 128, 512], F32)

    with ExitStack() as st:
        kb = KB(nc, st)
        S = kb.S
        sb, ps, buf = kb.sb, kb.ps, kb.buf

        def arena(name, nbytes):
            return sb(name, [128, nbytes // 4], F32)

        def carve(ar, off, shape, dt):
            free = 1
            for d_ in shape[1:]:
                free *= d_
            nb = free * (2 if dt == BF16 else 4)
            assert off % 4 == 0 and nb % 4 == 0
            ap = ar[0:shape[0], off // 4:(off + nb) // 4]
            if dt != F32:
                ap = ap.bitcast(dt)
            if len(shape) == 3:
                ap = ap.rearrange("p (a b) -> p a b", b=shape[2])
            return ap

        A_h_t = arena("A_h", KC * NT * 4)
        A_h = carve(A_h_t, 0, [128, KC, NT], F32)
        xT_t = arena("xT", KC * NT * 2)
        xT = carve(xT_t, 0, [128, KC, NT], BF16)
        oT_t = arena("oT", KC * NT * 2)
        oT = carve(oT_t, 0, [128, KC, NT], BF16)
        NW = 2
        WR_t = [arena("wr%d" % i, 16384) for i in range(NW)]
        WR = [carve(WR_t[i], 0, [128, 8192], BF16) for i in range(NW)]
        A_m = arena("A_m", 19456)
        b_hT = [buf("hT%d" % c) for c in range(KC)]
        b_xT = buf("xT")
        b_oT = [buf("oT%d" % c) for c in range(KC)]
        b_WR = [buf("wr%d" % i) for i in range(NW)]
        d_WR = [S.dma_sem("wr%d" % i) for i in range(NW)]
        TSZ = NT * 4
        b_T = [buf("T%d" % i) for i in range(9)]

        ident_bf = sb("ident_bf", [128, 128], BF16)
        ident_f = sb("ident_f", [128, 128], F32)
        ones_bf = sb("ones_bf", [128, 128], BF16)
        triu = sb("triu", [64, 8, 64], BF16)
        smask = sb("smask", [16, 16], F32)
        rowsel = sb("rowsel", [16, NS], F32)
        vecsT = sb("vecsT", [128, 128], F32)
        vecs_sb = sb("vecs_sb", [128, 128], F32)
        lbt = sb("lbt", [128, 2, H], F32)
        onehot = sb("onehot", [128, 8], F32)
        scanmask = sb("scanmask", [128, NT], F32)
        b_scan = buf("scanmask")
        onecol = sb("onecol", [128, 1], F32)
        b_const = buf("const")
        b_vecs = buf("vecs")
        fence_t = sb("fence_t", [128, 1], F32)

        PS = [ps("ps%d" % i, [128, 512], F32) for i in range(6)]
        PT = [ps("pt%d" % i, [128, 1024], BF16) for i in range(2)]
        b_PS = [buf("ps%d" % i) for i in range(6)]
        b_PT = [buf("pt%d" % i) for i in range(2)]

        d_ld = S.dma_sem("ld")
        d_c = S.dma_sem("const")
        d_out = S.dma_sem("out")
        d_cc = S.dma_sem("cc")
        out_bufs = []

        g = "gpsimd"
        kb.ve("memset", (ident_bf[:], 1.0), [], [b_const], eng=g)
        kb.ve("affine_select", (ident_bf[:], ident_bf[:], [[-1, 128]], ALU.is_equal, 0.0), [b_const], [b_const], eng=g,
              base=0, channel_multiplier=1)
        kb.ve("memset", (ident_f[:], 1.0), [], [b_const], eng=g)
        kb.ve("affine_select", (ident_f[:], ident_f[:], [[-1, 128]], ALU.is_equal, 0.0), [b_const], [b_const], eng=g,
              base=0, channel_multiplier=1)
        kb.ve("memset", (ones_bf[:], 1.0), [], [b_const], eng=g)
        kb.ve("memset", (triu[:], 1.0), [], [b_const], eng=g)
        kb.ve("affine_select", (triu[:], triu[:], [[0, 8], [1, 64]], ALU.is_ge, 0.0), [b_const], [b_const], eng=g,
              base=0, channel_multiplier=-1)
        kb.ve("memset", (smask[:], 1.0), [], [b_const], eng=g)
        kb.ve("affine_select", (smask[:], smask[:], [[1, 16]], ALU.is_ge, 0.0), [b_const], [b_const], eng=g,
              base=0, channel_multiplier=-1)
        for j in range(NS):
            kb.ve("affine_select", (smask[:, 4 * j:4 * j + 4], smask[:, 4 * j:4 * j + 4], [[0, 4]], ALU.is_ge, 0.0),
                  [b_const], [b_const], eng=g, base=-4 * j, channel_multiplier=1)
        kb.ve("memset", (rowsel[:], 1.0), [], [b_const], eng=g)
        kb.ve("affine_select", (rowsel[:], rowsel[:], [[-4, NS]], ALU.is_ge, 0.0), [b_const], [b_const], eng=g,
              base=0, channel_multiplier=1)
        kb.ve("affine_select", (rowsel[:], rowsel[:], [[4, NS]], ALU.is_ge, 0.0), [b_const], [b_const], eng=g,
              base=3, channel_multiplier=-1)
        kb.ve("memset", (scanmask[:], 1.0), [], [b_scan], eng=g)
        kb.ve("memset", (scanmask[:, 0:LP].rearrange("p (c t) -> p c t", t=64)[:, :, 0:1], 0.0), [], [b_scan], eng=g)
        kb.ve("memset", (scanmask[:, LP:NT].rearrange("p (c t) -> p c t", t=ST)[:, :, 0:1], 0.0), [], [b_scan], eng=g)

        kb.dma("sync", vecs_sb[:], vecs_h.ap(), [], [b_vecs], d_c)
        d_c2 = S.dma_sem("const2")
        b_oh = buf("onehot")
        kb.dma("sync", onehot[:], onehot_h.ap(), [], [b_oh], d_c2)
        kb.tr(PS[0][:, 0:128], vecs_sb[:], ident_f[:], [b_vecs, b_const], [b_PS[0]])
        kb.ve("tensor_copy", (vecsT[:], PS[0][:, 0:128]), [b_PS[0]], [b_vecs])
        kb.ve("tensor_sub", (lbt[:, 0, :], vecsT[:, 16:32], vecsT[:, 0:16]), [b_vecs], [b_vecs])
        kb.act(lbt[:, 0, :], lbt[:, 0, :], AF.Exp, [b_vecs], [b_vecs])
        kb.ve("tensor_scalar_add", (lbt[:, 0, :], lbt[:, 0, :], 1.0), [b_vecs], [b_vecs])
        kb.ve("reciprocal", (lbt[:, 0, :], lbt[:, 0, :]), [b_vecs], [b_vecs])
        kb.ve("tensor_scalar", (lbt[:, 1, :], lbt[:, 0, :], -1.0, 1.0, ALU.mult, ALU.add), [b_vecs], [b_vecs])
        V_NMA, V_NFFN, V_NKV, V_NMB, V_ON, V_KN, V_QN = 32, 48, 80, 96, 112, 113, 114

        wstate = {"n": 0}

        def wload(pieces):
            s = wstate["n"] % NW
            wstate["n"] += 1
            for dst_fn, src in pieces:
                kb.dma("gpsimd", dst_fn(WR[s]), src, [], [b_WR[s]], d_WR[s])
            return s

        jobs = []

        def run_jobs():
            loaded = {}
            order = [i for i, j in enumerate(jobs) if j[0] is not None and j[0] != "NOPREFETCH"]
            issued = 0
            for i, (ld, run) in enumerate(jobs):
                p = next((n for n, ii in enumerate(order) if ii >= i), len(order))
                tgt = min(p + NW, len(order))
                nxt_np = next((ii for ii in range(i, len(jobs)) if jobs[ii][0] == "NOPREFETCH"), None)
                if nxt_np is not None:
                    tgt = min(tgt, sum(1 for ii in order if ii < nxt_np))
                while issued < tgt:
                    loaded[order[issued]] = jobs[order[issued]][0]()
                    issued += 1
                run(loaded.get(i))

        w_in_v = w_in_h.ap().rearrange("(k p) n -> p k n", p=128)
        w_out_v = w_out_h.ap().rearrange("(k p) n -> p k n", p=128)

        xst = [carve(A_h_t, 0, [128, D], F32), carve(A_h_t, 2 * TSZ, [128, D], F32)]
        xsb = [carve(A_h_t, 4 * TSZ, [128, D], BF16), carve(A_h_t, 5 * TSZ, [128, D], BF16)]
        bl_xst = [[b_T[0], b_T[1]], [b_T[2], b_T[3]]]
        bl_xsb = [[b_T[4]], [b_T[5]]]
        d_xst = [S.dma_sem("xst%d" % i) for i in range(2)]
        ssq = sb("ssq", [128, 16], F32)
        b_ssq = buf()

        def phase1a(_):
            for t in range(9):
                rows = 128 if t < 8 else NS * ST
                i = t % 2
                src = xp_h[t * 128:(t + 1) * 128, :] if t < 8 else xs_h[:, :]
                kb.dma("sync", xst[i][0:rows, :], src, [], bl_xst[i], d_xst[i])
                kb.act(xsb[i][0:rows, :], xst[i][0:rows, :], AF.Square, bl_xst[i], bl_xsb[i] + [b_ssq],
                       accum_out=ssq[0:rows, t:t + 1])
                kb.act(ssq[0:rows, t:t + 1], ssq[0:rows, t:t + 1], AF.Ln, [b_ssq], [b_ssq], bias=EPS, scale=1.0 / D)
                kb.act(ssq[0:rows, t:t + 1], ssq[0:rows, t:t + 1], AF.Exp, [b_ssq], [b_ssq], scale=-0.5)
                kb.ve("tensor_scalar_mul", (xsb[i][0:rows, :], xst[i][0:rows, :], ssq[0:rows, t:t + 1]),
                      bl_xst[i] + [b_ssq], bl_xsb[i])
                col0 = t * 128
                for q4 in range(4):
                    pt = PT[q4 % 2]
                    bpt = b_PT[q4 % 2]
                    for cc in range(4):
                        c = q4 * 4 + cc
                        kb.tr(pt[:, cc * 128:cc * 128 + rows], xsb[i][0:rows, c * 128:(c + 1) * 128],
                              ident_bf[0:rows, 0:rows], bl_xsb[i] + [b_const], [bpt])
                    gv = vecsT[:, V_NMA + q4 * 4:V_NMA + q4 * 4 + 4]
                    kb.ve("tensor_tensor", (xT[:, q4 * 4:q4 * 4 + 4, col0:col0 + rows],
                                            pt[:, 0:512].rearrange("p (c t) -> p c t", t=128)[:, :, 0:rows],
                                            gv.unsqueeze(2).to_broadcast([128, 4, rows]), ALU.mult),
                          [bpt, b_vecs], [b_xT])

        jobs.append((None, phase1a))

        def tmp(i):
            return carve(A_h_t, i * TSZ, [128, NT], F32)
        T_ZQ, T_O = 0, 0
        T_EQ, T_QS = 1, 1
        T_F, T_OMF, T_CUM = 2, 3, 4
        T_ZG, T_A1 = 5, 5
        T_EG, T_A4 = 6, 6
        T_E = 7
        T_LF, T_RS = 8, 8
        ob = 9 * TSZ
        qt_bf = carve(A_h_t, ob + 0 * 2080, [128, NT], BF16)
        kt_bf = carve(A_h_t, ob + 1 * 2080, [128, NT], BF16)
        qi_bf = carve(A_h_t, ob + 2 * 2080, [128, NT], BF16)
        ko_bf = carve(A_h_t, ob + 3 * 2080, [128, NT], BF16)
        vt_bf = carve(A_h_t, ob + 4 * 2080, [128, NT], BF16)
        sq_bf = carve(A_h_t, ob + 5 * 2080, [128, NT], BF16)
        gate_bf = carve(A_h_t, ob + 6 * 2080, [128, NT], BF16)
        ob += 7 * 2080
        b_qt, b_kt, b_qi, b_ko, b_vt, b_sq, b_gate = [buf() for _ in range(7)]
        vtok = carve(A_h_t, ob, [64, 16, 128], BF16)
        kotok = carve(A_h_t, ob + 4096, [64, 16, 128], BF16)
        vgtok = carve(A_h_t, ob, [128, 8, 128], BF16)
        kgtok = carve(A_h_t, ob + 4096, [128, 8, 128], BF16)
        attm = carve(A_h_t, ob + 8192, [64, 16, 64], BF16)
        Sbf = carve(A_h_t, ob + 10240, [128, 16, 128], BF16)
        assert ob + 10240 + 4096 <= KC * NT * 4
        b_vtok, b_kotok, b_toks = buf(), buf(), buf()
        b_vgtok, b_kgtok = b_vtok, b_kotok
        b_attm = buf()
        b_Sbf = buf()
        Sin = carve(A_m, 0, [128, H, 128], F32)
        s0_sb = carve(A_m, 8192, [128, NS, 128], F32)
        s0_bf = carve(A_m, 10240, [128, NS, 128], BF16)
        sfin = carve(A_m, 11264, [128, NS, 128], F32)
        sloc = carve(A_m, 13312, [128, 4, 128], F32)
        mo = 15360
        Dall = carve(A_m, mo, [128, 8, H], F32)
        dtot = carve(A_m, mo + 512, [128, H], F32)
        dec = carve(A_m, mo + 576, [128, 20], F32)
        vtok_s = carve(A_m, mo + 656, [16, 128], BF16)
        kotok_s = carve(A_m, mo + 912, [16, 128], BF16)
        vm_s = carve(A_m, mo + 1168, [16, NS, 128], BF16)
        attm_s = carve(A_m, mo + 2192, [16, 16], BF16)
        Sst = carve(A_m, mo + 2224, [128, 2, 128], F32)
        assert mo + 2224 + 1024 <= 19456
        b_dec = buf()
        b_Sst = [buf(), buf()]
        b_s0, b_s0bf, b_sfin = buf(), buf(), buf()
        d_s0 = S.dma_sem("s0")
        d_sfin = S.dma_sem("sfin")
        b_Sin = [buf() for _ in range(4)]
        b_sloc, b_dtot = buf(), buf()
        d_sloc = S.dma_sem("sloc")
        d_dtot = S.dma_sem("dtot")
        b_gs = [buf() for _ in range(4)]
        b_gd = buf()

        def proj(slot, wcol0, dst_parts, kparts=PARTS):
            wv = WR[slot][:, :].rearrange("p (k n) -> p k n", k=KC)
            for (c0, c1), (pap, pb) in zip(kparts, dst_parts):
                for k in range(KC):
                    kb.mm(pap, wv[:, k, wcol0:wcol0 + 128], xT[:, k, c0:c1], k == 0, k == KC - 1,
                          [b_WR[slot], b_xT], [pb])

        HALF = [(0, 512), (512, 1024)]

        def make_prepass(h):
            def load():
                return wload([
                    (lambda w: w[:, :].rearrange("p (k n) -> p k n", k=KC)[:, :, 0:128], w_in_v[:, :, D + h * 128:D + (h + 1) * 128]),
                    (lambda w: w[:, :].rearrange("p (k n) -> p k n", k=KC)[:, :, 128:256], w_in_v[:, :, 2 * D + h * 128:2 * D + (h + 1) * 128]),
                ])

            def run(slot):
                wv = WR[slot][:, :].rearrange("p (k n) -> p k n", k=KC)
                for pi, (c0, c1) in enumerate(HALF):
                    for k in range(KC):
                        kb.mm(PS[pi][:, :], wv[:, k, 0:128], xT[:, k, c0:c1], k == 0, k == KC - 1, [b_WR[slot], b_xT], [b_PS[pi]])
                for pi, (c0, c1) in enumerate(HALF):
                    for k in range(KC):
                        kb.mm(PS[2 + pi][:, :], wv[:, k, 128:256], xT[:, k, c0:c1], k == 0, k == KC - 1, [b_WR[slot], b_xT], [b_PS[2 + pi]])
                F_, LF_, OMF_, CUM_, E_ = tmp(T_F), tmp(T_LF), tmp(T_OMF), tmp(T_CUM), tmp(T_E)
                for pi, (c0, c1) in enumerate(HALF):
                    kb.act(F_[:, c0:c1], PS[pi][:, :], AF.Exp, [b_PS[pi]], [b_T[T_F]], scale=-1.0)
                    kb.act(vt_bf[:, c0:c1], PS[2 + pi][:, :], AF.Copy, [b_PS[2 + pi]], [b_vt])
                kb.ve("tensor_scalar_add", (F_[:, 0:LP], F_[:, 0:LP], 1.0), [b_T[T_F]], [b_T[T_F]])
                kb.ve("reciprocal", (F_[:, 0:LP], F_[:, 0:LP]), [b_T[T_F]], [b_T[T_F]])
                kb.ve("tensor_scalar", (F_[:, 0:LP], F_[:, 0:LP], lbt[:, 1, h:h + 1], lbt[:, 0, h:h + 1], ALU.mult, ALU.add),
                      [b_T[T_F], b_vecs], [b_T[T_F]])
                kb.act(LF_[:, 0:LP], F_[:, 0:LP], AF.Ln, [b_T[T_F]], [b_T[T_LF]])
                kb.ve("tensor_scalar", (OMF_[:, 0:LP], F_[:, 0:LP], -1.0, 1.0, ALU.mult, ALU.add), [b_T[T_F]], [b_T[T_OMF]])
                kb.ve("tensor_tensor_scan", (CUM_[:, 0:LP], onecol[:, 0:1].to_broadcast([128, LP]), LF_[:, 0:LP], 0.0, ALU.mult, ALU.add),
                      [b_T[T_LF], b_const], [b_T[T_CUM]])
                kb.act(E_[:, 0:LP], CUM_[:, 0:LP], AF.Exp, [b_T[T_CUM]], [b_T[T_E]], bias=CUM_[:, LP - 1:LP], scale=-1.0)
                kb.ve("tensor_tensor", (ko_bf[:, 0:LP], OMF_[:, 0:LP], E_[:, 0:LP], ALU.mult), [b_T[T_OMF], b_T[T_E]], [b_ko])
                kb.act(dtot[:, h:h + 1], CUM_[:, LP - 1:LP], AF.Exp, [b_T[T_CUM]], [b_dtot])
                for t in range(8):
                    kb.tr(PT[0][:, t * 128:(t + 1) * 128], ko_bf[:, t * 128:(t + 1) * 128], ident_bf[:], [b_ko, b_const], [b_PT[0]])
                kb.ve("tensor_copy", (kgtok[:, :, :], PT[0][:, :].rearrange("p (t k) -> p t k", k=128)), [b_PT[0]], [b_kgtok])
                for t in range(8):
                    kb.tr(PT[1][:, t * 128:(t + 1) * 128], vt_bf[:, t * 128:(t + 1) * 128], ident_bf[:], [b_vt, b_const], [b_PT[1]])
                kb.act(vgtok[:, :, :], PT[1][:, :].rearrange("p (t k) -> p t k", k=128), AF.Copy, [b_PT[1]], [b_vgtok])
                for t in range(8):
                    kb.mm(PS[4][:, 0:128], kgtok[:, t, :], vgtok[:, t, :], t == 0, t == 7, [b_kgtok, b_vgtok], [b_PS[4]])
                hh = h % 4
                kb.ve("tensor_copy", (sloc[:, hh, :], PS[4][:, 0:128]), [b_PS[4]], [b_sloc])
                if hh == 3:
                    gq = h // 4
                    kb.dma("sync", gs_in[gq].ap(), sloc[:, :, :].rearrange("p a b -> p (a b)"), [b_sloc], [b_gs[gq]], d_sloc)
                    kb.cc(GRP4, gs_in[gq].ap().opt(), gs_mid[gq].ap().opt(), [b_gs[gq]], [b_gs[gq]], d_cc)
                    kb.cc(PAIRS, gs_mid[gq].ap().opt(), gs_out[gq].ap().opt(), [b_gs[gq]], [b_gs[gq]], d_cc)
                if h == H - 1:
                    kb.dma("sync", gd_in.ap(), carve(A_m, mo + 512, [128, 512], F32), [b_dtot], [b_gd], d_dtot)
                    kb.cc(GRP4, gd_in.ap().opt(), gd_mid.ap().opt(), [b_gd], [b_gd], d_cc)
                    kb.cc(PAIRS, gd_mid.ap().opt(), gd_out.ap().opt(), [b_gd], [b_gd], d_cc)
            return load, run

        kb.ve("memset", (onecol[:], 1.0), [], [b_const], eng=g)
        for h in range(H):
            jobs.append(make_prepass(h))

        G_sb = carve(A_h_t, 0, [128, 8, 512], F32)
        Pch = [carve(A_h_t, 4 * TSZ, [128, 512], F32), carve(A_h_t, 5 * TSZ, [128, 512], F32)]
        b_Dall = buf()
        bl_G = [b_T[0], b_T[1], b_T[2], b_T[3]]
        b_Pch = [b_T[4], b_T[5]]
        d_G = S.dma_sem("G")
        d_Dall = S.dma_sem("Dall")
        d_sp = S.dma_sem("sp")

        def chain(_):
            kb.dma("sync", Dall[:, :, :], gd_out.ap().rearrange("(r p) h -> p r h", p=128)[:, :, 0:H], [b_gd], [b_Dall], d_Dall)
            for gq in range(4):
                kb.dma("sync", G_sb[:, :, :], gs_out[gq].ap().rearrange("(r p) n -> p r n", p=128), [b_gs[gq]], bl_G, d_G)
                Sg = Sin[:, gq * 4:(gq + 1) * 4, :].rearrange("p a b -> p (a b)")
                kb.ve("memset", (Sg, 0.0), [], [b_Sin[gq]])
                kb.ve("tensor_copy", (Pch[0][:, :], G_sb[:, 0, :]), bl_G, [b_Pch[0]])
                cur = 0
                for j in range(1, 8):
                    kb.ve("scalar_tensor_tensor", (Sg, Pch[cur][:, :], onehot[:, j:j + 1], Sg, ALU.mult, ALU.add),
                          [b_Pch[cur], b_oh, b_Sin[gq]], [b_Sin[gq]])
                    nx = 1 - cur
                    kb.ve("tensor_tensor", (Pch[nx][:, :].rearrange("p (a b) -> p a b", a=4),
                                            Pch[cur][:, :].rearrange("p (a b) -> p a b", a=4),
                                            Dall[:, j, gq * 4:(gq + 1) * 4].unsqueeze(2).to_broadcast([128, 4, 128]), ALU.mult),
                          [b_Pch[cur], b_Dall], [b_Pch[nx]])
                    kb.ve("tensor_tensor", (Pch[nx][:, :], Pch[nx][:, :], G_sb[:, j, :], ALU.add), [b_Pch[nx]] + bl_G, [b_Pch[nx]])
                    cur = nx
                kb.dma("sync", o_sp[gq * 4:(gq + 1) * 4, :, :].rearrange("h k v -> k h v"),
                       Pch[cur][:, :].rearrange("p (a b) -> p a b", a=4), [b_Pch[cur]], [], d_sp)
                out_bufs.append(b_Pch[cur])

        jobs.append((None, chain))

        def make_main(h):
            def load():
                rr = lambda j: (lambda w: w[:, :].rearrange("p (k n) -> p k n", k=KC)[:, :, j * 128:(j + 1) * 128])
                return wload([(rr(j), w_in_v[:, :, j * D + h * 128:j * D + (h + 1) * 128]) for j in range(4)])

            def run(slot):
                gq = h // 4
                kb.dma("sync", s0_sb[:, :, :], s0_h[:, h, :, :].rearrange("j k v -> k j v"), [], [b_s0], d_s0)
                kb.act(s0_bf[:, :, :], s0_sb[:, :, :], AF.Copy, [b_s0], [b_s0bf])
                ZQ, EQ, QS, F_, LF_, OMF_, CUM_, A1, A4, E_, O_, RS, ZG, EG = [tmp(i) for i in (T_ZQ, T_EQ, T_QS, T_F, T_LF, T_OMF, T_CUM, T_A1, T_A4, T_E, T_O, T_RS, T_ZG, T_EG)]
                pq = [(PS[i][:, 0:c1 - c0], b_PS[i]) for i, (c0, c1) in enumerate(PARTS)]
                pf = [(PS[3 + i][:, 0:c1 - c0], b_PS[3 + i]) for i, (c0, c1) in enumerate(PARTS)]
                proj(slot, 0, pq)
                for i, (c0, c1) in enumerate(PARTS):
                    kb.act(ZQ[:, c0:c1], pq[i][0], AF.Copy, [pq[i][1]], [b_T[T_ZQ]])
                proj(slot, 128, pf)
                for i, (c0, c1) in enumerate(PARTS):
                    kb.act(F_[:, c0:c1], pf[i][0], AF.Exp, [pf[i][1]], [b_T[T_F]], scale=-1.0)
                proj(slot, 256, pq)
                for i, (c0, c1) in enumerate(PARTS):
                    kb.act(vt_bf[:, c0:c1], pq[i][0], AF.Copy, [pq[i][1]], [b_vt])
                proj(slot, 384, pf)
                for i, (c0, c1) in enumerate(PARTS):
                    kb.act(ZG[:, c0:c1], pf[i][0], AF.Copy, [pf[i][1]], [b_T[T_ZG]])
                kb.act(EQ, ZQ, AF.Exp, [b_T[T_ZQ]], [b_T[T_EQ]], scale=-1.0)
                kb.ve("tensor_scalar_add", (EQ, EQ, 1.0), [b_T[T_EQ]], [b_T[T_EQ]])
                kb.ve("reciprocal", (EQ, EQ), [b_T[T_EQ]], [b_T[T_EQ]])
                kb.ve("tensor_tensor", (QS, ZQ, EQ, ALU.mult), [b_T[T_ZQ], b_T[T_EQ]], [b_T[T_QS]])
                kb.ve("tensor_scalar_add", (F_, F_, 1.0), [b_T[T_F]], [b_T[T_F]])
                kb.ve("reciprocal", (F_, F_), [b_T[T_F]], [b_T[T_F]])
                kb.ve("tensor_scalar", (F_, F_, lbt[:, 1, h:h + 1], lbt[:, 0, h:h + 1], ALU.mult, ALU.add),
                      [b_T[T_F], b_vecs], [b_T[T_F]])
                kb.act(LF_, F_, AF.Ln, [b_T[T_F]], [b_T[T_LF]])
                kb.ve("tensor_scalar", (OMF_, F_, -1.0, 1.0, ALU.mult, ALU.add), [b_T[T_F]], [b_T[T_OMF]])
                kb.act(EG, ZG, AF.Exp, [b_T[T_ZG]], [b_T[T_EG]], scale=-1.0)
                kb.ve("tensor_scalar_add", (EG, EG, 1.0), [b_T[T_EG]], [b_T[T_EG]])
                kb.ve("reciprocal", (EG, EG), [b_T[T_EG]], [b_T[T_EG]])
                kb.ve("tensor_tensor", (gate_bf[:, :], ZG, EG, ALU.mult), [b_T[T_ZG], b_T[T_EG]], [b_gate])
                kb.ve("tensor_tensor_scan", (CUM_, scanmask[:, :], LF_, 0.0, ALU.mult, ALU.add), [b_T[T_LF], b_scan], [b_T[T_CUM]])
                cp = CUM_[:, 0:LP].rearrange("p (c t) -> p c t", t=64)
                cs = CUM_[:, LP:NT].rearrange("p (c t) -> p c t", t=ST)
                for (dst, mid_p, mid_s) in ((A1, 31, 1), (A4, 63, 3)):
                    kb.ve("tensor_tensor", (dst[:, 0:LP].rearrange("p (c t) -> p c t", t=64), cp,
                                            cp[:, :, mid_p:mid_p + 1].to_broadcast([128, 16, 64]), ALU.subtract),
                          [b_T[T_CUM]], [b_T[T_A1 if dst is A1 else T_A4]])
                    kb.ve("tensor_tensor", (dst[:, LP:NT].rearrange("p (c t) -> p c t", t=ST), cs,
                                            cs[:, :, mid_s:mid_s + 1].to_broadcast([128, NS, ST]), ALU.subtract),
                          [b_T[T_CUM]], [b_T[T_A1 if dst is A1 else T_A4]])
                kb.act(dec[:, 0:16], cp[:, :, 63], AF.Exp, [b_T[T_CUM]], [b_dec])
                kb.act(dec[:, 16:20], cs[:, :, 3], AF.Exp, [b_T[T_CUM]], [b_dec])
                kb.act(E_, A1, AF.Exp, [b_T[T_A1]], [b_T[T_E]])
                kb.ve("tensor_tensor", (qt_bf[:, :], QS, E_, ALU.mult), [b_T[T_QS], b_T[T_E]], [b_qt])
                kb.act(E_, A1, AF.Exp, [b_T[T_A1], b_T[T_E]], [b_T[T_E]], scale=-1.0)
                kb.ve("tensor_tensor", (kt_bf[:, :], OMF_, E_, ALU.mult), [b_T[T_OMF], b_T[T_E]], [b_kt])
                kb.act(E_, CUM_, AF.Exp, [b_T[T_CUM], b_T[T_E]], [b_T[T_E]])
                kb.ve("tensor_tensor", (qi_bf[:, :], QS, E_, ALU.mult), [b_T[T_QS], b_T[T_E]], [b_qi])
                kb.act(E_, A4, AF.Exp, [b_T[T_A4], b_T[T_E]], [b_T[T_E]], scale=-1.0)
                kb.ve("tensor_tensor", (ko_bf[:, :], OMF_, E_, ALU.mult), [b_T[T_OMF], b_T[T_E]], [b_ko])
                for half in range(2):
                    for cc in range(8):
                        c = half * 8 + cc
                        kb.tr(PT[0][0:64, cc * 128:(cc + 1) * 128], vt_bf[:, c * 64:(c + 1) * 64], ident_bf[:], [b_vt, b_const], [b_PT[0]])
                    kb.ve("tensor_copy", (vtok[:, half * 8:half * 8 + 8, :], PT[0][0:64, :].rearrange("p (c k) -> p c k", k=128)),
                          [b_PT[0]], [b_vtok])
                    for cc in range(8):
                        c = half * 8 + cc
                        kb.tr(PT[1][0:64, cc * 128:(cc + 1) * 128], ko_bf[:, c * 64:(c + 1) * 64], ident_bf[:], [b_ko, b_const], [b_PT[1]])
                    kb.act(kotok[:, half * 8:half * 8 + 8, :], PT[1][0:64, :].rearrange("p (c k) -> p c k", k=128), AF.Copy,
                           [b_PT[1]], [b_kotok])
                kb.tr(PT[0][0:16, 0:128], vt_bf[:, LP:NT], ident_bf[:], [b_vt, b_const], [b_PT[0]])
                kb.tr(PT[0][0:16, 128:256], ko_bf[:, LP:NT], ident_bf[:], [b_ko, b_const], [b_PT[0]])
                kb.ve("tensor_copy", (vtok_s[:, :], PT[0][0:16, 0:128]), [b_PT[0]], [b_toks])
                kb.ve("tensor_copy", (kotok_s[:, :], PT[0][0:16, 128:256]), [b_PT[0]], [b_toks])
                for j in range(NS):
                    kb.ve("tensor_scalar_mul", (vm_s[:, j, :], vtok_s[:, :], rowsel[:, j:j + 1]), [b_toks, b_const], [b_toks])
                for half in range(2):
                    for cc in range(8):
                        c = half * 8 + cc
                        kb.mm(PS[half][0:64, cc * 64:(cc + 1) * 64], kt_bf[:, c * 64:(c + 1) * 64], qt_bf[:, c * 64:(c + 1) * 64],
                              True, True, [b_kt, b_qt], [b_PS[half]])
                    kb.ve("tensor_tensor", (attm[:, half * 8:half * 8 + 8, :], PS[half][0:64, :].rearrange("p (c t) -> p c t", t=64),
                                            triu[:, :, :], ALU.mult), [b_PS[half], b_const], [b_attm])
                kb.mm(PS[2][0:16, 0:16], kt_bf[:, LP:NT], qt_bf[:, LP:NT], True, True, [b_kt, b_qt], [b_PS[2]])
                kb.ve("tensor_tensor", (attm_s[:, :], PS[2][0:16, 0:16], smask[:, :], ALU.mult), [b_PS[2], b_const], [b_attm])
                for q4 in range(4):
                    for cc in range(4):
                        c = q4 * 4 + cc
                        kb.mm(PS[2 + (q4 % 2)][:, cc * 128:(cc + 1) * 128], kotok[:, c, :], vtok[:, c, :], True, True,
                              [b_kotok, b_vtok], [b_PS[2 + (q4 % 2)]])
                    for cc in range(4):
                        c = q4 * 4 + cc
                        if c == 0:
                            kb.ve("tensor_copy", (Sbf[:, 0, :], Sin[:, h, :]), [b_Sin[gq]], [b_Sbf])
                            prev = Sin[:, h, :]
                            prev_b = b_Sin[gq]
                        else:
                            prev = Sst[:, (c - 1) % 2, :]
                            prev_b = b_Sst[(c - 1) % 2]
                        kb.ve("scalar_tensor_tensor", (Sst[:, c % 2, :], prev, dec[:, c:c + 1], PS[2 + (q4 % 2)][:, cc * 128:(cc + 1) * 128],
                                                       ALU.mult, ALU.add), [prev_b, b_dec, b_PS[2 + (q4 % 2)]], [b_Sst[c % 2]])
                        if c < 15:
                            kb.act(Sbf[:, c + 1, :], Sst[:, c % 2, :], AF.Copy, [b_Sst[c % 2]], [b_Sbf])
                for half in range(2):
                    for cc in range(8):
                        c = half * 8 + cc
                        dst = PS[4 + half][:, cc * 64:(cc + 1) * 64]
                        kb.mm(dst, vtok[:, c, :], attm[:, c, :], True, False, [b_vtok, b_attm], [b_PS[4 + half]])
                        kb.mm(dst, Sbf[:, c, :], qi_bf[:, c * 64:(c + 1) * 64], False, True, [b_Sbf, b_qi], [b_PS[4 + half]])
                    kb.act(O_[:, half * 512:(half + 1) * 512], PS[4 + half][:, :], AF.Copy, [b_PS[4 + half]], [b_T[T_O]])
                kb.mm(PS[0][:, 0:16], vtok_s[:, :], attm_s[:, :], True, False, [b_toks, b_attm], [b_PS[0]])
                for j in range(NS):
                    kb.mm(PS[0][:, 4 * j:4 * j + 4], s0_bf[:, j, :], qi_bf[:, LP + 4 * j:LP + 4 * j + 4], False, j == NS - 1,
                          [b_s0bf, b_qi], [b_PS[0]])
                kb.act(O_[:, LP:NT], PS[0][:, 0:16], AF.Copy, [b_PS[0]], [b_T[T_O]])
                for j in range(NS):
                    kb.mm(PS[1][:, j * 128:(j + 1) * 128], kotok_s[:, :], vm_s[:, j, :], True, True, [b_toks], [b_PS[1]])
                for j in range(NS):
                    kb.ve("scalar_tensor_tensor", (sfin[:, j, :], s0_sb[:, j, :], dec[:, 16 + j:17 + j], PS[1][:, j * 128:(j + 1) * 128],
                                                   ALU.mult, ALU.add), [b_s0, b_dec, b_PS[1]], [b_sfin])
                kb.dma("sync", o_ss[:, h, :, :].rearrange("j k v -> k j v"), sfin[:, :, :], [b_sfin], [], d_sfin)
                kb.act(sq_bf[:, :], O_, AF.Square, [b_T[T_O]], [b_sq])
                for i, (c0, c1) in enumerate(PARTS):
                    kb.mm(PS[1 + i][:, 0:c1 - c0], ones_bf[:, :], sq_bf[:, c0:c1], True, True, [b_const, b_sq], [b_PS[1 + i]])
                    kb.act(RS[:, c0:c1], PS[1 + i][:, 0:c1 - c0], AF.Ln, [b_PS[1 + i]], [b_T[T_RS]], bias=EPS, scale=1.0 / 128)
                kb.act(RS, RS, AF.Exp, [b_T[T_RS]], [b_T[T_RS]], scale=-0.5)
                kb.ve("scalar_tensor_tensor", (O_, O_, vecsT[:, V_ON:V_ON + 1], RS, ALU.mult, ALU.mult),
                      [b_T[T_O], b_vecs, b_T[T_RS]], [b_T[T_O]])
                kb.ve("tensor_tensor", (oT[:, h, :], O_, gate_bf[:, :], ALU.mult), [b_T[T_O], b_gate], [b_oT[h]])
            return load, run

        if upto >= 2:
            for h in range(H):
                jobs.append(make_main(h))

        PARTS_R = [(0, 512), (512, 1024), (1024, NT)]
        ALL_AH = b_T + [b_qt, b_kt, b_qi, b_ko, b_vt, b_sq, b_gate, b_vtok, b_kotok, b_attm, b_Sbf]
        ALL_AM = b_Sin + [b_s0, b_s0bf, b_sfin, b_sloc, b_dtot, b_dec, b_toks, b_attm, b_Dall] + b_Sst
        xblk = [carve(A_m, i * 4608, [128, 8, 128], F32) for i in range(2)]
        xblk_s = [carve(A_m, i * 4608 + 4096, [16, 128], F32) for i in range(2)]
        b_xblk = [buf(), buf()]
        d_xblk = [S.dma_sem("xblk0"), S.dma_sem("xblk1")]
        sg = carve(A_m, 9216, [128, NT], F32)
        b_sg = buf()
        rstd = scanmask
        b_rstd = b_scan
        xp_blk = xp_h.ap().rearrange("(t p) d -> p t d", p=128)
        fence = {"ah": True, "am": True}

        def psparts(sel, parts):
            return [(PS[3 * sel + i][:, 0:c1 - c0], b_PS[3 * sel + i]) for i, (c0, c1) in enumerate(parts)]

        def make_wout(gq):
            def load():
                return wload([(lambda w: w[:, :].rearrange("p (k n) -> p k n", k=KC), w_out_v[:, :, gq * 512:(gq + 1) * 512])])

            def run(slot):
                wv = WR[slot][:, :].rearrange("p (k n) -> p k n", k=KC)
                for oo in range(4):
                    o = gq * 4 + oo
                    i = o % 2
                    extra = ALL_AM if fence["am"] else []
                    kb.dma("sync", xblk[i][:, :, :], xp_blk[:, :, o * 128:(o + 1) * 128], [], [b_xblk[i]] + (extra if o < 2 else []), d_xblk[i])
                    kb.dma("sync", xblk_s[i][:, :], xs_h[:, o * 128:(o + 1) * 128], [], [b_xblk[i]], d_xblk[i])
                    pp = psparts(o % 2, PARTS_R)
                    for pi, (c0, c1) in enumerate(PARTS_R):
                        pap, pb = pp[pi]
                        for k in range(KC):
                            kb.mm(pap, wv[:, k, oo * 128:(oo + 1) * 128], oT[:, k, c0:c1], k == 0, False, [b_WR[slot]] + b_oT, [pb])
                        if pi < 2:
                            for tt in range(4):
                                t = pi * 4 + tt
                                kb.mm(pap[:, tt * 128:(tt + 1) * 128], xblk[i][:, t, :], ident_f[:, :], False, tt == 3,
                                      [b_xblk[i], b_const], [pb])
                        else:
                            kb.mm(pap, xblk_s[i][:, :], ident_f[0:16, 0:16], False, True, [b_xblk[i], b_const], [pb])
                        kb.act(A_h[:, o, c0:c1], pap, AF.Copy, [pb], [b_hT[o]] + ALL_AH)
            return load, run

        def norm_to_xT(gcol):
            for c in range(KC):
                kb.act(xT[:, c, :], A_h[:, c, :], AF.Square, [b_hT[c]], [b_xT])
            for pi, (c0, c1) in enumerate(PARTS):
                for c in range(KC):
                    kb.mm(PS[pi][:, 0:c1 - c0], ones_bf[:, :], xT[:, c, c0:c1], c == 0, c == KC - 1, [b_const, b_xT], [b_PS[pi]])
                kb.act(rstd[:, c0:c1], PS[pi][:, 0:c1 - c0], AF.Ln, [b_PS[pi]], [b_rstd], bias=EPS, scale=1.0 / D)
            kb.act(rstd[:, :], rstd[:, :], AF.Exp, [b_rstd], [b_rstd], scale=-0.5)
            for c in range(KC):
                kb.ve("scalar_tensor_tensor", (xT[:, c, :], A_h[:, c, :], vecsT[:, gcol + c:gcol + c + 1], rstd[:, :], ALU.mult, ALU.mult),
                      [b_hT[c], b_vecs, b_rstd], [b_xT])

        def add_ffn(l):
            w_gu_v = w_gu_h[l].rearrange("(k p) n -> p k n", p=128)
            w_dn_v = w_dn_h[l].rearrange("(k p) n -> p k n", p=128)
            def pre(_, l=l):
                if l == 1:
                    S.op("vector", lambda e: e.memset(fence_t[:, 0:1], 0.0), [], [b_sg, b_Vloc, b_Kloc, b_pm])
                norm_to_xT(V_NFFN + 16 * l)
            jobs.append((None, pre))
            for gi in range(4):
                tiles = list(range(gi * 11, gi * 11 + 11))
                pairs = [tiles[i:i + 2] for i in range(0, 11, 2)]
                for pr in pairs:
                    def load(pr=pr):
                        pcs = []
                        for n_, j in enumerate(pr):
                            pcs.append((lambda w, n_=n_: w[:, :].rearrange("p (k n) -> p k n", k=KC)[:, :, n_ * 256:n_ * 256 + 128],
                                        w_gu_v[:, :, j * 128:(j + 1) * 128]))
                            pcs.append((lambda w, n_=n_: w[:, :].rearrange("p (k n) -> p k n", k=KC)[:, :, n_ * 256 + 128:n_ * 256 + 256],
                                        w_gu_v[:, :, DFF + j * 128:DFF + (j + 1) * 128]))
                        return wload(pcs)

                    def run(slot, pr=pr, gi=gi):
                        for n_, j in enumerate(pr):
                            jj = j - gi * 11
                            pg = psparts(0, PARTS)
                            pu = psparts(1, PARTS)
                            proj(slot, n_ * 256, pg)
                            for i, (c0, c1) in enumerate(PARTS):
                                kb.act(sg[:, c0:c1], pg[i][0], AF.Silu, [pg[i][1]], [b_sg])
                            proj(slot, n_ * 256 + 128, pu)
                            for i, (c0, c1) in enumerate(PARTS):
                                kb.ve("tensor_tensor", (oT[:, jj, c0:c1], sg[:, c0:c1], pu[i][0], ALU.mult), [b_sg, pu[i][1]], [b_oT[jj]])
                    jobs.append((load, run))
                for oq in range(4):
                    def load(oq=oq, gi=gi):
                        return wload([(lambda w: w[:, 0:11 * 512].rearrange("p (k n) -> p k n", k=11),
                                       w_dn_v[:, gi * 11:gi * 11 + 11, oq * 512:(oq + 1) * 512])])

                    def run(slot, oq=oq, gi=gi):
                        wv = WR[slot][:, 0:11 * 512].rearrange("p (k n) -> p k n", k=11)
                        for oo in range(4):
                            o = oq * 4 + oo
                            pp = psparts(o % 2, PARTS)
                            for pi, (c0, c1) in enumerate(PARTS):
                                pap, pb = pp[pi]
                                for kk in range(11):
                                    kb.mm(pap, wv[:, kk, oo * 128:(oo + 1) * 128], oT[:, kk, c0:c1], kk == 0, kk == 10,
                                          [b_WR[slot], b_oT[kk]], [pb])
                                kb.ve("tensor_tensor", (A_h[:, o, c0:c1], A_h[:, o, c0:c1], pap, ALU.add), [b_hT[o], pb], [b_hT[o]])
                    jobs.append((load, run))

        kT = carve(oT_t, 0, [128, 4, NT], F32)
        vT = carve(oT_t, 4 * NT * 4, [128, 4, NT], F32)
        Kloc = carve(A_m, 0, [128, 4, NT], BF16)
        Vloc = carve(A_m, 8320, [128, 9, 512], BF16)
        Kloc32 = carve(A_m, 0, [128, 4, NT // 2], F32)
        Vloc32 = carve(A_m, 8320, [128, 9, 256], F32)
        b_Kloc, b_Vloc = buf(), buf()
        kstage = [sb("kstage%d" % i, [128, 512], F32) for i in range(2)] + [sb("kstage2", [16, 512], F32)]
        b_kstage = [buf(), buf(), buf()]
        d_kst = [S.dma_sem("kst0"), S.dma_sem("kst1"), S.dma_sem("kst2")]
        kmean_own = sb("kmean_own", [128, 4, 4], F32)
        b_kmo = buf()
        d_ckx = [S.dma_sem("ckx%d" % i) for i in range(4)]
        d_cvx = [S.dma_sem("cvx%d" % i) for i in range(4)]
        d_cmx = S.dma_sem("cmx")
        b_ck = [buf() for _ in range(4)]
        b_cv = [buf() for _ in range(4)]
        b_cm = buf()
        w_kv_v = w_kv_h.ap().rearrange("(k p) n -> p k n", p=128)

        def make_kv(which):
            def load():
                return wload([(lambda w: w[:, :].rearrange("p (k n) -> p k n", k=KC), w_kv_v[:, :, which * 512:(which + 1) * 512])])

            def run(slot):
                dstT = kT if which == 0 else vT
                for kvh in range(4):
                    pp = psparts(kvh % 2, PARTS)
                    proj(slot, kvh * 128, pp)
                    for i, (c0, c1) in enumerate(PARTS):
                        kb.act(dstT[:, kvh, c0:c1], pp[i][0], AF.Copy, [pp[i][1]], [b_oT[kvh + 4 * which]])
                if which == 0:
                    for kvh in range(4):
                        bk = b_oT[kvh]
                        kb.act(Kloc[:, kvh, :], kT[:, kvh, :], AF.Square, [bk], [b_Kloc] + ((b_xblk + [b_sg]) if kvh == 0 else []))
                        for pi, (c0, c1) in enumerate(PARTS):
                            kb.mm(PS[pi][:, 0:c1 - c0], ones_bf[:, :], Kloc[:, kvh, c0:c1], True, True, [b_const, b_Kloc], [b_PS[pi]])
                            kb.act(rstd[:, c0:c1], PS[pi][:, 0:c1 - c0], AF.Ln, [b_PS[pi]], [b_rstd], bias=EPS, scale=1.0 / 128)
                        kb.act(rstd[:, :], rstd[:, :], AF.Exp, [b_rstd], [b_rstd], scale=-0.5)
                        kb.ve("scalar_tensor_tensor", (kT[:, kvh, :], kT[:, kvh, :], vecsT[:, V_KN:V_KN + 1], rstd[:, :], ALU.mult, ALU.mult),
                              [bk, b_vecs, b_rstd], [bk])
                        kb.act(Kloc[:, kvh, :], kT[:, kvh, :], AF.Copy, [bk], [b_Kloc])
                        kb.ve("tensor_reduce", (kmean_own[:, kvh, :], kT[:, kvh, 0:LP].rearrange("p (b t) -> p b t", t=256), AX.X, ALU.add),
                              [bk], [b_kmo])
                        kb.dma("sync", ck_in[kvh].ap(), Kloc32[:, kvh, 0:LP // 2], [b_Kloc], [b_ck[kvh]], d_ckx[kvh])
                        if not cfg.get("nocc_k"):
                            kb.cc(GRP4, ck_in[kvh].ap().opt(), ck_mid[kvh].ap().opt(), [b_ck[kvh]], [b_ck[kvh]], d_cc)
                            kb.cc(PAIRS, ck_mid[kvh].ap().opt(), ck_out[kvh].ap().opt(), [b_ck[kvh]], [b_ck[kvh]], d_cc)
                    kb.ve("tensor_scalar_mul", (rstd[:, 0:16], kmean_own[:, :, :].rearrange("p a b -> p (a b)"), 1.0 / 256), [b_kmo], [b_rstd])
                    kb.dma("sync", cm_in.ap(), rstd[:, 0:512], [b_rstd], [b_cm], d_cmx)
                    if not cfg.get("nocc_m"):
                        kb.cc(GRP4, cm_in.ap().opt(), cm_mid.ap().opt(), [b_cm], [b_cm], d_cc)
                        kb.cc(PAIRS, cm_mid.ap().opt(), cm_out.ap().opt(), [b_cm], [b_cm], d_cc)
                o_p, o_s = (o_kp, o_ks) if which == 0 else (o_vp, o_vs)
                for t in range(0 if cfg.get("kv_noout") else 9):
                    rows = 128 if t < 8 else NS * ST
                    i = t % 2 if t < 8 else 2
                    for kvh in range(4):
                        kb.tr(PS[i][0:rows, kvh * 128:(kvh + 1) * 128], dstT[:, kvh, t * 128:t * 128 + rows], ident_f[:, :],
                              [b_oT[kvh + 4 * which], b_const], [b_PS[i]])
                    kb.act(kstage[i][0:rows, :], PS[i][0:rows, :], AF.Copy, [b_PS[i]], [b_kstage[i]])
                    if which == 1:
                        kb.ve("tensor_copy", (Vloc[0:rows, t, :], kstage[i][0:rows, :]), [b_kstage[i]], [b_Vloc] + ((b_xblk + [b_sg]) if t == 0 else []))
                    dst = o_p[t * 128:(t + 1) * 128, :] if t < 8 else o_s[:, :]
                    kb.dma("sync", dst, kstage[i][0:rows, :], [b_kstage[i]], [], d_kst[i])
                if which == 1 and not cfg.get("kv_noout"):
                    for kvh in range(4):
                        kb.dma("sync", cv_in[kvh].ap().rearrange("p (t d) -> p t d", d=64), Vloc32[:, 0:8, kvh * 64:(kvh + 1) * 64],
                               [b_Vloc], [b_cv[kvh]], d_cvx[kvh])
                        if not cfg.get("nocc_v"):
                            kb.cc(GRP4, cv_in[kvh].ap().opt(), cv_mid[kvh].ap().opt(), [b_cv[kvh]], [b_cv[kvh]], d_cc)
                            kb.cc(PAIRS, cv_mid[kvh].ap().opt(), cv_out[kvh].ap().opt(), [b_cv[kvh]], [b_cv[kvh]], d_cc)
                    out_bufs.extend(b_kstage)
            return load, run

        if upto >= 3:
            for gq in range(4):
                jobs.append(make_wout(gq))
        if upto >= 4:
            add_ffn(0)
        if upto >= 5:
            jobs.append((None, lambda _: norm_to_xT(V_NKV)))
            jobs.append(make_kv(0))
            jobs.append(make_kv(1))

        SCALE = 128.0 ** -0.5
        NEG = -30000.0
        QF = sb("QF", [128, NT], F32)
        biasS = sb("biasS", [128, 8, H, 32], BF16)
        b_QF, b_biasS = buf(), buf()
        GM = carve(kstage[0], 0, [128, 8, 32], F32)
        SEL = carve(kstage[0], 1024, [128, 8, 32], F32)
        kmean_all = carve(kstage[1], 0, [128, 4, 32], F32)
        max8 = carve(kstage[1], 512, [128, 8, 8], F32)
        thr = carve(kstage[1], 768, [128, 8], F32)
        kml = carve(kstage[1], 1024, [128, 8, 16], F32)
        b_gm, b_km = b_kstage[0], b_kstage[1]
        d_att = S.dma_sem("att")
        d_pm = S.dma_sem("pm")
        w_q_v = w_q_h.ap().rearrange("(k p) n -> p k n", p=128)
        w_o_v = w_o_h.ap().rearrange("(k p) n -> p k n", p=128)

        def load_kmean(_):
            kb.dma("sync", kml[:, :, :], cm_out.ap().rearrange("(r p) n -> p r n", p=128)[:, :, 0:16], [b_cm], [b_km], d_att)
            for kvh in range(4):
                kb.ve("tensor_copy", (kmean_all[:, kvh, :].rearrange("p (r b) -> p r b", b=4), kml[:, :, kvh * 4:(kvh + 1) * 4]),
                      [b_km], [b_km])
            kb.dma("sync", pm_sb[:, :, :], pastmask_h.ap().rearrange("p (t n) -> p t n", n=32), [], [b_pm] + ALL_AM + b_xblk + [b_sg], d_pm)

        pm_sb = carve(A_m, 17536, [128, 8, 32], F32)
        b_pm = buf()

        def gate_tiles(hd, src, rows, ntile, gm, sel, mx, th, pm, ps_ap, ps_b):
            kvh = hd // 4
            for t in range(ntile):
                kb.mm(ps_ap[0:rows, t * 32:(t + 1) * 32], src(t), kmean_all[:, kvh, :], True, True, [b_QF, b_km], [ps_b])
            g3 = ps_ap[0:rows, 0:ntile * 32].rearrange("p (t n) -> p t n", n=32)
            if pm is not None:
                kb.ve("tensor_tensor", (gm, g3, pm, ALU.add), [ps_b, b_pm], [b_gm])
            else:
                kb.ve("tensor_copy", (gm, g3), [ps_b], [b_gm])
            for t in range(ntile):
                kb.ve("max", (mx[:, t, :], gm[:, t, :]), [b_gm], [b_km])
            kb.ve("tensor_scalar_max", (th, mx[:, :, 2], -1e29), [b_km], [b_km])
            kb.ve("tensor_tensor", (sel, gm, th.unsqueeze(2).to_broadcast([rows, ntile, 32]), ALU.is_ge), [b_gm, b_km], [b_gm])
            return sel

        def make_q(gq):
            def load():
                return wload([(lambda w: w[:, :].rearrange("p (k n) -> p k n", k=KC), w_q_v[:, :, gq * 512:(gq + 1) * 512])])

            def run(slot):
                for hh in range(4):
                    hd = gq * 4 + hh
                    pp = psparts(hh % 2, PARTS)
                    proj(slot, hh * 128, pp)
                    for i, (c0, c1) in enumerate(PARTS):
                        kb.act(QF[:, c0:c1], pp[i][0], AF.Copy, [pp[i][1]], [b_QF])
                    fence_w = b_oT if hd == 0 else [b_oT[hd]]
                    kb.act(oT[:, hd, :], QF[:, :], AF.Square, [b_QF], fence_w)
                    for pi, (c0, c1) in enumerate(PARTS):
                        kb.mm(PS[pi][:, 0:c1 - c0], ones_bf[:, :], oT[:, hd, c0:c1], True, True, [b_const, b_oT[hd]], [b_PS[pi]])
                        kb.act(rstd[:, c0:c1], PS[pi][:, 0:c1 - c0], AF.Ln, [b_PS[pi]], [b_rstd], bias=EPS, scale=1.0 / 128)
                    kb.act(rstd[:, :], rstd[:, :], AF.Exp, [b_rstd], [b_rstd], scale=-0.5)
                    kb.ve("scalar_tensor_tensor", (QF[:, :], QF[:, :], vecsT[:, V_QN:V_QN + 1], rstd[:, :], ALU.mult, ALU.mult),
                          [b_QF, b_vecs, b_rstd], [b_QF])
                    kb.act(oT[:, hd, :], QF[:, :], AF.Copy, [b_QF], [b_oT[hd]])
                    sel = gate_tiles(hd, lambda t: QF[:, t * 128:(t + 1) * 128], 128, 8, GM, SEL, max8, thr, pm_sb[:, :, :], PS[4][:, 0:256], b_PS[4])
                    kb.ve("tensor_scalar", (biasS[:, 0:8, hd, :], sel, -1.0, -NEG, ALU.add, ALU.mult), [b_gm], [b_biasS])
                    kb.ve("tensor_copy", (qs_f32[:, :, hd, :], QF[:, LP:NT].rearrange("p (j t) -> p j t", t=ST)), [b_QF], [b_qs])
            return load, run

        NPT = 2
        Pt = [carve(WR_t[1], i * 1024, [128, 512], BF16) for i in range(NPT)]
        biasT = [carve(WR_t[1], 2048 + i * 1024, [32, 512], BF16) for i in range(2)]
        causalT = carve(WR_t[1], 4096, [128, 4, 128], BF16)
        rden = carve(WR_t[1], 5120, [128, 512], F32)
        cm_s = carve(WR_t[1], 7168, [16, NS, 16], BF16)
        Esel = carve(WR_t[1], 7680, [32, 32, 128], BF16)
        b_Pt = [buf() for _ in range(NPT)]
        b_biasT = [buf(), buf()]
        b_causal, b_rden = buf(), buf()
        KA = carve(xT_t, 0, [128, 8, LP], BF16)
        VA = carve(xT_t, 16384, [128, 8, 8, 128], BF16) if False else None
        VA32 = carve(xT_t, 16384, [128, 8, 512], F32)
        VAb = xT_t[:, 4096:8192].bitcast(BF16).rearrange("p (r t d) -> p r t d", r=8, t=8)
        KA32 = carve(xT_t, 0, [128, 8, LP // 2], F32)
        b_KA, b_VA = buf(), buf()
        d_KA = S.dma_sem("KA")

        ring_bufs = []

        def attention(_):
            if cfg.get("serial_att", True):
                S.serial_buf = Buf("serial")
            W0, W1 = b_WR[0], b_WR[1]
            fme = S.op("gpsimd", lambda e: e.memset(fence_t[:, 0:1], 0.0), [], [W0, W1])
            for b_ in b_Pt + b_biasT + [b_causal, b_rden]:
                b_.w = fme
                ring_bufs.append(b_)
            attention.fme = fme

            def fw():
                return []
            kb.ve("memset", (causalT, 0.0), [], [b_causal] + fw(), eng="gpsimd")
            kb.ve("affine_select", (causalT, causalT, [[0, 4], [1, 128]], ALU.is_ge, NEG), [b_causal], [b_causal], eng="gpsimd",
                  base=0, channel_multiplier=-1)
            kb.ve("memset", (Esel, 1.0), [], [b_causal], eng="gpsimd")
            kb.ve("affine_select", (Esel, Esel, [[-1, 32], [0, 128]], ALU.is_equal, 0.0), [b_causal], [b_causal], eng="gpsimd",
                  base=0, channel_multiplier=1)
            kb.ve("memset", (cm_s, 0.0), [], [b_causal], eng="gpsimd")
            for j in range(NS):
                kb.ve("affine_select", (cm_s[:, j, :].rearrange("p (h t) -> p h t", t=4), cm_s[:, j, :].rearrange("p (h t) -> p h t", t=4),
                                        [[0, 4], [1, 4]], ALU.is_ge, NEG), [b_causal], [b_causal], eng="gpsimd", base=4 * j, channel_multiplier=-1)
                kb.ve("affine_select", (cm_s[:, j, :], cm_s[:, j, :], [[0, 16]], ALU.is_ge, NEG), [b_causal], [b_causal], eng="gpsimd",
                      base=-4 * j, channel_multiplier=1)
            pcount = {"n": 0, "sc": 0}

            def key_tile(qrhs, ncol, kT_ap, v_ap, bias_mm, acc, den, accb, denb, first_t, last_t, kr, vr, kparts=128, scbank=None):
                si = pcount["sc"] % 2 if scbank is None else scbank
                pcount["sc"] += 1
                sc, scb = PS[si][0:kparts, 0:ncol], b_PS[si]
                kb.mm(sc, kT_ap, qrhs, True, bias_mm is None, kr + b_oT, [scb])
                if bias_mm is not None:
                    l_, r_, rb = bias_mm
                    kb.mm(sc, l_, r_, False, True, rb, [scb])
                pi = pcount["n"] % NPT
                pcount["n"] += 1
                kb.act(Pt[pi][0:kparts, 0:ncol], sc, AF.Exp, [scb], [b_Pt[pi]], scale=SCALE)
                kb.mm(acc, v_ap, Pt[pi][0:kparts, 0:ncol], first_t, last_t, vr + [b_Pt[pi]], [accb])
                kb.mm(den, ones_bf[0:kparts, :], Pt[pi][0:kparts, 0:ncol], first_t, last_t, [b_const, b_Pt[pi]], [denb])

            for kvh in range(4):
                kb.dma("sync", KA32[:, :, :], ck_out[kvh].ap().rearrange("(r p) n -> p r n", p=128), [b_ck[kvh]], [b_KA, b_xT], d_KA)
                kb.dma("sync", VA32[:, :, :], cv_out[kvh].ap().rearrange("(r p) n -> p r n", p=128), [b_cv[kvh]], [b_VA, b_xT], d_KA)
                for i in range(8):
                    qrhs = oT[:, 4 * kvh:4 * kvh + 4, i * 128:(i + 1) * 128]
                    bt = biasT[i % 2]
                    for h4 in range(4):
                        kb.tr(PT[0][0:32, h4 * 128:(h4 + 1) * 128], biasS[:, i, 4 * kvh + h4, :], ident_bf[:, :], [b_biasS, b_const], [b_PT[0]])
                    kb.ve("tensor_copy", (bt, PT[0][0:32, 0:512]), [b_PT[0]], [b_biasT[i % 2]])
                    nb = 28 + i // 2
                    acc, den = PS[2][:, :], PS[3][:, :]
                    for kt in range(2 * nb):
                        n = kt // 2
                        r, lc = kt // 8, (kt % 8) * 128
                        key_tile(qrhs, 512, KA[:, r, lc:lc + 128], VAb[:, r, kt % 8, :],
                                 (Esel[:, n, :], bt, [b_causal, b_biasT[i % 2]]),
                                 acc, den, b_PS[2], b_PS[3], kt == 0, False, [b_KA], [b_VA])
                    if i % 2 == 1:
                        key_tile(qrhs, 512, Kloc[:, kvh, (i - 1) * 128:i * 128], Vloc[:, i - 1, kvh * 128:(kvh + 1) * 128], None,
                                 acc, den, b_PS[2], b_PS[3], False, False, [b_Kloc], [b_Vloc])
                    key_tile(qrhs, 512, Kloc[:, kvh, i * 128:(i + 1) * 128], Vloc[:, i, kvh * 128:(kvh + 1) * 128],
                             (ident_bf[:, :], causalT.rearrange("p h t -> p (h t)"), [b_const, b_causal]),
                             acc, den, b_PS[2], b_PS[3], False, True, [b_Kloc], [b_Vloc])
                    kb.ve("reciprocal", (rden[:, :], den), [b_PS[3]], [b_rden])
                    kb.ve("tensor_tensor", (qrhs, acc.rearrange("p (h t) -> p h t", t=128), rden[:, :].rearrange("p (h t) -> p h t", t=128),
                                            ALU.mult), [b_PS[2], b_rden], [b_oT[4 * kvh + x] for x in range(4)])
            if with_cache:
                sample_attention(key_tile, fme)
            S.op("gpsimd", lambda e: e.memset(fence_t[:, 0:1], 0.0), [], [W0, W1] + ring_bufs)
            S.serial_buf = None

        qs_f32 = sb("qs_f32", [128, NS, H, ST], F32)
        b_qs = buf()

        def sample_attention(key_tile, fme):
            kpg = [carve(WR_t[0], i * 2048, [128, 512], F32) for i in range(2)]
            vpg = [carve(WR_t[0], 4096 + i * 2048, [128, 512], F32) for i in range(2)]
            kpb = [carve(WR_t[0], 8192 + i * 1024, [128, 512], BF16) for i in range(2)]
            vpb = [carve(WR_t[0], 10240 + i * 1024, [128, 512], BF16) for i in range(2)]
            KTs = [carve(WR_t[0], 12288 + i * 1024, [128, 4, 128], BF16) for i in range(2)]
            idx = carve(WR_t[0], 14336, [128, 64], I32)
            ptb = carve(WR_t[0], 14592, [128, 64], I32)
            iota_p = carve(WR_t[0], 14848, [128, 1], I32)
            km_s = carve(WR_t[0], 14852, [128, 4, 32], F32)
            bT_s = carve(WR_t[0], 15364, [32, 64], BF16)
            gm_s = carve(WR_t[0], 15492, [16, 4, 32], F32)
            sel_s = carve(WR_t[0], 16004, [16, 4, 32], BF16)
            mx_s = carve(WR_t[0], 16260, [16, 8], F32)
            th_s = carve(WR_t[0], 16292, [16, 1], F32)
            b_kpg, b_vpg, b_kpb, b_vpb, b_KTs = [[buf(), buf()] for _ in range(5)]
            b_idx, b_kms, b_bTs, b_gs_ = buf(), buf(), buf(), buf()
            for b_ in b_kpg + b_vpg + b_kpb + b_vpb + b_KTs + [b_idx, b_kms, b_bTs, b_gs_]:
                b_.w = fme
                ring_bufs.append(b_)
            d_kg = [S.dma_sem("kg0"), S.dma_sem("kg1")]
            d_vg = [S.dma_sem("vg0"), S.dma_sem("vg1")]
            d_pt = S.dma_sem("pt")
            kb.ve("iota", (iota_p, [[0, 1]]), [], [b_idx], eng="gpsimd", base=0, channel_multiplier=1)

            def gather(dst, bdst, dsem, src_h, pg):
                S.op("gpsimd", lambda e: e.indirect_dma_start(out=dst, out_offset=None, in_=src_h.ap(),
                                                              in_offset=bass.IndirectOffsetOnAxis(ap=idx[:, pg:pg + 1], axis=0)),
                     [b_idx], [bdst], dsem=dsem)

            for j in range(NS):
                kb.dma("sync", ptb[:, :], pt_h[j:j + 1, :].partition_broadcast(128), [], [b_idx], d_pt)
                kb.ve("tensor_scalar", (idx[:, :], ptb[:, :], 128, iota_p[:, 0:1], ALU.mult, ALU.add), [b_idx], [b_idx])
                for pg in range(64):
                    i = pg % 2
                    gather(kpg[i][:, :], b_kpg[i], d_kg[i], ck_h, pg)
                    kb.act(kpb[i][:, :], kpg[i][:, :], AF.Copy, [b_kpg[i]], [b_kpb[i]])
                    for kvh in range(4):
                        col = kvh * 32 + pg // 2
                        kb.mm(PS[4][:, col:col + 1], kpb[i][:, kvh * 128:(kvh + 1) * 128], ones_bf[:, 0:1],
                              pg == 0 and kvh == 0, pg == 63 and kvh == 3, [b_kpb[i], b_const], [b_PS[4]])
                kb.ve("tensor_scalar_mul", (km_s.rearrange("p a b -> p (a b)"), PS[4][:, 0:128], 1.0 / 256), [b_PS[4]], [b_kms])
                for kvh in range(4):
                    kb.mm(PS[5][0:16, kvh * 32:(kvh + 1) * 32], qs_f32[:, j, 4 * kvh:4 * kvh + 4, :].rearrange("p h t -> p (h t)"),
                          km_s[:, kvh, :], True, True, [b_qs, b_kms], [b_PS[5]])
                kb.ve("tensor_copy", (gm_s, PS[5][0:16, 0:128].rearrange("p (a b) -> p a b", b=32)), [b_PS[5]], [b_gs_])
                for kvh in range(4):
                    kb.ve("max", (mx_s[:, :], gm_s[:, kvh, :]), [b_gs_], [b_gs_])
                    kb.ve("tensor_scalar", (sel_s[:, kvh, :], gm_s[:, kvh, :], mx_s[:, 2:3], None, ALU.is_ge), [b_gs_], [b_gs_])
                kb.ve("tensor_scalar", (sel_s[:, :, :], sel_s[:, :, :], -1.0, -NEG, ALU.add, ALU.mult), [b_gs_], [b_gs_])
                for kvh in range(4):
                    kb.tr(PT[0][0:32, kvh * 16:(kvh + 1) * 16], sel_s[:, kvh, :], ident_bf[0:16, 0:16], [b_gs_, b_const], [b_PT[0]])
                kb.ve("tensor_copy", (bT_s, PT[0][0:32, 0:64]), [b_PT[0]], [b_bTs])
                for pg in range(64):
                    i = pg % 2
                    n = pg // 2
                    gather(kpg[i][:, :], b_kpg[i], d_kg[i], ck_h, pg)
                    gather(vpg[i][:, :], b_vpg[i], d_vg[i], cvv_h, pg)
                    kb.act(kpb[i][:, :], kpg[i][:, :], AF.Copy, [b_kpg[i]], [b_kpb[i]])
                    kb.ve("tensor_copy", (vpb[i][:, :], vpg[i][:, :]), [b_vpg[i]], [b_vpb[i]])
                    for kvh in range(4):
                        kb.tr(PT[1][:, kvh * 128:(kvh + 1) * 128], kpb[i][:, kvh * 128:(kvh + 1) * 128], ident_bf[:, :],
                              [b_kpb[i], b_const], [b_PT[1]])
                    kb.ve("tensor_copy", (KTs[i], PT[1][:, 0:512].rearrange("p (a b) -> p a b", b=128)), [b_PT[1]], [b_KTs[i]])
                    for kvh in range(4):
                        qr = oT[:, 4 * kvh:4 * kvh + 4, LP + 4 * j:LP + 4 * j + 4]
                        key_tile(qr, 16, KTs[i][:, kvh, :], vpb[i][:, kvh * 128:(kvh + 1) * 128],
                                 (Esel[:, n, :], bT_s[:, kvh * 16:(kvh + 1) * 16], [b_causal, b_bTs]),
                                 PS[2][:, kvh * 16:(kvh + 1) * 16], PS[3][:, kvh * 16:(kvh + 1) * 16], b_PS[2], b_PS[3],
                                 pg == 0 and kvh == 0, False, [b_KTs[i]], [b_vpb[i]])
                for kvh in range(4):
                    qr = oT[:, 4 * kvh:4 * kvh + 4, LP + 4 * j:LP + 4 * j + 4]
                    key_tile(qr, 16, Kloc[:, kvh, LP:NT], Vloc[0:16, 8, kvh * 128:(kvh + 1) * 128],
                             (ident_bf[0:16, 0:16], cm_s[:, j, :], [b_const, b_causal]),
                             PS[2][:, kvh * 16:(kvh + 1) * 16], PS[3][:, kvh * 16:(kvh + 1) * 16], b_PS[2], b_PS[3],
                             False, kvh == 3, [b_Kloc], [b_Vloc], kparts=16, scbank=5)
                kb.ve("reciprocal", (rden[:, 0:64], PS[3][:, 0:64]), [b_PS[3]], [b_rden])
                for kvh in range(4):
                    qr = oT[:, 4 * kvh:4 * kvh + 4, LP + 4 * j:LP + 4 * j + 4]
                    kb.ve("tensor_tensor", (qr, PS[2][:, kvh * 16:(kvh + 1) * 16].rearrange("p (h t) -> p h t", t=4),
                                            rden[:, kvh * 16:(kvh + 1) * 16].rearrange("p (h t) -> p h t", t=4), ALU.mult),
                          [b_PS[2], b_rden], [b_oT[4 * kvh + x] for x in range(4)])

        RING_SCRATCH = []

        def make_wo(gq):
            def load():
                return wload([(lambda w: w[:, :].rearrange("p (k n) -> p k n", k=KC), w_o_v[:, :, gq * 512:(gq + 1) * 512])])

            def run(slot):
                wv = WR[slot][:, :].rearrange("p (k n) -> p k n", k=KC)
                for oo in range(4):
                    o = gq * 4 + oo
                    pp = psparts(o % 2, PARTS)
                    for pi, (c0, c1) in enumerate(PARTS):
                        pap, pb = pp[pi]
                        for k in range(KC):
                            kb.mm(pap, wv[:, k, oo * 128:(oo + 1) * 128], oT[:, k, c0:c1], k == 0, k == KC - 1, [b_WR[slot], b_oT[k]], [pb])
                        kb.ve("tensor_tensor", (A_h[:, o, c0:c1], A_h[:, o, c0:c1], pap, ALU.add), [b_hT[o], pb], [b_hT[o]])
            return load, run

        def write_y(_):
            for t in range(9):
                rows = 128 if t < 8 else NS * ST
                i = t % 2 if t < 8 else 2
                for cg in range(4):
                    for cc in range(4):
                        c = cg * 4 + cc
                        kb.tr(PS[i][0:rows, cc * 128:(cc + 1) * 128], A_h[:, c, t * 128:t * 128 + rows], ident_f[:, :],
                              [b_hT[c], b_const], [b_PS[i]])
                    kb.act(kstage[i][0:rows, :], PS[i][0:rows, :], AF.Copy, [b_PS[i]], [b_kstage[i]])
                    dst = o_yp[t * 128:(t + 1) * 128, cg * 512:(cg + 1) * 512] if t < 8 else o_ys[:, cg * 512:(cg + 1) * 512]
                    kb.dma("sync", dst, kstage[i][0:rows, :], [b_kstage[i]], [], d_kst[i])
            out_bufs.extend(b_kstage)

        if upto >= 6:
            jobs.append((None, lambda _: norm_to_xT(V_NMB)))
            jobs.append((None, load_kmean))
            for gq in range(4):
                jobs.append(make_q(gq))
            if cfg.get("dbg2"):
                def dbg2(_):
                    d_d2 = S.dma_sem("dbg2")
                    kb.dma("sync", o_km.ap(), kmean_all.rearrange("p a b -> p (a b)"), [b_km], [], d_d2)
                    kb.dma("gpsimd", o_bias.ap(), biasS[:, :, :, :].rearrange("p a b c -> p (a b c)"), [b_biasS], [], d_d2)
                    out_bufs.extend([b_km, b_biasS])
                jobs.append((None, dbg2))
            jobs.append(("NOPREFETCH", attention))
            for gq in range(4):
                jobs.append(make_wo(gq))
        if upto >= 7:
            add_ffn(1)
            jobs.append((None, write_y))

        def dbg_out(_):
            d_dbg = S.dma_sem("dbg")
            which = cfg.get("dbg", "xT")
            if which == "xT":
                kb.ve("tensor_copy", (A_h[:, :, :], xT[:, :, :]), [b_xT] + b_T + [b_vtok, b_kotok, b_attm, b_Sbf, b_qt, b_kt, b_qi, b_ko, b_vt, b_sq, b_gate], b_T)
            elif which == "hT":
                pass
            elif which == "oT":
                kb.ve("tensor_copy", (A_h[:, :, :], oT[:, :, :]), b_oT + b_T + [b_vtok, b_kotok, b_attm, b_Sbf, b_qt, b_kt, b_qi, b_ko, b_vt, b_sq, b_gate], b_T)
            kb.dma("sync", o_dbg.ap(), A_h[:, :, :], b_T + b_hT, [], d_dbg)
            out_bufs.extend(b_T + b_hT)

        if cfg.get("dbg"):
            jobs.append((None, dbg_out))

        run_jobs()
        out_bufs.extend([b_sfin])
        S.final_wait("sync", out_bufs)
        S.replay()
    return nc


def pack_vecs(inp):
    v = np.zeros((128, 128), np.float32)
    v[0:32] = np.asarray(inp["lb_logits"], np.float32).reshape(32, 128)
    v[32:48] = np.asarray(inp["norm_mix_a"], np.float32).reshape(16, 128)
    v[48:80] = np.asarray(inp["norm_ffn"], np.float32).reshape(32, 128)
    v[80:96] = np.asarray(inp["norm_kv"], np.float32).reshape(16, 128)
    v[96:112] = np.asarray(inp["norm_mix_b"], np.float32).reshape(16, 128)
    v[112] = np.asarray(inp["onorm_a"], np.float32).reshape(128)
    v[113] = np.asarray(inp["k_norm"], np.float32).reshape(128)
    v[114] = np.asarray(inp["q_norm"], np.float32).reshape(128)
    return v


def make_in_maps(inp, cfg=None):
    vecs = pack_vecs(inp)
    maps = []
    for c in range(NCORES):
        oh = np.zeros((128, 8), np.float32)
        oh[:, c] = 1.0
        m = {
            "xp": np.ascontiguousarray(inp["x_prompt"][0, c * LP:(c + 1) * LP]),
            "xs": np.ascontiguousarray(inp["x_sample"][c * NS:(c + 1) * NS].reshape(NS * ST, D)),
            "s0": np.ascontiguousarray(inp["state_hgrn"][0, c * NS:(c + 1) * NS]),
            "vecs": vecs,
            "onehot": oh,
            "w_in": np.ascontiguousarray(inp["w_in_a"][0]),
            "w_out": np.ascontiguousarray(inp["w_out_a"][0]),
            "w_gu": np.asarray(inp["w_gate_up"]),
            "w_dn": np.asarray(inp["w_down"]),
            "w_kv": np.asarray(inp["w_kv"]),
            "w_q": np.ascontiguousarray(inp["w_q_b"][0]),
            "w_o": np.ascontiguousarray(inp["w_o_b"][0]),
        }
        pm = np.full((128, 8, 32), -1e30, np.float32)
        for i in range(8):
            pm[:, i, :4 * c + i // 2] = 0.0
        m["pastmask"] = pm.reshape(128, 256)
        if (cfg or {}).get("npages"):
            ptc = np.asarray(inp["page_table"][c * NS:(c + 1) * NS]).reshape(-1)
            m["cache_k"] = np.ascontiguousarray(np.asarray(inp["cache_k"])[ptc]).reshape(256 * 128, 512)
            m["cache_v"] = np.ascontiguousarray(np.asarray(inp["cache_v"])[ptc]).reshape(256 * 128, 512)
            m["pt"] = np.arange(256, dtype=np.int32).reshape(NS, 64)
        elif not (cfg or {}).get("nocache"):
            m["cache_k"] = np.asarray(inp["cache_k"]).reshape(2560 * 128, 512)
            m["cache_v"] = np.asarray(inp["cache_v"]).reshape(2560 * 128, 512)
            m["pt"] = np.ascontiguousarray(inp["page_table"][c * NS:(c + 1) * NS]).astype(np.int32)
        maps.append(m)
    return maps


def kernel(**inputs):
    inp = {k: np.asarray(v) for k, v in inputs.items()}
    nc = build()
    res = run_bass_kernel_spmd(nc, make_in_maps(inp), core_ids=list(range(NCORES)))
    r = res.results
    cat = lambda name: np.concatenate([np.asarray(r[c][name]) for c in range(NCORES)], axis=0)
    y_prompt = cat("o_yp").reshape(1, NCORES * LP, D).astype(np.float32)
    y_sample = cat("o_ys").reshape(NCORES * NS, ST, D).astype(np.float32)
    s_prompt = np.asarray(r[NCORES - 1]["o_sp"]).reshape(1, 1, H, 128, 128).astype(np.float32)
    s_sample = cat("o_ss").reshape(1, NCORES * NS, H, 128, 128).astype(np.float32)
    k_prompt = cat("o_kp").reshape(1, NCORES * LP, 4, 128).astype(np.float32)
    v_prompt = cat("o_vp").reshape(1, NCORES * LP, 4, 128).astype(np.float32)
    k_sample = cat("o_ks").reshape(NCORES * NS, ST, 4, 128).astype(np.float32)
    v_sample = cat("o_vs").reshape(NCORES * NS, ST, 4, 128).astype(np.float32)
    return (y_prompt, y_sample, s_prompt, s_sample, k_prompt, v_prompt, k_sample, v_sample)
```

```python
import numpy as np
from contextlib import ExitStack
import concourse.bass as bass
import concourse.mybir as mybir
from concourse.bass_utils import run_bass_kernel_spmd

F32 = mybir.dt.float32
BF16 = mybir.dt.bfloat16
I32 = mybir.dt.int32
ALU = mybir.AluOpType
AF = mybir.ActivationFunctionType
AX = mybir.AxisListType

NCORES = 8
D = 2048
KC = 16
LP = 1024
NS = 4
ST = 4
NT = LP + NS * ST
H = 16
DFF = 5632
FC = DFF // 128
EPS = 1e-6
PARTS = [(0, 347), (347, 694), (694, 1040)]
ENGS = ("tensor", "vector", "scalar", "gpsimd", "sync")
GRP4 = [[0, 1, 2, 3], [4, 5, 6, 7]]
PAIRS = [[0, 4], [1, 5], [2, 6], [3, 7]]


class Buf:
    __slots__ = ("name", "w", "r")

    def __init__(self, name):
        self.name = name
        self.w = None
        self.r = {}


class Sched:
    def __init__(self, nc, stack):
        self.nc = nc
        self.stack = stack
        self.q = {e: [] for e in ENGS}
        self.cnt = {e: 0 for e in ENGS}
        self.sems = {}
        for e in ENGS:
            self.sems["E_" + e] = stack.enter_context(nc.semaphore("sem_" + e))
        self.seen = {e: {} for e in ENGS}
        self.dmacnt = {}

    def dma_sem(self, name):
        key = "D_" + name
        assert key not in self.sems
        self.sems[key] = self.stack.enter_context(self.nc.semaphore("dsem_" + name))
        self.dmacnt[key] = 0
        return key

    def op(self, eng, fn, reads=(), writes=(), dsem=None, inc=16):
        if getattr(self, "serial_buf", None) is not None:
            writes = list(writes) + [self.serial_buf]
        need = {}
        for b in reads:
            if b.w is not None:
                k, v = b.w
                if need.get(k, 0) < v:
                    need[k] = v
        for b in writes:
            if b.w is not None:
                k, v = b.w
                if need.get(k, 0) < v:
                    need[k] = v
            for k, v in b.r.items():
                if need.get(k, 0) < v:
                    need[k] = v
        if eng == "tensor":
            need.pop("E_tensor", None)
        for k in need:
            if k.startswith("D_"):
                need[k] = max(need[k], self.dmacnt[k])
        seen = self.seen[eng]
        waits = [(k, v) for k, v in need.items() if seen.get(k, 0) < v]
        for k, v in waits:
            seen[k] = v
        if dsem is None:
            self.cnt[eng] += 1
            me = ("E_" + eng, self.cnt[eng])
            incr = ("E_" + eng, 1)
        else:
            self.dmacnt[dsem] += inc
            me = (dsem, self.dmacnt[dsem])
            incr = (dsem, inc)
        self.q[eng].append((waits, fn, incr))
        for b in reads:
            if b.r.get(me[0], 0) < me[1]:
                b.r[me[0]] = me[1]
        for b in writes:
            b.w = me
            b.r = {}
        return me

    def final_wait(self, eng, bufs):
        need = {}
        for b in bufs:
            deps = list(b.r.items()) + ([b.w] if b.w else [])
            for k, v in deps:
                need[k] = max(need.get(k, 0), v)
        waits = [(k, v) for k, v in need.items() if self.seen[eng].get(k, 0) < v]
        for k, v in waits:
            self.seen[eng][k] = v
        self.q[eng].append((waits, None, None))

    def replay(self):
        nc, sems, q = self.nc, self.sems, self.q

        def run(name, e):
            for waits, fn, inc in q[name]:
                for k, v in waits:
                    e.wait_ge(sems[k], v)
                if fn is not None:
                    fn(e).then_inc(sems[inc[0]], inc[1])

        with nc.Block() as block:
            @block.tensor
            def _(e):
                run("tensor", e)

            @block.vector
            def _(e):
                run("vector", e)

            @block.scalar
            def _(e):
                run("scalar", e)

            @block.gpsimd
            def _(e):
                run("gpsimd", e)

            @block.sync
            def _(e):
                run("sync", e)


class KB:
    def __init__(self, nc, st):
        self.nc = nc
        self.st = st
        self.S = Sched(nc, st)
        self.nbuf = 0

    def sb(self, name, shape, dt):
        return self.st.enter_context(self.nc.sbuf_tensor("s_" + name, shape, dt))

    def ps(self, name, shape, dt):
        return self.st.enter_context(self.nc.psum_tensor("p_" + name, shape, dt))

    def buf(self, name=None):
        self.nbuf += 1
        return Buf(name or "b%d" % self.nbuf)

    def mm(self, out, lhsT, rhs, start, stop, r, w):
        self.S.op("tensor", lambda e: e.matmul(out, lhsT, rhs, start=start, stop=stop), r, w)

    def tr(self, out, in_, ident, r, w):
        self.S.op("tensor", lambda e: e.transpose(out, in_, ident), r, w)

    def act(self, out, in_, func, r, w, bias=None, scale=None, accum_out=None):
        kw = {}
        if bias is not None:
            kw["bias"] = bias
        if scale is not None:
            kw["scale"] = scale
        if accum_out is not None:
            kw["accum_out"] = accum_out
        self.S.op("scalar", lambda e: e.activation(out, in_, func, **kw), r, w)

    def ve(self, method, args, r, w, eng="vector", **kw):
        self.S.op(eng, lambda e: getattr(e, method)(*args, **kw), r, w)

    def dma(self, eng, out, in_, r, w, dsem, **kw):
        self.S.op(eng, lambda e: e.dma_start(out=out, in_=in_, **kw), r, w, dsem=dsem)

    def cc(self, groups, in_ap, out_ap, r, w, dsem):
        self.ncc = getattr(self, "ncc", 0) + 1
        own = self.S.dma_sem("cc%d" % self.ncc)
        if not hasattr(self, "cc_chain"):
            self.cc_chain = Buf("cc_chain")
        self.S.op("gpsimd", lambda e: e.collective_compute("AllGather", ALU.bypass, replica_groups=groups,
                                                           ins=[in_ap], outs=[out_ap]), r, list(w), dsem=own, inc=1)


def build(cfg=None):
    cfg = cfg or {}
    upto = cfg.get("upto", 99)
    nc = bass.Bass("TRN2", target_bir_lowering=False)

    def din(name, shape, dt=F32):
        return nc.dram_tensor(name, list(shape), dt, kind="ExternalInput")

    def dout(name, shape, dt=F32):
        return nc.dram_tensor(name, list(shape), dt, kind="ExternalOutput")

    xp_h = din("xp", [LP, D])
    xs_h = din("xs", [NS * ST, D])
    s0_h = din("s0", [NS, H, 128, 128])
    vecs_h = din("vecs", [128, 128])
    onehot_h = din("onehot", [128, 8])
    w_in_h = din("w_in", [D, 4 * D])
    w_out_h = din("w_out", [D, D])
    w_gu_h = din("w_gu", [2, D, 2 * DFF])
    w_dn_h = din("w_dn", [2, DFF, D])
    w_kv_h = din("w_kv", [D, 1024])
    w_q_h = din("w_q", [D, D])
    w_o_h = din("w_o", [D, D])
    pastmask_h = din("pastmask", [128, 8 * 32])
    with_cache = not cfg.get("nocache")
    if with_cache:
        npages = cfg.get("npages", 2560)
        ck_h = din("cache_k", [npages * 128, 512])
        cvv_h = din("cache_v", [npages * 128, 512])
        pt_h = din("pt", [NS, 64], I32)
    o_yp = dout("o_yp", [LP, D])
    o_ys = dout("o_ys", [NS * ST, D])
    o_kp = dout("o_kp", [LP, 512])
    o_vp = dout("o_vp", [LP, 512])
    o_ks = dout("o_ks", [NS * ST, 512])
    o_vs = dout("o_vs", [NS * ST, 512])
    o_sp = dout("o_sp", [H, 128, 128])
    o_ss = dout("o_ss", [NS, H, 128, 128])
    o_dbg = dout("o_dbg", [128, KC, NT]) if cfg.get("dbg") else None
    o_km = dout("o_km", [128, 128]) if cfg.get("dbg2") else None
    o_bias = dout("o_bias", [128, 8 * H * 32]) if cfg.get("dbg2") else None

    gs_in = [nc.dram_tensor("gs_in%d" % g, [128, 512], F32) for g in range(4)]
    gs_mid = [nc.dram_tensor("gs_mid%d" % g, [4 * 128, 512], F32) for g in range(4)]
    gs_out = [nc.dram_tensor("gs_out%d" % g, [8 * 128, 512], F32) for g in range(4)]
    gd_in = nc.dram_tensor("gd_in", [128, 512], F32)
    gd_mid = nc.dram_tensor("gd_mid", [4 * 128, 512], F32)
    gd_out = nc.dram_tensor("gd_out", [8 * 128, 512], F32)
    ck_in = [nc.dram_tensor("ck_in%d" % g, [128, LP // 2], F32) for g in range(4)]
    ck_mid = [nc.dram_tensor("ck_mid%d" % g, [4 * 128, LP // 2], F32) for g in range(4)]
    ck_out = [nc.dram_tensor("ck_out%d" % g, [8 * 128, LP // 2], F32) for g in range(4)]
    cv_in = [nc.dram_tensor("cv_in%d" % g, [128, 512], F32) for g in range(4)]
    cv_mid = [nc.dram_tensor("cv_mid%d" % g, [4 * 128, 512], F32) for g in range(4)]
    cv_out = [nc.dram_tensor("cv_out%d" % g, [8 * 128, 512], F32) for g in range(4)]
    cm_in = nc.dram_tensor("cm_in", [128, 512], F32)
    cm_mid = nc.dram_tensor("cm_mid", [4 * 128, 512], F32)
    cm_out = nc.dram_tensor("cm_out", [8 * 128, 512], F32)

    with ExitStack() as st:
        kb = KB(nc, st)
        S = kb.S
        sb, ps, buf = kb.sb, kb.ps, kb.buf

        def arena(name, nbytes):
            return sb(name, [128, nbytes // 4], F32)

        def carve(ar, off, shape, dt):
            free = 1
            for d_ in shape[1:]:
                free *= d_
            nb = free * (2 if dt == BF16 else 4)
            assert off % 4 == 0 and nb % 4 == 0
            ap = ar[0:shape[0], off // 4:(off + nb) // 4]
            if dt != F32:
                ap = ap.bitcast(dt)
            if len(shape) == 3:
                ap = ap.rearrange("p (a b) -> p a b", b=shape[2])
            return ap

        A_h_t = arena("A_h", KC * NT * 4)
        A_h = carve(A_h_t, 0, [128, KC, NT], F32)
        xT_t = arena("xT", KC * NT * 2)
        xT = carve(xT_t, 0, [128, KC, NT], BF16)
        oT_t = arena("oT", KC * NT * 2)
        oT = carve(oT_t, 0, [128, KC, NT], BF16)
        NW = 2
        WR_t = [arena("wr%d" % i, 16384) for i in range(NW)]
        WR = [carve(WR_t[i], 0, [128, 8192], BF16) for i in range(NW)]
        A_m = arena("A_m", 19456)
        b_hT = [buf("hT%d" % c) for c in range(KC)]
        b_xT = buf("xT")
        b_oT = [buf("oT%d" % c) for c in range(KC)]
        b_WR = [buf("wr%d" % i) for i in range(NW)]
        d_WR = [S.dma_sem("wr%d" % i) for i in range(NW)]
        TSZ = NT * 4
        b_T = [buf("T%d" % i) for i in range(9)]

        ident_bf = sb("ident_bf", [128, 128], BF16)
        ident_f = sb("ident_f", [128, 128], F32)
        ones_bf = sb("ones_bf", [128, 128], BF16)
        triu = sb("triu", [64, 8, 64], BF16)
        smask = sb("smask", [16, 16], F32)
        rowsel = sb("rowsel", [16, NS], F32)
        vecsT = sb("vecsT", [128, 128], F32)
        vecs_sb = sb("vecs_sb", [128, 128], F32)
        lbt = sb("lbt", [128, 2, H], F32)
        onehot = sb("onehot", [128, 8], F32)
        scanmask = sb("scanmask", [128, NT], F32)
        b_scan = buf("scanmask")
        onecol = sb("onecol", [128, 1], F32)
        b_const = buf("const")
        b_vecs = buf("vecs")
        fence_t = sb("fence_t", [128, 1], F32)

        PS = [ps("ps%d" % i, [128, 512], F32) for i in range(6)]
        PT = [ps("pt%d" % i, [128, 1024], BF16) for i in range(2)]
        b_PS = [buf("ps%d" % i) for i in range(6)]
        b_PT = [buf("pt%d" % i) for i in range(2)]

        d_ld = S.dma_sem("ld")
        d_c = S.dma_sem("const")
        d_out = S.dma_sem("out")
        d_cc = S.dma_sem("cc")
        out_bufs = []

        g = "gpsimd"
        kb.ve("memset", (ident_bf[:], 1.0), [], [b_const], eng=g)
        kb.ve("affine_select", (ident_bf[:], ident_bf[:], [[-1, 128]], ALU.is_equal, 0.0), [b_const], [b_const], eng=g,
              base=0, channel_multiplier=1)
        kb.ve("memset", (ident_f[:], 1.0), [], [b_const], eng=g)
        kb.ve("affine_select", (ident_f[:], ident_f[:], [[-1, 128]], ALU.is_equal, 0.0), [b_const], [b_const], eng=g,
              base=0, channel_multiplier=1)
        kb.ve("memset", (ones_bf[:], 1.0), [], [b_const], eng=g)
        kb.ve("memset", (triu[:], 1.0), [], [b_const], eng=g)
        kb.ve("affine_select", (triu[:], triu[:], [[0, 8], [1, 64]], ALU.is_ge, 0.0), [b_const], [b_const], eng=g,
              base=0, channel_multiplier=-1)
        kb.ve("memset", (smask[:], 1.0), [], [b_const], eng=g)
        kb.ve("affine_select", (smask[:], smask[:], [[1, 16]], ALU.is_ge, 0.0), [b_const], [b_const], eng=g,
              base=0, channel_multiplier=-1)
        for j in range(NS):
            kb.ve("affine_select", (smask[:, 4 * j:4 * j + 4], smask[:, 4 * j:4 * j + 4], [[0, 4]], ALU.is_ge, 0.0),
                  [b_const], [b_const], eng=g, base=-4 * j, channel_multiplier=1)
        kb.ve("memset", (rowsel[:], 1.0), [], [b_const], eng=g)
        kb.ve("affine_select", (rowsel[:], rowsel[:], [[-4, NS]], ALU.is_ge, 0.0), [b_const], [b_const], eng=g,
              base=0, channel_multiplier=1)
        kb.ve("affine_select", (rowsel[:], rowsel[:], [[4, NS]], ALU.is_ge, 0.0), [b_const], [b_const], eng=g,
              base=3, channel_multiplier=-1)
        kb.ve("memset", (scanmask[:], 1.0), [], [b_scan], eng=g)
        kb.ve("memset", (scanmask[:, 0:LP].rearrange("p (c t) -> p c t", t=64)[:, :, 0:1], 0.0), [], [b_scan], eng=g)
        kb.ve("memset", (scanmask[:, LP:NT].rearrange("p (c t) -> p c t", t=ST)[:, :, 0:1], 0.0), [], [b_scan], eng=g)

        kb.dma("sync", vecs_sb[:], vecs_h.ap(), [], [b_vecs], d_c)
        d_c2 = S.dma_sem("const2")
        b_oh = buf("onehot")
        kb.dma("sync", onehot[:], onehot_h.ap(), [], [b_oh], d_c2)
        kb.tr(PS[0][:, 0:128], vecs_sb[:], ident_f[:], [b_vecs, b_const], [b_PS[0]])
        kb.ve("tensor_copy", (vecsT[:], PS[0][:, 0:128]), [b_PS[0]], [b_vecs])
        kb.ve("tensor_sub", (lbt[:, 0, :], vecsT[:, 16:32], vecsT[:, 0:16]), [b_vecs], [b_vecs])
        kb.act(lbt[:, 0, :], lbt[:, 0, :], AF.Exp, [b_vecs], [b_vecs])
        kb.ve("tensor_scalar_add", (lbt[:, 0, :], lbt[:, 0, :], 1.0), [b_vecs], [b_vecs])
        kb.ve("reciprocal", (lbt[:, 0, :], lbt[:, 0, :]), [b_vecs], [b_vecs])
        kb.ve("tensor_scalar", (lbt[:, 1, :], lbt[:, 0, :], -1.0, 1.0, ALU.mult, ALU.add), [b_vecs], [b_vecs])
        V_NMA, V_NFFN, V_NKV, V_NMB, V_ON, V_KN, V_QN = 32, 48, 80, 96, 112, 113, 114

        wstate = {"n": 0}

        def wload(pieces):
            s = wstate["n"] % NW
            wstate["n"] += 1
            for dst_fn, src in pieces:
                kb.dma("gpsimd", dst_fn(WR[s]), src, [], [b_WR[s]], d_WR[s])
            return s

        jobs = []

        def run_jobs():
            loaded = {}
            order = [i for i, j in enumerate(jobs) if j[0] is not None and j[0] != "NOPREFETCH"]
            issued = 0
            for i, (ld, run) in enumerate(jobs):
                p = next((n for n, ii in enumerate(order) if ii >= i), len(order))
                tgt = min(p + NW, len(order))
                nxt_np = next((ii for ii in range(i, len(jobs)) if jobs[ii][0] == "NOPREFETCH"), None)
                if nxt_np is not None:
                    tgt = min(tgt, sum(1 for ii in order if ii < nxt_np))
                while issued < tgt:
                    loaded[order[issued]] = jobs[order[issued]][0]()
                    issued += 1
                run(loaded.get(i))

        w_in_v = w_in_h.ap().rearrange("(k p) n -> p k n", p=128)
        w_out_v = w_out_h.ap().rearrange("(k p) n -> p k n", p=128)

        xst = [carve(A_h_t, 0, [128, D], F32), carve(A_h_t, 2 * TSZ, [128, D], F32)]
        xsb = [carve(A_h_t, 4 * TSZ, [128, D], BF16), carve(A_h_t, 5 * TSZ, [128, D], BF16)]
        bl_xst = [[b_T[0], b_T[1]], [b_T[2], b_T[3]]]
        bl_xsb = [[b_T[4]], [b_T[5]]]
        d_xst = [S.dma_sem("xst%d" % i) for i in range(2)]
        ssq = sb("ssq", [128, 16], F32)
        b_ssq = buf()

        def phase1a(_):
            for t in range(9):
                rows = 128 if t < 8 else NS * ST
                i = t % 2
                src = xp_h[t * 128:(t + 1) * 128, :] if t < 8 else xs_h[:, :]
                kb.dma("sync", xst[i][0:rows, :], src, [], bl_xst[i], d_xst[i])
                kb.act(xsb[i][0:rows, :], xst[i][0:rows, :], AF.Square, bl_xst[i], bl_xsb[i] + [b_ssq],
                       accum_out=ssq[0:rows, t:t + 1])
                kb.act(ssq[0:rows, t:t + 1], ssq[0:rows, t:t + 1], AF.Ln, [b_ssq], [b_ssq], bias=EPS, scale=1.0 / D)
                kb.act(ssq[0:rows, t:t + 1], ssq[0:rows, t:t + 1], AF.Exp, [b_ssq], [b_ssq], scale=-0.5)
                kb.ve("tensor_scalar_mul", (xsb[i][0:rows, :], xst[i][0:rows, :], ssq[0:rows, t:t + 1]),
                      bl_xst[i] + [b_ssq], bl_xsb[i])
                col0 = t * 128
                for q4 in range(4):
                    pt = PT[q4 % 2]
                    bpt = b_PT[q4 % 2]
                    for cc in range(4):
                        c = q4 * 4 + cc
                        kb.tr(pt[:, cc * 128:cc * 128 + rows], xsb[i][0:rows, c * 128:(c + 1) * 128],
                              ident_bf[0:rows, 0:rows], bl_xsb[i] + [b_const], [bpt])
                    gv = vecsT[:, V_NMA + q4 * 4:V_NMA + q4 * 4 + 4]
                    kb.ve("tensor_tensor", (xT[:, q4 * 4:q4 * 4 + 4, col0:col0 + rows],
                                            pt[:, 0:512].rearrange("p (c t) -> p c t", t=128)[:, :, 0:rows],
                                            gv.unsqueeze(2).to_broadcast([128, 4, rows]), ALU.mult),
                          [bpt, b_vecs], [b_xT])

        jobs.append((None, phase1a))

        def tmp(i):
            return carve(A_h_t, i * TSZ, [128, NT], F32)
        T_ZQ, T_O = 0, 0
        T_EQ, T_QS = 1, 1
        T_F, T_OMF, T_CUM = 2, 3, 4
        T_ZG, T_A1 = 5, 5
        T_EG, T_A4 = 6, 6
        T_E = 7
        T_LF, T_RS = 8, 8
        ob = 9 * TSZ
        qt_bf = carve(A_h_t, ob + 0 * 2080, [128, NT], BF16)
        kt_bf = carve(A_h_t, ob + 1 * 2080, [128, NT], BF16)
        qi_bf = carve(A_h_t, ob + 2 * 2080, [128, NT], BF16)
        ko_bf = carve(A_h_t, ob + 3 * 2080, [128, NT], BF16)
        vt_bf = carve(A_h_t, ob + 4 * 2080, [128, NT], BF16)
        sq_bf = carve(A_h_t, ob + 5 * 2080, [128, NT], BF16)
        gate_bf = carve(A_h_t, ob + 6 * 2080, [128, NT], BF16)
        ob += 7 * 2080
        b_qt, b_kt, b_qi, b_ko, b_vt, b_sq, b_gate = [buf() for _ in range(7)]
        vtok = carve(A_h_t, ob, [64, 16, 128], BF16)
        kotok = carve(A_h_t, ob + 4096, [64, 16, 128], BF16)
        vgtok = carve(A_h_t, ob, [128, 8, 128], BF16)
        kgtok = carve(A_h_t, ob + 4096, [128, 8, 128], BF16)
        attm = carve(A_h_t, ob + 8192, [64, 16, 64], BF16)
        Sbf = carve(A_h_t, ob + 10240, [128, 16, 128], BF16)
        assert ob + 10240 + 4096 <= KC * NT * 4
        b_vtok, b_kotok, b_toks = buf(), buf(), buf()
        b_vgtok, b_kgtok = b_vtok, b_kotok
        b_attm = buf()
        b_Sbf = buf()
        Sin = carve(A_m, 0, [128, H, 128], F32)
        s0_sb = carve(A_m, 8192, [128, NS, 128], F32)
        s0_bf = carve(A_m, 10240, [128, NS, 128], BF16)
        sfin = carve(A_m, 11264, [128, NS, 128], F32)
        sloc = carve(A_m, 13312, [128, 4, 128], F32)
        mo = 15360
        Dall = carve(A_m, mo, [128, 8, H], F32)
        dtot = carve(A_m, mo + 512, [128, H], F32)
        dec = carve(A_m, mo + 576, [128, 20], F32)
        vtok_s = carve(A_m, mo + 656, [16, 128], BF16)
        kotok_s = carve(A_m, mo + 912, [16, 128], BF16)
        vm_s = carve(A_m, mo + 1168, [16, NS, 128], BF16)
        attm_s = carve(A_m, mo + 2192, [16, 16], BF16)
        Sst = carve(A_m, mo + 2224, [128, 2, 128], F32)
        assert mo + 2224 + 1024 <= 19456
        b_dec = buf()
        b_Sst = [buf(), buf()]
        b_s0, b_s0bf, b_sfin = buf(), buf(), buf()
        d_s0 = S.dma_sem("s0")
        d_sfin = S.dma_sem("sfin")
        b_Sin = [buf() for _ in range(4)]
        b_sloc, b_dtot = buf(), buf()
        d_sloc = S.dma_sem("sloc")
        d_dtot = S.dma_sem("dtot")
        b_gs = [buf() for _ in range(4)]
        b_gd = buf()

        def proj(slot, wcol0, dst_parts, kparts=PARTS):
            wv = WR[slot][:, :].rearrange("p (k n) -> p k n", k=KC)
            for (c0, c1), (pap, pb) in zip(kparts, dst_parts):
                for k in range(KC):
                    kb.mm(pap, wv[:, k, wcol0:wcol0 + 128], xT[:, k, c0:c1], k == 0, k == KC - 1,
                          [b_WR[slot], b_xT], [pb])

        HALF = [(0, 512), (512, 1024)]

        def make_prepass(h):
            def load():
                return wload([
                    (lambda w: w[:, :].rearrange("p (k n) -> p k n", k=KC)[:, :, 0:128], w_in_v[:, :, D + h * 128:D + (h + 1) * 128]),
                    (lambda w: w[:, :].rearrange("p (k n) -> p k n", k=KC)[:, :, 128:256], w_in_v[:, :, 2 * D + h * 128:2 * D + (h + 1) * 128]),
                ])

            def run(slot):
                wv = WR[slot][:, :].rearrange("p (k n) -> p k n", k=KC)
                for pi, (c0, c1) in enumerate(HALF):
                    for k in range(KC):
                        kb.mm(PS[pi][:, :], wv[:, k, 0:128], xT[:, k, c0:c1], k == 0, k == KC - 1, [b_WR[slot], b_xT], [b_PS[pi]])
                for pi, (c0, c1) in enumerate(HALF):
                    for k in range(KC):
                        kb.mm(PS[2 + pi][:, :], wv[:, k, 128:256], xT[:, k, c0:c1], k == 0, k == KC - 1, [b_WR[slot], b_xT], [b_PS[2 + pi]])
                F_, LF_, OMF_, CUM_, E_ = tmp(T_F), tmp(T_LF), tmp(T_OMF), tmp(T_CUM), tmp(T_E)
                for pi, (c0, c1) in enumerate(HALF):
                    kb.act(F_[:, c0:c1], PS[pi][:, :], AF.Exp, [b_PS[pi]], [b_T[T_F]], scale=-1.0)
                    kb.act(vt_bf[:, c0:c1], PS[2 + pi][:, :], AF.Copy, [b_PS[2 + pi]], [b_vt])
                kb.ve("tensor_scalar_add", (F_[:, 0:LP], F_[:, 0:LP], 1.0), [b_T[T_F]], [b_T[T_F]])
                kb.ve("reciprocal", (F_[:, 0:LP], F_[:, 0:LP]), [b_T[T_F]], [b_T[T_F]])
                kb.ve("tensor_scalar", (F_[:, 0:LP], F_[:, 0:LP], lbt[:, 1, h:h + 1], lbt[:, 0, h:h + 1], ALU.mult, ALU.add),
                      [b_T[T_F], b_vecs], [b_T[T_F]])
                kb.act(LF_[:, 0:LP], F_[:, 0:LP], AF.Ln, [b_T[T_F]], [b_T[T_LF]])
                kb.ve("tensor_scalar", (OMF_[:, 0:LP], F_[:, 0:LP], -1.0, 1.0, ALU.mult, ALU.add), [b_T[T_F]], [b_T[T_OMF]])
                kb.ve("tensor_tensor_scan", (CUM_[:, 0:LP], onecol[:, 0:1].to_broadcast([128, LP]), LF_[:, 0:LP], 0.0, ALU.mult, ALU.add),
                      [b_T[T_LF], b_const], [b_T[T_CUM]])
                kb.act(E_[:, 0:LP], CUM_[:, 0:LP], AF.Exp, [b_T[T_CUM]], [b_T[T_E]], bias=CUM_[:, LP - 1:LP], scale=-1.0)
                kb.ve("tensor_tensor", (ko_bf[:, 0:LP], OMF_[:, 0:LP], E_[:, 0:LP], ALU.mult), [b_T[T_OMF], b_T[T_E]], [b_ko])
                kb.act(dtot[:, h:h + 1], CUM_[:, LP - 1:LP], AF.Exp, [b_T[T_CUM]], [b_dtot])
                for t in range(8):
                    kb.tr(PT[0][:, t * 128:(t + 1) * 128], ko_bf[:, t * 128:(t + 1) * 128], ident_bf[:], [b_ko, b_const], [b_PT[0]])
                kb.ve("tensor_copy", (kgtok[:, :, :], PT[0][:, :].rearrange("p (t k) -> p t k", k=128)), [b_PT[0]], [b_kgtok])
                for t in range(8):
                    kb.tr(PT[1][:, t * 128:(t + 1) * 128], vt_bf[:, t * 128:(t + 1) * 128], ident_bf[:], [b_vt, b_const], [b_PT[1]])
                kb.act(vgtok[:, :, :], PT[1][:, :].rearrange("p (t k) -> p t k", k=128), AF.Copy, [b_PT[1]], [b_vgtok])
                for t in range(8):
                    kb.mm(PS[4][:, 0:128], kgtok[:, t, :], vgtok[:, t, :], t == 0, t == 7, [b_kgtok, b_vgtok], [b_PS[4]])
                hh = h % 4
                kb.ve("tensor_copy", (sloc[:, hh, :], PS[4][:, 0:128]), [b_PS[4]], [b_sloc])
                if hh == 3:
                    gq = h // 4
                    kb.dma("sync", gs_in[gq].ap(), sloc[:, :, :].rearrange("p a b -> p (a b)"), [b_sloc], [b_gs[gq]], d_sloc)
                    kb.cc(GRP4, gs_in[gq].ap().opt(), gs_mid[gq].ap().opt(), [b_gs[gq]], [b_gs[gq]], d_cc)
                    kb.cc(PAIRS, gs_mid[gq].ap().opt(), gs_out[gq].ap().opt(), [b_gs[gq]], [b_gs[gq]], d_cc)
                if h == H - 1:
                    kb.dma("sync", gd_in.ap(), carve(A_m, mo + 512, [128, 512], F32), [b_dtot], [b_gd], d_dtot)
                    kb.cc(GRP4, gd_in.ap().opt(), gd_mid.ap().opt(), [b_gd], [b_gd], d_cc)
                    kb.cc(PAIRS, gd_mid.ap().opt(), gd_out.ap().opt(), [b_gd], [b_gd], d_cc)
            return load, run

        kb.ve("memset", (onecol[:], 1.0), [], [b_const], eng=g)
        for h in range(H):
            jobs.append(make_prepass(h))

        G_sb = carve(A_h_t, 0, [128, 8, 512], F32)
        Pch = [carve(A_h_t, 4 * TSZ, [128, 512], F32), carve(A_h_t, 5 * TSZ, [128, 512], F32)]
        b_Dall = buf()
        bl_G = [b_T[0], b_T[1], b_T[2], b_T[3]]
        b_Pch = [b_T[4], b_T[5]]
        d_G = S.dma_sem("G")
        d_Dall = S.dma_sem("Dall")
        d_sp = S.dma_sem("sp")

        def chain(_):
            kb.dma("sync", Dall[:, :, :], gd_out.ap().rearrange("(r p) h -> p r h", p=128)[:, :, 0:H], [b_gd], [b_Dall], d_Dall)
            for gq in range(4):
                kb.dma("sync", G_sb[:, :, :], gs_out[gq].ap().rearrange("(r p) n -> p r n", p=128), [b_gs[gq]], bl_G, d_G)
                Sg = Sin[:, gq * 4:(gq + 1) * 4, :].rearrange("p a b -> p (a b)")
                kb.ve("memset", (Sg, 0.0), [], [b_Sin[gq]])
                kb.ve("tensor_copy", (Pch[0][:, :], G_sb[:, 0, :]), bl_G, [b_Pch[0]])
                cur = 0
                for j in range(1, 8):
                    kb.ve("scalar_tensor_tensor", (Sg, Pch[cur][:, :], onehot[:, j:j + 1], Sg, ALU.mult, ALU.add),
                          [b_Pch[cur], b_oh, b_Sin[gq]], [b_Sin[gq]])
                    nx = 1 - cur
                    kb.ve("tensor_tensor", (Pch[nx][:, :].rearrange("p (a b) -> p a b", a=4),
                                            Pch[cur][:, :].rearrange("p (a b) -> p a b", a=4),
                                            Dall[:, j, gq * 4:(gq + 1) * 4].unsqueeze(2).to_broadcast([128, 4, 128]), ALU.mult),
                          [b_Pch[cur], b_Dall], [b_Pch[nx]])
                    kb.ve("tensor_tensor", (Pch[nx][:, :], Pch[nx][:, :], G_sb[:, j, :], ALU.add), [b_Pch[nx]] + bl_G, [b_Pch[nx]])
                    cur = nx
                kb.dma("sync", o_sp[gq * 4:(gq + 1) * 4, :, :].rearrange("h k v -> k h v"),
                       Pch[cur][:, :].rearrange("p (a b) -> p a b", a=4), [b_Pch[cur]], [], d_sp)
                out_bufs.append(b_Pch[cur])

        jobs.append((None, chain))

        def make_main(h):
            def load():
                rr = lambda j: (lambda w: w[:, :].rearrange("p (k n) -> p k n", k=KC)[:, :, j * 128:(j + 1) * 128])
                return wload([(rr(j), w_in_v[:, :, j * D + h * 128:j * D + (h + 1) * 128]) for j in range(4)])

            def run(slot):
                gq = h // 4
                kb.dma("sync", s0_sb[:, :, :], s0_h[:, h, :, :].rearrange("j k v -> k j v"), [], [b_s0], d_s0)
                kb.act(s0_bf[:, :, :], s0_sb[:, :, :], AF.Copy, [b_s0], [b_s0bf])
                ZQ, EQ, QS, F_, LF_, OMF_, CUM_, A1, A4, E_, O_, RS, ZG, EG = [tmp(i) for i in (T_ZQ, T_EQ, T_QS, T_F, T_LF, T_OMF, T_CUM, T_A1, T_A4, T_E, T_O, T_RS, T_ZG, T_EG)]
                pq = [(PS[i][:, 0:c1 - c0], b_PS[i]) for i, (c0, c1) in enumerate(PARTS)]
                pf = [(PS[3 + i][:, 0:c1 - c0], b_PS[3 + i]) for i, (c0, c1) in enumerate(PARTS)]
                proj(slot, 0, pq)
                for i, (c0, c1) in enumerate(PARTS):
                    kb.act(ZQ[:, c0:c1], pq[i][0], AF.Copy, [pq[i][1]], [b_T[T_ZQ]])
                proj(slot, 128, pf)
                for i, (c0, c1) in enumerate(PARTS):
                    kb.act(F_[:, c0:c1], pf[i][0], AF.Exp, [pf[i][1]], [b_T[T_F]], scale=-1.0)
                proj(slot, 256, pq)
                for i, (c0, c1) in enumerate(PARTS):
                    kb.act(vt_bf[:, c0:c1], pq[i][0], AF.Copy, [pq[i][1]], [b_vt])
                proj(slot, 384, pf)
                for i, (c0, c1) in enumerate(PARTS):
                    kb.act(ZG[:, c0:c1], pf[i][0], AF.Copy, [pf[i][1]], [b_T[T_ZG]])
                kb.act(EQ, ZQ, AF.Exp, [b_T[T_ZQ]], [b_T[T_EQ]], scale=-1.0)
                kb.ve("tensor_scalar_add", (EQ, EQ, 1.0), [b_T[T_EQ]], [b_T[T_EQ]])
                kb.ve("reciprocal", (EQ, EQ), [b_T[T_EQ]], [b_T[T_EQ]])
                kb.ve("tensor_tensor", (QS, ZQ, EQ, ALU.mult), [b_T[T_ZQ], b_T[T_EQ]], [b_T[T_QS]])
                kb.ve("tensor_scalar_add", (F_, F_, 1.0), [b_T[T_F]], [b_T[T_F]])
                kb.ve("reciprocal", (F_, F_), [b_T[T_F]], [b_T[T_F]])
                kb.ve("tensor_scalar", (F_, F_, lbt[:, 1, h:h + 1], lbt[:, 0, h:h + 1], ALU.mult, ALU.add),
                      [b_T[T_F], b_vecs], [b_T[T_F]])
                kb.act(LF_, F_, AF.Ln, [b_T[T_F]], [b_T[T_LF]])
                kb.ve("tensor_scalar", (OMF_, F_, -1.0, 1.0, ALU.mult, ALU.add), [b_T[T_F]], [b_T[T_OMF]])
                kb.act(EG, ZG, AF.Exp, [b_T[T_ZG]], [b_T[T_EG]], scale=-1.0)
                kb.ve("tensor_scalar_add", (EG, EG, 1.0), [b_T[T_EG]], [b_T[T_EG]])
                kb.ve("reciprocal", (EG, EG), [b_T[T_EG]], [b_T[T_EG]])
                kb.ve("tensor_tensor", (gate_bf[:, :], ZG, EG, ALU.mult), [b_T[T_ZG], b_T[T_EG]], [b_gate])
                kb.ve("tensor_tensor_scan", (CUM_, scanmask[:, :], LF_, 0.0, ALU.mult, ALU.add), [b_T[T_LF], b_scan], [b_T[T_CUM]])
                cp = CUM_[:, 0:LP].rearrange("p (c t) -> p c t", t=64)
                cs = CUM_[:, LP:NT].rearrange("p (c t) -> p c t", t=ST)
                for (dst, mid_p, mid_s) in ((A1, 31, 1), (A4, 63, 3)):
                    kb.ve("tensor_tensor", (dst[:, 0:LP].rearrange("p (c t) -> p c t", t=64), cp,
                                            cp[:, :, mid_p:mid_p + 1].to_broadcast([128, 16, 64]), ALU.subtract),
                          [b_T[T_CUM]], [b_T[T_A1 if dst is A1 else T_A4]])
                    kb.ve("tensor_tensor", (dst[:, LP:NT].rearrange("p (c t) -> p c t", t=ST), cs,
                                            cs[:, :, mid_s:mid_s + 1].to_broadcast([128, NS, ST]), ALU.subtract),
                          [b_T[T_CUM]], [b_T[T_A1 if dst is A1 else T_A4]])
                kb.act(dec[:, 0:16], cp[:, :, 63], AF.Exp, [b_T[T_CUM]], [b_dec])
                kb.act(dec[:, 16:20], cs[:, :, 3], AF.Exp, [b_T[T_CUM]], [b_dec])
                kb.act(E_, A1, AF.Exp, [b_T[T_A1]], [b_T[T_E]])
                kb.ve("tensor_tensor", (qt_bf[:, :], QS, E_, ALU.mult), [b_T[T_QS], b_T[T_E]], [b_qt])
                kb.act(E_, A1, AF.Exp, [b_T[T_A1], b_T[T_E]], [b_T[T_E]], scale=-1.0)
                kb.ve("tensor_tensor", (kt_bf[:, :], OMF_, E_, ALU.mult), [b_T[T_OMF], b_T[T_E]], [b_kt])
                kb.act(E_, CUM_, AF.Exp, [b_T[T_CUM], b_T[T_E]], [b_T[T_E]])
                kb.ve("tensor_tensor", (qi_bf[:, :], QS, E_, ALU.mult), [b_T[T_QS], b_T[T_E]], [b_qi])
                kb.act(E_, A4, AF.Exp, [b_T[T_A4], b_T[T_E]], [b_T[T_E]], scale=-1.0)
                kb.ve("tensor_tensor", (ko_bf[:, :], OMF_, E_, ALU.mult), [b_T[T_OMF], b_T[T_E]], [b_ko])
                for half in range(2):
                    for cc in range(8):
                        c = half * 8 + cc
                        kb.tr(PT[0][0:64, cc * 128:(cc + 1) * 128], vt_bf[:, c * 64:(c + 1) * 64], ident_bf[:], [b_vt, b_const], [b_PT[0]])
                    kb.ve("tensor_copy", (vtok[:, half * 8:half * 8 + 8, :], PT[0][0:64, :].rearrange("p (c k) -> p c k", k=128)),
                          [b_PT[0]], [b_vtok])
                    for cc in range(8):
                        c = half * 8 + cc
                        kb.tr(PT[1][0:64, cc * 128:(cc + 1) * 128], ko_bf[:, c * 64:(c + 1) * 64], ident_bf[:], [b_ko, b_const], [b_PT[1]])
                    kb.act(kotok[:, half * 8:half * 8 + 8, :], PT[1][0:64, :].rearrange("p (c k) -> p c k", k=128), AF.Copy,
                           [b_PT[1]], [b_kotok])
                kb.tr(PT[0][0:16, 0:128], vt_bf[:, LP:NT], ident_bf[:], [b_vt, b_const], [b_PT[0]])
                kb.tr(PT[0][0:16, 128:256], ko_bf[:, LP:NT], ident_bf[:], [b_ko, b_const], [b_PT[0]])
                kb.ve("tensor_copy", (vtok_s[:, :], PT[0][0:16, 0:128]), [b_PT[0]], [b_toks])
                kb.ve("tensor_copy", (kotok_s[:, :], PT[0][0:16, 128:256]), [b_PT[0]], [b_toks])
                for j in range(NS):
                    kb.ve("tensor_scalar_mul", (vm_s[:, j, :], vtok_s[:, :], rowsel[:, j:j + 1]), [b_toks, b_const], [b_toks])
                for half in range(2):
                    for cc in range(8):
                        c = half * 8 + cc
                        kb.mm(PS[half][0:64, cc * 64:(cc + 1) * 64], kt_bf[:, c * 64:(c + 1) * 64], qt_bf[:, c * 64:(c + 1) * 64],
                              True, True, [b_kt, b_qt], [b_PS[half]])
                    kb.ve("tensor_tensor", (attm[:, half * 8:half * 8 + 8, :], PS[half][0:64, :].rearrange("p (c t) -> p c t", t=64),
                                            triu[:, :, :], ALU.mult), [b_PS[half], b_const], [b_attm])
                kb.mm(PS[2][0:16, 0:16], kt_bf[:, LP:NT], qt_bf[:, LP:NT], True, True, [b_kt, b_qt], [b_PS[2]])
                kb.ve("tensor_tensor", (attm_s[:, :], PS[2][0:16, 0:16], smask[:, :], ALU.mult), [b_PS[2], b_const], [b_attm])
                for q4 in range(4):
                    for cc in range(4):
                        c = q4 * 4 + cc
                        kb.mm(PS[2 + (q4 % 2)][:, cc * 128:(cc + 1) * 128], kotok[:, c, :], vtok[:, c, :], True, True,
                              [b_kotok, b_vtok], [b_PS[2 + (q4 % 2)]])
                    for cc in range(4):
                        c = q4 * 4 + cc
                        if c == 0:
                            kb.ve("tensor_copy", (Sbf[:, 0, :], Sin[:, h, :]), [b_Sin[gq]], [b_Sbf])
                            prev = Sin[:, h, :]
                            prev_b = b_Sin[gq]
                        else:
                            prev = Sst[:, (c - 1) % 2, :]
                            prev_b = b_Sst[(c - 1) % 2]
                        kb.ve("scalar_tensor_tensor", (Sst[:, c % 2, :], prev, dec[:, c:c + 1], PS[2 + (q4 % 2)][:, cc * 128:(cc + 1) * 128],
                                                       ALU.mult, ALU.add), [prev_b, b_dec, b_PS[2 + (q4 % 2)]], [b_Sst[c % 2]])
                        if c < 15:
                            kb.act(Sbf[:, c + 1, :], Sst[:, c % 2, :], AF.Copy, [b_Sst[c % 2]], [b_Sbf])
                for half in range(2):
                    for cc in range(8):
                        c = half * 8 + cc
                        dst = PS[4 + half][:, cc * 64:(cc + 1) * 64]
                        kb.mm(dst, vtok[:, c, :], attm[:, c, :], True, False, [b_vtok, b_attm], [b_PS[4 + half]])
                        kb.mm(dst, Sbf[:, c, :], qi_bf[:, c * 64:(c + 1) * 64], False, True, [b_Sbf, b_qi], [b_PS[4 + half]])
                    kb.act(O_[:, half * 512:(half + 1) * 512], PS[4 + half][:, :], AF.Copy, [b_PS[4 + half]], [b_T[T_O]])
                kb.mm(PS[0][:, 0:16], vtok_s[:, :], attm_s[:, :], True, False, [b_toks, b_attm], [b_PS[0]])
                for j in range(NS):
                    kb.mm(PS[0][:, 4 * j:4 * j + 4], s0_bf[:, j, :], qi_bf[:, LP + 4 * j:LP + 4 * j + 4], False, j == NS - 1,
                          [b_s0bf, b_qi], [b_PS[0]])
                kb.act(O_[:, LP:NT], PS[0][:, 0:16], AF.Copy, [b_PS[0]], [b_T[T_O]])
                for j in range(NS):
                    kb.mm(PS[1][:, j * 128:(j + 1) * 128], kotok_s[:, :], vm_s[:, j, :], True, True, [b_toks], [b_PS[1]])
                for j in range(NS):
                    kb.ve("scalar_tensor_tensor", (sfin[:, j, :], s0_sb[:, j, :], dec[:, 16 + j:17 + j], PS[1][:, j * 128:(j + 1) * 128],
                                                   ALU.mult, ALU.add), [b_s0, b_dec, b_PS[1]], [b_sfin])
                kb.dma("sync", o_ss[:, h, :, :].rearrange("j k v -> k j v"), sfin[:, :, :], [b_sfin], [], d_sfin)
                kb.act(sq_bf[:, :], O_, AF.Square, [b_T[T_O]], [b_sq])
                for i, (c0, c1) in enumerate(PARTS):
                    kb.mm(PS[1 + i][:, 0:c1 - c0], ones_bf[:, :], sq_bf[:, c0:c1], True, True, [b_const, b_sq], [b_PS[1 + i]])
                    kb.act(RS[:, c0:c1], PS[1 + i][:, 0:c1 - c0], AF.Ln, [b_PS[1 + i]], [b_T[T_RS]], bias=EPS, scale=1.0 / 128)
                kb.act(RS, RS, AF.Exp, [b_T[T_RS]], [b_T[T_RS]], scale=-0.5)
                kb.ve("scalar_tensor_tensor", (O_, O_, vecsT[:, V_ON:V_ON + 1], RS, ALU.mult, ALU.mult),
                      [b_T[T_O], b_vecs, b_T[T_RS]], [b_T[T_O]])
                kb.ve("tensor_tensor", (oT[:, h, :], O_, gate_bf[:, :], ALU.mult), [b_T[T_O], b_gate], [b_oT[h]])
            return load, run

        if upto >= 2:
            for h in range(H):
                jobs.append(make_main(h))

        PARTS_R = [(0, 512), (512, 1024), (1024, NT)]
        ALL_AH = b_T + [b_qt, b_kt, b_qi, b_ko, b_vt, b_sq, b_gate, b_vtok, b_kotok, b_attm, b_Sbf]
        ALL_AM = b_Sin + [b_s0, b_s0bf, b_sfin, b_sloc, b_dtot, b_dec, b_toks, b_attm, b_Dall] + b_Sst
        xblk = [carve(A_m, i * 4608, [128, 8, 128], F32) for i in range(2)]
        xblk_s = [carve(A_m, i * 4608 + 4096, [16, 128], F32) for i in range(2)]
        b_xblk = [buf(), buf()]
        d_xblk = [S.dma_sem("xblk0"), S.dma_sem("xblk1")]
        sg = carve(A_m, 9216, [128, NT], F32)
        b_sg = buf()
        rstd = scanmask
        b_rstd = b_scan
        xp_blk = xp_h.ap().rearrange("(t p) d -> p t d", p=128)
        fence = {"ah": True, "am": True}

        def psparts(sel, parts):
            return [(PS[3 * sel + i][:, 0:c1 - c0], b_PS[3 * sel + i]) for i, (c0, c1) in enumerate(parts)]

        def make_wout(gq):
            def load():
                return wload([(lambda w: w[:, :].rearrange("p (k n) -> p k n", k=KC), w_out_v[:, :, gq * 512:(gq + 1) * 512])])

            def run(slot):
                wv = WR[slot][:, :].rearrange("p (k n) -> p k n", k=KC)
                for oo in range(4):
                    o = gq * 4 + oo
                    i = o % 2
                    extra = ALL_AM if fence["am"] else []
                    kb.dma("sync", xblk[i][:, :, :], xp_blk[:, :, o * 128:(o + 1) * 128], [], [b_xblk[i]] + (extra if o < 2 else []), d_xblk[i])
                    kb.dma("sync", xblk_s[i][:, :], xs_h[:, o * 128:(o + 1) * 128], [], [b_xblk[i]], d_xblk[i])
                    pp = psparts(o % 2, PARTS_R)
                    for pi, (c0, c1) in enumerate(PARTS_R):
                        pap, pb = pp[pi]
                        for k in range(KC):
                            kb.mm(pap, wv[:, k, oo * 128:(oo + 1) * 128], oT[:, k, c0:c1], k == 0, False, [b_WR[slot]] + b_oT, [pb])
                        if pi < 2:
                            for tt in range(4):
                                t = pi * 4 + tt
                                kb.mm(pap[:, tt * 128:(tt + 1) * 128], xblk[i][:, t, :], ident_f[:, :], False, tt == 3,
                                      [b_xblk[i], b_const], [pb])
                        else:
                            kb.mm(pap, xblk_s[i][:, :], ident_f[0:16, 0:16], False, True, [b_xblk[i], b_const], [pb])
                        kb.act(A_h[:, o, c0:c1], pap, AF.Copy, [pb], [b_hT[o]] + ALL_AH)
            return load, run

        def norm_to_xT(gcol):
            for c in range(KC):
                kb.act(xT[:, c, :], A_h[:, c, :], AF.Square, [b_hT[c]], [b_xT])
            for pi, (c0, c1) in enumerate(PARTS):
                for c in range(KC):
                    kb.mm(PS[pi][:, 0:c1 - c0], ones_bf[:, :], xT[:, c, c0:c1], c == 0, c == KC - 1, [b_const, b_xT], [b_PS[pi]])
                kb.act(rstd[:, c0:c1], PS[pi][:, 0:c1 - c0], AF.Ln, [b_PS[pi]], [b_rstd], bias=EPS, scale=1.0 / D)
            kb.act(rstd[:, :], rstd[:, :], AF.Exp, [b_rstd], [b_rstd], scale=-0.5)
            for c in range(KC):
                kb.ve("scalar_tensor_tensor", (xT[:, c, :], A_h[:, c, :], vecsT[:, gcol + c:gcol + c + 1], rstd[:, :], ALU.mult, ALU.mult),
                      [b_hT[c], b_vecs, b_rstd], [b_xT])

        def add_ffn(l):
            w_gu_v = w_gu_h[l].rearrange("(k p) n -> p k n", p=128)
            w_dn_v = w_dn_h[l].rearrange("(k p) n -> p k n", p=128)
            def pre(_, l=l):
                if l == 1:
                    S.op("vector", lambda e: e.memset(fence_t[:, 0:1], 0.0), [], [b_sg, b_Vloc, b_Kloc, b_pm])
                norm_to_xT(V_NFFN + 16 * l)
            jobs.append((None, pre))
            for gi in range(4):
                tiles = list(range(gi * 11, gi * 11 + 11))
                pairs = [tiles[i:i + 2] for i in range(0, 11, 2)]
                for pr in pairs:
                    def load(pr=pr):
                        pcs = []
                        for n_, j in enumerate(pr):
                            pcs.append((lambda w, n_=n_: w[:, :].rearrange("p (k n) -> p k n", k=KC)[:, :, n_ * 256:n_ * 256 + 128],
                                        w_gu_v[:, :, j * 128:(j + 1) * 128]))
                            pcs.append((lambda w, n_=n_: w[:, :].rearrange("p (k n) -> p k n", k=KC)[:, :, n_ * 256 + 128:n_ * 256 + 256],
                                        w_gu_v[:, :, DFF + j * 128:DFF + (j + 1) * 128]))
                        return wload(pcs)

                    def run(slot, pr=pr, gi=gi):
                        for n_, j in enumerate(pr):
                            jj = j - gi * 11
                            pg = psparts(0, PARTS)
                            pu = psparts(1, PARTS)
                            proj(slot, n_ * 256, pg)
                            for i, (c0, c1) in enumerate(PARTS):
                                kb.act(sg[:, c0:c1], pg[i][0], AF.Silu, [pg[i][1]], [b_sg])
                            proj(slot, n_ * 256 + 128, pu)
                            for i, (c0, c1) in enumerate(PARTS):
                                kb.ve("tensor_tensor", (oT[:, jj, c0:c1], sg[:, c0:c1], pu[i][0], ALU.mult), [b_sg, pu[i][1]], [b_oT[jj]])
                    jobs.append((load, run))
                for oq in range(4):
                    def load(oq=oq, gi=gi):
                        return wload([(lambda w: w[:, 0:11 * 512].rearrange("p (k n) -> p k n", k=11),
                                       w_dn_v[:, gi * 11:gi * 11 + 11, oq * 512:(oq + 1) * 512])])

                    def run(slot, oq=oq, gi=gi):
                        wv = WR[slot][:, 0:11 * 512].rearrange("p (k n) -> p k n", k=11)
                        for oo in range(4):
                            o = oq * 4 + oo
                            pp = psparts(o % 2, PARTS)
                            for pi, (c0, c1) in enumerate(PARTS):
                                pap, pb = pp[pi]
                                for kk in range(11):
                                    kb.mm(pap, wv[:, kk, oo * 128:(oo + 1) * 128], oT[:, kk, c0:c1], kk == 0, kk == 10,
                                          [b_WR[slot], b_oT[kk]], [pb])
                                kb.ve("tensor_tensor", (A_h[:, o, c0:c1], A_h[:, o, c0:c1], pap, ALU.add), [b_hT[o], pb], [b_hT[o]])
                    jobs.append((load, run))

        kT = carve(oT_t, 0, [128, 4, NT], F32)
        vT = carve(oT_t, 4 * NT * 4, [128, 4, NT], F32)
        Kloc = carve(A_m, 0, [128, 4, NT], BF16)
        Vloc = carve(A_m, 8320, [128, 9, 512], BF16)
        Kloc32 = carve(A_m, 0, [128, 4, NT // 2], F32)
        Vloc32 = carve(A_m, 8320, [128, 9, 256], F32)
        b_Kloc, b_Vloc = buf(), buf()
        kstage = [sb("kstage%d" % i, [128, 512], F32) for i in range(2)] + [sb("kstage2", [16, 512], F32)]
        b_kstage = [buf(), buf(), buf()]
        d_kst = [S.dma_sem("kst0"), S.dma_sem("kst1"), S.dma_sem("kst2")]
        kmean_own = sb("kmean_own", [128, 4, 4], F32)
        b_kmo = buf()
        d_ckx = [S.dma_sem("ckx%d" % i) for i in range(4)]
        d_cvx = [S.dma_sem("cvx%d" % i) for i in range(4)]
        d_cmx = S.dma_sem("cmx")
        b_ck = [buf() for _ in range(4)]
        b_cv = [buf() for _ in range(4)]
        b_cm = buf()
        w_kv_v = w_kv_h.ap().rearrange("(k p) n -> p k n", p=128)

        def make_kv(which):
            def load():
                return wload([(lambda w: w[:, :].rearrange("p (k n) -> p k n", k=KC), w_kv_v[:, :, which * 512:(which + 1) * 512])])

            def run(slot):
                dstT = kT if which == 0 else vT
                for kvh in range(4):
                    pp = psparts(kvh % 2, PARTS)
                    proj(slot, kvh * 128, pp)
                    for i, (c0, c1) in enumerate(PARTS):
                        kb.act(dstT[:, kvh, c0:c1], pp[i][0], AF.Copy, [pp[i][1]], [b_oT[kvh + 4 * which]])
                if which == 0:
                    for kvh in range(4):
                        bk = b_oT[kvh]
                        kb.act(Kloc[:, kvh, :], kT[:, kvh, :], AF.Square, [bk], [b_Kloc] + ((b_xblk + [b_sg]) if kvh == 0 else []))
                        for pi, (c0, c1) in enumerate(PARTS):
                            kb.mm(PS[pi][:, 0:c1 - c0], ones_bf[:, :], Kloc[:, kvh, c0:c1], True, True, [b_const, b_Kloc], [b_PS[pi]])
                            kb.act(rstd[:, c0:c1], PS[pi][:, 0:c1 - c0], AF.Ln, [b_PS[pi]], [b_rstd], bias=EPS, scale=1.0 / 128)
                        kb.act(rstd[:, :], rstd[:, :], AF.Exp, [b_rstd], [b_rstd], scale=-0.5)
                        kb.ve("scalar_tensor_tensor", (kT[:, kvh, :], kT[:, kvh, :], vecsT[:, V_KN:V_KN + 1], rstd[:, :], ALU.mult, ALU.mult),
                              [bk, b_vecs, b_rstd], [bk])
                        kb.act(Kloc[:, kvh, :], kT[:, kvh, :], AF.Copy, [bk], [b_Kloc])
                        kb.ve("tensor_reduce", (kmean_own[:, kvh, :], kT[:, kvh, 0:LP].rearrange("p (b t) -> p b t", t=256), AX.X, ALU.add),
                              [bk], [b_kmo])
                        kb.dma("sync", ck_in[kvh].ap(), Kloc32[:, kvh, 0:LP // 2], [b_Kloc], [b_ck[kvh]], d_ckx[kvh])
                        if not cfg.get("nocc_k"):
                            kb.cc(GRP4, ck_in[kvh].ap().opt(), ck_mid[kvh].ap().opt(), [b_ck[kvh]], [b_ck[kvh]], d_cc)
                            kb.cc(PAIRS, ck_mid[kvh].ap().opt(), ck_out[kvh].ap().opt(), [b_ck[kvh]], [b_ck[kvh]], d_cc)
                    kb.ve("tensor_scalar_mul", (rstd[:, 0:16], kmean_own[:, :, :].rearrange("p a b -> p (a b)"), 1.0 / 256), [b_kmo], [b_rstd])
                    kb.dma("sync", cm_in.ap(), rstd[:, 0:512], [b_rstd], [b_cm], d_cmx)
                    if not cfg.get("nocc_m"):
                        kb.cc(GRP4, cm_in.ap().opt(), cm_mid.ap().opt(), [b_cm], [b_cm], d_cc)
                        kb.cc(PAIRS, cm_mid.ap().opt(), cm_out.ap().opt(), [b_cm], [b_cm], d_cc)
                o_p, o_s = (o_kp, o_ks) if which == 0 else (o_vp, o_vs)
                for t in range(0 if cfg.get("kv_noout") else 9):
                    rows = 128 if t < 8 else NS * ST
                    i = t % 2 if t < 8 else 2
                    for kvh in range(4):
                        kb.tr(PS[i][0:rows, kvh * 128:(kvh + 1) * 128], dstT[:, kvh, t * 128:t * 128 + rows], ident_f[:, :],
                              [b_oT[kvh + 4 * which], b_const], [b_PS[i]])
                    kb.act(kstage[i][0:rows, :], PS[i][0:rows, :], AF.Copy, [b_PS[i]], [b_kstage[i]])
                    if which == 1:
                        kb.ve("tensor_copy", (Vloc[0:rows, t, :], kstage[i][0:rows, :]), [b_kstage[i]], [b_Vloc] + ((b_xblk + [b_sg]) if t == 0 else []))
                    dst = o_p[t * 128:(t + 1) * 128, :] if t < 8 else o_s[:, :]
                    kb.dma("sync", dst, kstage[i][0:rows, :], [b_kstage[i]], [], d_kst[i])
                if which == 1 and not cfg.get("kv_noout"):
                    for kvh in range(4):
                        kb.dma("sync", cv_in[kvh].ap().rearrange("p (t d) -> p t d", d=64), Vloc32[:, 0:8, kvh * 64:(kvh + 1) * 64],
                               [b_Vloc], [b_cv[kvh]], d_cvx[kvh])
                        if not cfg.get("nocc_v"):
                            kb.cc(GRP4, cv_in[kvh].ap().opt(), cv_mid[kvh].ap().opt(), [b_cv[kvh]], [b_cv[kvh]], d_cc)
                            kb.cc(PAIRS, cv_mid[kvh].ap().opt(), cv_out[kvh].ap().opt(), [b_cv[kvh]], [b_cv[kvh]], d_cc)
                    out_bufs.extend(b_kstage)
            return load, run

        if upto >= 3:
            for gq in range(4):
                jobs.append(make_wout(gq))
        if upto >= 4:
            add_ffn(0)
        if upto >= 5:
            jobs.append((None, lambda _: norm_to_xT(V_NKV)))
            jobs.append(make_kv(0))
            jobs.append(make_kv(1))

        SCALE = 128.0 ** -0.5
        NEG = -30000.0
        QF = sb("QF", [128, NT], F32)
        biasS = sb("biasS", [128, 8, H, 32], BF16)
        b_QF, b_biasS = buf(), buf()
        GM = carve(kstage[0], 0, [128, 8, 32], F32)
        SEL = carve(kstage[0], 1024, [128, 8, 32], F32)
        kmean_all = carve(kstage[1], 0, [128, 4, 32], F32)
        max8 = carve(kstage[1], 512, [128, 8, 8], F32)
        thr = carve(kstage[1], 768, [128, 8], F32)
        kml = carve(kstage[1], 1024, [128, 8, 16], F32)
        b_gm, b_km = b_kstage[0], b_kstage[1]
        d_att = S.dma_sem("att")
        d_pm = S.dma_sem("pm")
        w_q_v = w_q_h.ap().rearrange("(k p) n -> p k n", p=128)
        w_o_v = w_o_h.ap().rearrange("(k p) n -> p k n", p=128)

        def load_kmean(_):
            kb.dma("sync", kml[:, :, :], cm_out.ap().rearrange("(r p) n -> p r n", p=128)[:, :, 0:16], [b_cm], [b_km], d_att)
            for kvh in range(4):
                kb.ve("tensor_copy", (kmean_all[:, kvh, :].rearrange("p (r b) -> p r b", b=4), kml[:, :, kvh * 4:(kvh + 1) * 4]),
                      [b_km], [b_km])
            kb.dma("sync", pm_sb[:, :, :], pastmask_h.ap().rearrange("p (t n) -> p t n", n=32), [], [b_pm] + ALL_AM + b_xblk + [b_sg], d_pm)

        pm_sb = carve(A_m, 17536, [128, 8, 32], F32)
        b_pm = buf()

        def gate_tiles(hd, src, rows, ntile, gm, sel, mx, th, pm, ps_ap, ps_b):
            kvh = hd // 4
            for t in range(ntile):
                kb.mm(ps_ap[0:rows, t * 32:(t + 1) * 32], src(t), kmean_all[:, kvh, :], True, True, [b_QF, b_km], [ps_b])
            g3 = ps_ap[0:rows, 0:ntile * 32].rearrange("p (t n) -> p t n", n=32)
            if pm is not None:
                kb.ve("tensor_tensor", (gm, g3, pm, ALU.add), [ps_b, b_pm], [b_gm])
            else:
                kb.ve("tensor_copy", (gm, g3), [ps_b], [b_gm])
            for t in range(ntile):
                kb.ve("max", (mx[:, t, :], gm[:, t, :]), [b_gm], [b_km])
            kb.ve("tensor_scalar_max", (th, mx[:, :, 2], -1e29), [b_km], [b_km])
            kb.ve("tensor_tensor", (sel, gm, th.unsqueeze(2).to_broadcast([rows, ntile, 32]), ALU.is_ge), [b_gm, b_km], [b_gm])
            return sel

        def make_q(gq):
            def load():
                return wload([(lambda w: w[:, :].rearrange("p (k n) -> p k n", k=KC), w_q_v[:, :, gq * 512:(gq + 1) * 512])])

            def run(slot):
                for hh in range(4):
                    hd = gq * 4 + hh
                    pp = psparts(hh % 2, PARTS)
                    proj(slot, hh * 128, pp)
                    for i, (c0, c1) in enumerate(PARTS):
                        kb.act(QF[:, c0:c1], pp[i][0], AF.Copy, [pp[i][1]], [b_QF])
                    fence_w = b_oT if hd == 0 else [b_oT[hd]]
                    kb.act(oT[:, hd, :], QF[:, :], AF.Square, [b_QF], fence_w)
                    for pi, (c0, c1) in enumerate(PARTS):
                        kb.mm(PS[pi][:, 0:c1 - c0], ones_bf[:, :], oT[:, hd, c0:c1], True, True, [b_const, b_oT[hd]], [b_PS[pi]])
                        kb.act(rstd[:, c0:c1], PS[pi][:, 0:c1 - c0], AF.Ln, [b_PS[pi]], [b_rstd], bias=EPS, scale=1.0 / 128)
                    kb.act(rstd[:, :], rstd[:, :], AF.Exp, [b_rstd], [b_rstd], scale=-0.5)
                    kb.ve("scalar_tensor_tensor", (QF[:, :], QF[:, :], vecsT[:, V_QN:V_QN + 1], rstd[:, :], ALU.mult, ALU.mult),
                          [b_QF, b_vecs, b_rstd], [b_QF])
                    kb.act(oT[:, hd, :], QF[:, :], AF.Copy, [b_QF], [b_oT[hd]])
                    sel = gate_tiles(hd, lambda t: QF[:, t * 128:(t + 1) * 128], 128, 8, GM, SEL, max8, thr, pm_sb[:, :, :], PS[4][:, 0:256], b_PS[4])
                    kb.ve("tensor_scalar", (biasS[:, 0:8, hd, :], sel, -1.0, -NEG, ALU.add, ALU.mult), [b_gm], [b_biasS])
                    kb.ve("tensor_copy", (qs_f32[:, :, hd, :], QF[:, LP:NT].rearrange("p (j t) -> p j t", t=ST)), [b_QF], [b_qs])
            return load, run

        NPT = 2
        Pt = [carve(WR_t[1], i * 1024, [128, 512], BF16) for i in range(NPT)]
        biasT = [carve(WR_t[1], 2048 + i * 1024, [32, 512], BF16) for i in range(2)]
        causalT = carve(WR_t[1], 4096, [128, 4, 128], BF16)
        rden = carve(WR_t[1], 5120, [128, 512], F32)
        cm_s = carve(WR_t[1], 7168, [16, NS, 16], BF16)
        Esel = carve(WR_t[1], 7680, [32, 32, 128], BF16)
        b_Pt = [buf() for _ in range(NPT)]
        b_biasT = [buf(), buf()]
        b_causal, b_rden = buf(), buf()
        KA = carve(xT_t, 0, [128, 8, LP], BF16)
        VA = carve(xT_t, 16384, [128, 8, 8, 128], BF16) if False else None
        VA32 = carve(xT_t, 16384, [128, 8, 512], F32)
        VAb = xT_t[:, 4096:8192].bitcast(BF16).rearrange("p (r t d) -> p r t d", r=8, t=8)
        KA32 = carve(xT_t, 0, [128, 8, LP // 2], F32)
        b_KA, b_VA = buf(), buf()
        d_KA = S.dma_sem("KA")

        ring_bufs = []

        def attention(_):
            if cfg.get("serial_att", False):
                S.serial_buf = Buf("serial")
            W0, W1 = b_WR[0], b_WR[1]
            fme = S.op("gpsimd", lambda e: e.memset(fence_t[:, 0:1], 0.0), [], [W0, W1])
            for b_ in b_Pt + b_biasT + [b_causal, b_rden]:
                b_.w = fme
                ring_bufs.append(b_)
            attention.fme = fme

            def fw():
                return []
            kb.ve("memset", (causalT, 0.0), [], [b_causal] + fw(), eng="gpsimd")
            kb.ve("affine_select", (causalT, causalT, [[0, 4], [1, 128]], ALU.is_ge, NEG), [b_causal], [b_causal], eng="gpsimd",
                  base=0, channel_multiplier=-1)
            kb.ve("memset", (Esel, 1.0), [], [b_causal], eng="gpsimd")
            kb.ve("affine_select", (Esel, Esel, [[-1, 32], [0, 128]], ALU.is_equal, 0.0), [b_causal], [b_causal], eng="gpsimd",
                  base=0, channel_multiplier=1)
            kb.ve("memset", (cm_s, 0.0), [], [b_causal], eng="gpsimd")
            for j in range(NS):
                kb.ve("affine_select", (cm_s[:, j, :].rearrange("p (h t) -> p h t", t=4), cm_s[:, j, :].rearrange("p (h t) -> p h t", t=4),
                                        [[0, 4], [1, 4]], ALU.is_ge, NEG), [b_causal], [b_causal], eng="gpsimd", base=4 * j, channel_multiplier=-1)
                kb.ve("affine_select", (cm_s[:, j, :], cm_s[:, j, :], [[0, 16]], ALU.is_ge, NEG), [b_causal], [b_causal], eng="gpsimd",
                      base=-4 * j, channel_multiplier=1)
            pcount = {"n": 0, "sc": 0}

            def key_tile(qrhs, ncol, kT_ap, v_ap, bias_mm, acc, den, accb, denb, first_t, last_t, kr, vr, kparts=128, scbank=None):
                si = pcount["sc"] % 2 if scbank is None else scbank
                pcount["sc"] += 1
                sc, scb = PS[si][0:kparts, 0:ncol], b_PS[si]
                kb.mm(sc, kT_ap, qrhs, True, bias_mm is None, kr + b_oT, [scb])
                if bias_mm is not None:
                    l_, r_, rb = bias_mm
                    kb.mm(sc, l_, r_, False, True, rb, [scb])
                pi = pcount["n"] % NPT
                pcount["n"] += 1
                kb.act(Pt[pi][0:kparts, 0:ncol], sc, AF.Exp, [scb], [b_Pt[pi]], scale=SCALE)
                kb.mm(acc, v_ap, Pt[pi][0:kparts, 0:ncol], first_t, last_t, vr + [b_Pt[pi]], [accb])
                kb.mm(den, ones_bf[0:kparts, :], Pt[pi][0:kparts, 0:ncol], first_t, last_t, [b_const, b_Pt[pi]], [denb])

            for kvh in range(4):
                kb.dma("sync", KA32[:, :, :], ck_out[kvh].ap().rearrange("(r p) n -> p r n", p=128), [b_ck[kvh]], [b_KA, b_xT], d_KA)
                kb.dma("sync", VA32[:, :, :], cv_out[kvh].ap().rearrange("(r p) n -> p r n", p=128), [b_cv[kvh]], [b_VA, b_xT], d_KA)
                for i in range(8):
                    qrhs = oT[:, 4 * kvh:4 * kvh + 4, i * 128:(i + 1) * 128]
                    bt = biasT[i % 2]
                    for h4 in range(4):
                        kb.tr(PT[0][0:32, h4 * 128:(h4 + 1) * 128], biasS[:, i, 4 * kvh + h4, :], ident_bf[:, :], [b_biasS, b_const], [b_PT[0]])
                    kb.ve("tensor_copy", (bt, PT[0][0:32, 0:512]), [b_PT[0]], [b_biasT[i % 2]])
                    nb = 28 + i // 2
                    acc, den = PS[2][:, :], PS[3][:, :]
                    for kt in range(2 * nb):
                        n = kt // 2
                        r, lc = kt // 8, (kt % 8) * 128
                        key_tile(qrhs, 512, KA[:, r, lc:lc + 128], VAb[:, r, kt % 8, :],
                                 (Esel[:, n, :], bt, [b_causal, b_biasT[i % 2]]),
                                 acc, den, b_PS[2], b_PS[3], kt == 0, False, [b_KA], [b_VA])
                    if i % 2 == 1:
                        key_tile(qrhs, 512, Kloc[:, kvh, (i - 1) * 128:i * 128], Vloc[:, i - 1, kvh * 128:(kvh + 1) * 128], None,
                                 acc, den, b_PS[2], b_PS[3], False, False, [b_Kloc], [b_Vloc])
                    key_tile(qrhs, 512, Kloc[:, kvh, i * 128:(i + 1) * 128], Vloc[:, i, kvh * 128:(kvh + 1) * 128],
                             (ident_bf[:, :], causalT.rearrange("p h t -> p (h t)"), [b_const, b_causal]),
                             acc, den, b_PS[2], b_PS[3], False, True, [b_Kloc], [b_Vloc])
                    kb.ve("reciprocal", (rden[:, :], den), [b_PS[3]], [b_rden])
                    kb.ve("tensor_tensor", (qrhs, acc.rearrange("p (h t) -> p h t", t=128), rden[:, :].rearrange("p (h t) -> p h t", t=128),
                                            ALU.mult), [b_PS[2], b_rden], [b_oT[4 * kvh + x] for x in range(4)])
            if with_cache:
                sample_attention(key_tile, fme)
            S.op("gpsimd", lambda e: e.memset(fence_t[:, 0:1], 0.0), [], [W0, W1] + ring_bufs)
            S.serial_buf = None

        qs_f32 = sb("qs_f32", [128, NS, H, ST], F32)
        b_qs = buf()

        def sample_attention(key_tile, fme):
            kpg = [carve(WR_t[0], i * 2048, [128, 512], F32) for i in range(2)]
            vpg = [carve(WR_t[0], 4096 + i * 2048, [128, 512], F32) for i in range(2)]
            kpb = [carve(WR_t[0], 8192 + i * 1024, [128, 512], BF16) for i in range(2)]
            vpb = [carve(WR_t[0], 10240 + i * 1024, [128, 512], BF16) for i in range(2)]
            KTs = [carve(WR_t[0], 12288 + i * 1024, [128, 4, 128], BF16) for i in range(2)]
            idx = carve(WR_t[0], 14336, [128, 64], I32)
            ptb = carve(WR_t[0], 14592, [128, 64], I32)
            iota_p = carve(WR_t[0], 14848, [128, 1], I32)
            km_s = carve(WR_t[0], 14852, [128, 4, 32], F32)
            bT_s = carve(WR_t[0], 15364, [32, 64], BF16)
            gm_s = carve(WR_t[0], 15492, [16, 4, 32], F32)
            sel_s = carve(WR_t[0], 16004, [16, 4, 32], BF16)
            mx_s = carve(WR_t[0], 16260, [16, 8], F32)
            th_s = carve(WR_t[0], 16292, [16, 1], F32)
            b_kpg, b_vpg, b_kpb, b_vpb, b_KTs = [[buf(), buf()] for _ in range(5)]
            b_idx, b_kms, b_bTs, b_gs_ = buf(), buf(), buf(), buf()
            for b_ in b_kpg + b_vpg + b_kpb + b_vpb + b_KTs + [b_idx, b_kms, b_bTs, b_gs_]:
                b_.w = fme
                ring_bufs.append(b_)
            d_kg = [S.dma_sem("kg0"), S.dma_sem("kg1")]
            d_vg = [S.dma_sem("vg0"), S.dma_sem("vg1")]
            d_pt = S.dma_sem("pt")
            kb.ve("iota", (iota_p, [[0, 1]]), [], [b_idx], eng="gpsimd", base=0, channel_multiplier=1)

            def gather(dst, bdst, dsem, src_h, pg):
                S.op("gpsimd", lambda e: e.indirect_dma_start(out=dst, out_offset=None, in_=src_h.ap(),
                                                              in_offset=bass.IndirectOffsetOnAxis(ap=idx[:, pg:pg + 1], axis=0)),
                     [b_idx], [bdst], dsem=dsem)

            for j in range(NS):
                kb.dma("sync", ptb[:, :], pt_h[j:j + 1, :].partition_broadcast(128), [], [b_idx], d_pt)
                kb.ve("tensor_scalar", (idx[:, :], ptb[:, :], 128, iota_p[:, 0:1], ALU.mult, ALU.add), [b_idx], [b_idx])
                for pg in range(64):
                    i = pg % 2
                    gather(kpg[i][:, :], b_kpg[i], d_kg[i], ck_h, pg)
                    kb.act(kpb[i][:, :], kpg[i][:, :], AF.Copy, [b_kpg[i]], [b_kpb[i]])
                    for kvh in range(4):
                        col = kvh * 32 + pg // 2
                        kb.mm(PS[4][:, col:col + 1], kpb[i][:, kvh * 128:(kvh + 1) * 128], ones_bf[:, 0:1],
                              pg == 0 and kvh == 0, pg == 63 and kvh == 3, [b_kpb[i], b_const], [b_PS[4]])
                kb.ve("tensor_scalar_mul", (km_s.rearrange("p a b -> p (a b)"), PS[4][:, 0:128], 1.0 / 256), [b_PS[4]], [b_kms])
                for kvh in range(4):
                    kb.mm(PS[5][0:16, kvh * 32:(kvh + 1) * 32], qs_f32[:, j, 4 * kvh:4 * kvh + 4, :].rearrange("p h t -> p (h t)"),
                          km_s[:, kvh, :], True, True, [b_qs, b_kms], [b_PS[5]])
                kb.ve("tensor_copy", (gm_s, PS[5][0:16, 0:128].rearrange("p (a b) -> p a b", b=32)), [b_PS[5]], [b_gs_])
                for kvh in range(4):
                    kb.ve("max", (mx_s[:, :], gm_s[:, kvh, :]), [b_gs_], [b_gs_])
                    kb.ve("tensor_scalar", (sel_s[:, kvh, :], gm_s[:, kvh, :], mx_s[:, 2:3], None, ALU.is_ge), [b_gs_], [b_gs_])
                kb.ve("tensor_scalar", (sel_s[:, :, :], sel_s[:, :, :], -1.0, -NEG, ALU.add, ALU.mult), [b_gs_], [b_gs_])
                for kvh in range(4):
                    kb.tr(PT[0][0:32, kvh * 16:(kvh + 1) * 16], sel_s[:, kvh, :], ident_bf[0:16, 0:16], [b_gs_, b_const], [b_PT[0]])
                kb.ve("tensor_copy", (bT_s, PT[0][0:32, 0:64]), [b_PT[0]], [b_bTs])
                for pg in range(64):
                    i = pg % 2
                    n = pg // 2
                    gather(kpg[i][:, :], b_kpg[i], d_kg[i], ck_h, pg)
                    gather(vpg[i][:, :], b_vpg[i], d_vg[i], cvv_h, pg)
                    kb.act(kpb[i][:, :], kpg[i][:, :], AF.Copy, [b_kpg[i]], [b_kpb[i]])
                    kb.ve("tensor_copy", (vpb[i][:, :], vpg[i][:, :]), [b_vpg[i]], [b_vpb[i]])
                    for kvh in range(4):
                        kb.tr(PT[1][:, kvh * 128:(kvh + 1) * 128], kpb[i][:, kvh * 128:(kvh + 1) * 128], ident_bf[:, :],
                              [b_kpb[i], b_const], [b_PT[1]])
                    kb.ve("tensor_copy", (KTs[i], PT[1][:, 0:512].rearrange("p (a b) -> p a b", b=128)), [b_PT[1]], [b_KTs[i]])
                    for kvh in range(4):
                        qr = oT[:, 4 * kvh:4 * kvh + 4, LP + 4 * j:LP + 4 * j + 4]
                        key_tile(qr, 16, KTs[i][:, kvh, :], vpb[i][:, kvh * 128:(kvh + 1) * 128],
                                 (Esel[:, n, :], bT_s[:, kvh * 16:(kvh + 1) * 16], [b_causal, b_bTs]),
                                 PS[2][:, kvh * 16:(kvh + 1) * 16], PS[3][:, kvh * 16:(kvh + 1) * 16], b_PS[2], b_PS[3],
                                 pg == 0 and kvh == 0, False, [b_KTs[i]], [b_vpb[i]])
                for kvh in range(4):
                    qr = oT[:, 4 * kvh:4 * kvh + 4, LP + 4 * j:LP + 4 * j + 4]
                    key_tile(qr, 16, Kloc[:, kvh, LP:NT], Vloc[0:16, 8, kvh * 128:(kvh + 1) * 128],
                             (ident_bf[0:16, 0:16], cm_s[:, j, :], [b_const, b_causal]),
                             PS[2][:, kvh * 16:(kvh + 1) * 16], PS[3][:, kvh * 16:(kvh + 1) * 16], b_PS[2], b_PS[3],
                             False, kvh == 3, [b_Kloc], [b_Vloc], kparts=16, scbank=5)
                kb.ve("reciprocal", (rden[:, 0:64], PS[3][:, 0:64]), [b_PS[3]], [b_rden])
                for kvh in range(4):
                    qr = oT[:, 4 * kvh:4 * kvh + 4, LP + 4 * j:LP + 4 * j + 4]
                    kb.ve("tensor_tensor", (qr, PS[2][:, kvh * 16:(kvh + 1) * 16].rearrange("p (h t) -> p h t", t=4),
                                            rden[:, kvh * 16:(kvh + 1) * 16].rearrange("p (h t) -> p h t", t=4), ALU.mult),
                          [b_PS[2], b_rden], [b_oT[4 * kvh + x] for x in range(4)])

        RING_SCRATCH = []

        def make_wo(gq):
            def load():
                return wload([(lambda w: w[:, :].rearrange("p (k n) -> p k n", k=KC), w_o_v[:, :, gq * 512:(gq + 1) * 512])])

            def run(slot):
                wv = WR[slot][:, :].rearrange("p (k n) -> p k n", k=KC)
                for oo in range(4):
                    o = gq * 4 + oo
                    pp = psparts(o % 2, PARTS)
                    for pi, (c0, c1) in enumerate(PARTS):
                        pap, pb = pp[pi]
                        for k in range(KC):
                            kb.mm(pap, wv[:, k, oo * 128:(oo + 1) * 128], oT[:, k, c0:c1], k == 0, k == KC - 1, [b_WR[slot], b_oT[k]], [pb])
                        kb.ve("tensor_tensor", (A_h[:, o, c0:c1], A_h[:, o, c0:c1], pap, ALU.add), [b_hT[o], pb], [b_hT[o]])
            return load, run

        def write_y(_):
            for t in range(9):
                rows = 128 if t < 8 else NS * ST
                i = t % 2 if t < 8 else 2
                for cg in range(4):
                    for cc in range(4):
                        c = cg * 4 + cc
                        kb.tr(PS[i][0:rows, cc * 128:(cc + 1) * 128], A_h[:, c, t * 128:t * 128 + rows], ident_f[:, :],
                              [b_hT[c], b_const], [b_PS[i]])
                    kb.act(kstage[i][0:rows, :], PS[i][0:rows, :], AF.Copy, [b_PS[i]], [b_kstage[i]])
                    dst = o_yp[t * 128:(t + 1) * 128, cg * 512:(cg + 1) * 512] if t < 8 else o_ys[:, cg * 512:(cg + 1) * 512]
                    kb.dma("sync", dst, kstage[i][0:rows, :], [b_kstage[i]], [], d_kst[i])
            out_bufs.extend(b_kstage)

        if upto >= 6:
            jobs.append((None, lambda _: norm_to_xT(V_NMB)))
            jobs.append((None, load_kmean))
            for gq in range(4):
                jobs.append(make_q(gq))
            if cfg.get("dbg2"):
                def dbg2(_):
                    d_d2 = S.dma_sem("dbg2")
                    kb.dma("sync", o_km.ap(), kmean_all.rearrange("p a b -> p (a b)"), [b_km], [], d_d2)
                    kb.dma("gpsimd", o_bias.ap(), biasS[:, :, :, :].rearrange("p a b c -> p (a b c)"), [b_biasS], [], d_d2)
                    out_bufs.extend([b_km, b_biasS])
                jobs.append((None, dbg2))
            jobs.append(("NOPREFETCH", attention))
            for gq in range(4):
                jobs.append(make_wo(gq))
        if upto >= 7:
            add_ffn(1)
            jobs.append((None, write_y))

        def dbg_out(_):
            d_dbg = S.dma_sem("dbg")
            which = cfg.get("dbg", "xT")
            if which == "xT":
                kb.ve("tensor_copy", (A_h[:, :, :], xT[:, :, :]), [b_xT] + b_T + [b_vtok, b_kotok, b_attm, b_Sbf, b_qt, b_kt, b_qi, b_ko, b_vt, b_sq, b_gate], b_T)
            elif which == "hT":
                pass
            elif which == "oT":
                kb.ve("tensor_copy", (A_h[:, :, :], oT[:, :, :]), b_oT + b_T + [b_vtok, b_kotok, b_attm, b_Sbf, b_qt, b_kt, b_qi, b_ko, b_vt, b_sq, b_gate], b_T)
            kb.dma("sync", o_dbg.ap(), A_h[:, :, :], b_T + b_hT, [], d_dbg)
            out_bufs.extend(b_T + b_hT)

        if cfg.get("dbg"):
            jobs.append((None, dbg_out))

        run_jobs()
        out_bufs.extend([b_sfin])
        S.final_wait("sync", out_bufs)
        S.replay()
    return nc


def pack_vecs(inp):
    v = np.zeros((128, 128), np.float32)
    v[0:32] = np.asarray(inp["lb_logits"], np.float32).reshape(32, 128)
    v[32:48] = np.asarray(inp["norm_mix_a"], np.float32).reshape(16, 128)
    v[48:80] = np.asarray(inp["norm_ffn"], np.float32).reshape(32, 128)
    v[80:96] = np.asarray(inp["norm_kv"], np.float32).reshape(16, 128)
    v[96:112] = np.asarray(inp["norm_mix_b"], np.float32).reshape(16, 128)
    v[112] = np.asarray(inp["onorm_a"], np.float32).reshape(128)
    v[113] = np.asarray(inp["k_norm"], np.float32).reshape(128)
    v[114] = np.asarray(inp["q_norm"], np.float32).reshape(128)
    return v


def make_in_maps(inp, cfg=None):
    vecs = pack_vecs(inp)
    maps = []
    for c in range(NCORES):
        oh = np.zeros((128, 8), np.float32)
        oh[:, c] = 1.0
        m = {
            "xp": np.ascontiguousarray(inp["x_prompt"][0, c * LP:(c + 1) * LP]),
            "xs": np.ascontiguousarray(inp["x_sample"][c * NS:(c + 1) * NS].reshape(NS * ST, D)),
            "s0": np.ascontiguousarray(inp["state_hgrn"][0, c * NS:(c + 1) * NS]),
            "vecs": vecs,
            "onehot": oh,
            "w_in": np.ascontiguousarray(inp["w_in_a"][0]),
            "w_out": np.ascontiguousarray(inp["w_out_a"][0]),
            "w_gu": np.asarray(inp["w_gate_up"]),
            "w_dn": np.asarray(inp["w_down"]),
            "w_kv": np.asarray(inp["w_kv"]),
            "w_q": np.ascontiguousarray(inp["w_q_b"][0]),
            "w_o": np.ascontiguousarray(inp["w_o_b"][0]),
        }
        pm = np.full((128, 8, 32), -1e30, np.float32)
        for i in range(8):
            pm[:, i, :4 * c + i // 2] = 0.0
        m["pastmask"] = pm.reshape(128, 256)
        if (cfg or {}).get("npages"):
            ptc = np.asarray(inp["page_table"][c * NS:(c + 1) * NS]).reshape(-1)
            m["cache_k"] = np.ascontiguousarray(np.asarray(inp["cache_k"])[ptc]).reshape(256 * 128, 512)
            m["cache_v"] = np.ascontiguousarray(np.asarray(inp["cache_v"])[ptc]).reshape(256 * 128, 512)
            m["pt"] = np.arange(256, dtype=np.int32).reshape(NS, 64)
        elif not (cfg or {}).get("nocache"):
            m["cache_k"] = np.asarray(inp["cache_k"]).reshape(2560 * 128, 512)
            m["cache_v"] = np.asarray(inp["cache_v"]).reshape(2560 * 128, 512)
            m["pt"] = np.ascontiguousarray(inp["page_table"][c * NS:(c + 1) * NS]).astype(np.int32)
        maps.append(m)
    return maps


def kernel(**inputs):
    inp = {k: np.asarray(v) for k, v in inputs.items()}
    nc = build()
    res = run_bass_kernel_spmd(nc, make_in_maps(inp), core_ids=list(range(NCORES)))
    r = res.results
    cat = lambda name: np.concatenate([np.asarray(r[c][name]) for c in range(NCORES)], axis=0)
    y_prompt = cat("o_yp").reshape(1, NCORES * LP, D).astype(np.float32)
    y_sample = cat("o_ys").reshape(NCORES * NS, ST, D).astype(np.float32)
    s_prompt = np.asarray(r[NCORES - 1]["o_sp"]).reshape(1, 1, H, 128, 128).astype(np.float32)
    s_sample = cat("o_ss").reshape(1, NCORES * NS, H, 128, 128).astype(np.float32)
    k_prompt = cat("o_kp").reshape(1, NCORES * LP, 4, 128).astype(np.float32)
    v_prompt = cat("o_vp").reshape(1, NCORES * LP, 4, 128).astype(np.float32)
    k_sample = cat("o_ks").reshape(NCORES * NS, ST, 4, 128).astype(np.float32)
    v_sample = cat("o_vs").reshape(NCORES * NS, ST, 4, 128).astype(np.float32)
    return (y_prompt, y_sample, s_prompt, s_sample, k_prompt, v_prompt, k_sample, v_sample)
```

```python
import numpy as np
from contextlib import ExitStack
import concourse.bass as bass
import concourse.mybir as mybir
from concourse.bass_utils import run_bass_kernel_spmd

F32 = mybir.dt.float32
BF16 = mybir.dt.bfloat16
I32 = mybir.dt.int32
ALU = mybir.AluOpType
AF = mybir.ActivationFunctionType
AX = mybir.AxisListType

NCORES = 8
D = 2048
KC = 16
LP = 1024
NS = 4
ST = 4
NT = LP + NS * ST
H = 16
DFF = 5632
FC = DFF // 128
EPS = 1e-6
PARTS = [(0, 347), (347, 694), (694, 1040)]
ENGS = ("tensor", "vector", "scalar", "gpsimd", "sync")
GRP4 = [[0, 1, 2, 3], [4, 5, 6, 7]]
PAIRS = [[0, 4], [1, 5], [2, 6], [3, 7]]


class Buf:
    __slots__ = ("name", "w", "r")

    def __init__(self, name):
        self.name = name
        self.w = None
        self.r = {}


class Sched:
    def __init__(self, nc, stack):
        self.nc = nc
        self.stack = stack
        self.q = {e: [] for e in ENGS}
        self.cnt = {e: 0 for e in ENGS}
        self.sems = {}
        for e in ENGS:
            self.sems["E_" + e] = stack.enter_context(nc.semaphore("sem_" + e))
        self.seen = {e: {} for e in ENGS}
        self.dmacnt = {}

    def dma_sem(self, name):
        key = "D_" + name
        assert key not in self.sems
        self.sems[key] = self.stack.enter_context(self.nc.semaphore("dsem_" + name))
        self.dmacnt[key] = 0
        return key

    def op(self, eng, fn, reads=(), writes=(), dsem=None, inc=16):
        if getattr(self, "serial_buf", None) is not None:
            writes = list(writes) + [self.serial_buf]
        need = {}
        for b in reads:
            if b.w is not None:
                k, v = b.w
                if need.get(k, 0) < v:
                    need[k] = v
        for b in writes:
            if b.w is not None:
                k, v = b.w
                if need.get(k, 0) < v:
                    need[k] = v
            for k, v in b.r.items():
                if need.get(k, 0) < v:
                    need[k] = v
        if eng == "tensor":
            need.pop("E_tensor", None)
        for k in need:
            if k.startswith("D_"):
                need[k] = max(need[k], self.dmacnt[k])
        seen = self.seen[eng]
        waits = [(k, v) for k, v in need.items() if seen.get(k, 0) < v]
        for k, v in waits:
            seen[k] = v
        if dsem is None:
            self.cnt[eng] += 1
            me = ("E_" + eng, self.cnt[eng])
            incr = ("E_" + eng, 1)
        else:
            self.dmacnt[dsem] += inc
            me = (dsem, self.dmacnt[dsem])
            incr = (dsem, inc)
        self.q[eng].append((waits, fn, incr))
        for b in reads:
            if b.r.get(me[0], 0) < me[1]:
                b.r[me[0]] = me[1]
        for b in writes:
            b.w = me
            b.r = {}
        return me

    def final_wait(self, eng, bufs):
        need = {}
        for b in bufs:
            deps = list(b.r.items()) + ([b.w] if b.w else [])
            for k, v in deps:
                need[k] = max(need.get(k, 0), v)
        waits = [(k, v) for k, v in need.items() if self.seen[eng].get(k, 0) < v]
        for k, v in waits:
            self.seen[eng][k] = v
        self.q[eng].append((waits, None, None))

    def replay(self):
        nc, sems, q = self.nc, self.sems, self.q

        def run(name, e):
            for waits, fn, inc in q[name]:
                for k, v in waits:
                    e.wait_ge(sems[k], v)
                if fn is not None:
                    fn(e).then_inc(sems[inc[0]], inc[1])

        with nc.Block() as block:
            @block.tensor
            def _(e):
                run("tensor", e)

            @block.vector
            def _(e):
                run("vector", e)

            @block.scalar
            def _(e):
                run("scalar", e)

            @block.gpsimd
            def _(e):
                run("gpsimd", e)

            @block.sync
            def _(e):
                run("sync", e)


class KB:
    def __init__(self, nc, st):
        self.nc = nc
        self.st = st
        self.S = Sched(nc, st)
        self.nbuf = 0

    def sb(self, name, shape, dt):
        return self.st.enter_context(self.nc.sbuf_tensor("s_" + name, shape, dt))

    def ps(self, name, shape, dt):
        return self.st.enter_context(self.nc.psum_tensor("p_" + name, shape, dt))

    def buf(self, name=None):
        self.nbuf += 1
        return Buf(name or "b%d" % self.nbuf)

    def mm(self, out, lhsT, rhs, start, stop, r, w):
        self.S.op("tensor", lambda e: e.matmul(out, lhsT, rhs, start=start, stop=stop), r, w)

    def tr(self, out, in_, ident, r, w):
        self.S.op("tensor", lambda e: e.transpose(out, in_, ident), r, w)

    def act(self, out, in_, func, r, w, bias=None, scale=None, accum_out=None):
        kw = {}
        if bias is not None:
            kw["bias"] = bias
        if scale is not None:
            kw["scale"] = scale
        if accum_out is not None:
            kw["accum_out"] = accum_out
        self.S.op("scalar", lambda e: e.activation(out, in_, func, **kw), r, w)

    def ve(self, method, args, r, w, eng="vector", **kw):
        self.S.op(eng, lambda e: getattr(e, method)(*args, **kw), r, w)

    def dma(self, eng, out, in_, r, w, dsem, **kw):
        self.S.op(eng, lambda e: e.dma_start(out=out, in_=in_, **kw), r, w, dsem=dsem)

    def cc(self, groups, in_ap, out_ap, r, w, dsem):
        self.ncc = getattr(self, "ncc", 0) + 1
        own = self.S.dma_sem("cc%d" % self.ncc)
        if not hasattr(self, "cc_chain"):
            self.cc_chain = Buf("cc_chain")
        self.S.op("gpsimd", lambda e: e.collective_compute("AllGather", ALU.bypass, replica_groups=groups,
                                                           ins=[in_ap], outs=[out_ap]), r, list(w), dsem=own, inc=1)


def build(cfg=None):
    cfg = cfg or {}
    upto = cfg.get("upto", 99)
    nc = bass.Bass("TRN2", target_bir_lowering=False)

    def din(name, shape, dt=F32):
        return nc.dram_tensor(name, list(shape), dt, kind="ExternalInput")

    def dout(name, shape, dt=F32):
        return nc.dram_tensor(name, list(shape), dt, kind="ExternalOutput")

    xp_h = din("xp", [LP, D])
    xs_h = din("xs", [NS * ST, D])
    s0_h = din("s0", [NS, H, 128, 128])
    vecs_h = din("vecs", [128, 128])
    onehot_h = din("onehot", [128, 8])
    w_in_h = din("w_in", [D, 4 * D])
    w_out_h = din("w_out", [D, D])
    w_gu_h = din("w_gu", [2, D, 2 * DFF])
    w_dn_h = din("w_dn", [2, DFF, D])
    w_kv_h = din("w_kv", [D, 1024])
    w_q_h = din("w_q", [D, D])
    w_o_h = din("w_o", [D, D])
    pastmask_h = din("pastmask", [128, 8 * 32])
    with_cache = not cfg.get("nocache")
    if with_cache:
        npages = cfg.get("npages", 2560)
        ck_h = din("cache_k", [npages * 128, 512])
        cvv_h = din("cache_v", [npages * 128, 512])
        pt_h = din("pt", [NS, 64], I32)
    o_yp = dout("o_yp", [LP, D])
    o_ys = dout("o_ys", [NS * ST, D])
    o_kp = dout("o_kp", [LP, 512])
    o_vp = dout("o_vp", [LP, 512])
    o_ks = dout("o_ks", [NS * ST, 512])
    o_vs = dout("o_vs", [NS * ST, 512])
    o_sp = dout("o_sp", [H, 128, 128])
    o_ss = dout("o_ss", [NS, H, 128, 128])
    o_dbg = dout("o_dbg", [128, KC, NT]) if cfg.get("dbg") else None
    o_km = dout("o_km", [128, 128]) if cfg.get("dbg2") else None
    o_bias = dout("o_bias", [128, 8 * H * 32]) if cfg.get("dbg2") else None

    gs_in = [nc.dram_tensor("gs_in%d" % g, [128, 512], F32) for g in range(4)]
    gs_mid = [nc.dram_tensor("gs_mid%d" % g, [4 * 128, 512], F32) for g in range(4)]
    gs_out = [nc.dram_tensor("gs_out%d" % g, [8 * 128, 512], F32) for g in range(4)]
    gd_in = nc.dram_tensor("gd_in", [128, 512], F32)
    gd_mid = nc.dram_tensor("gd_mid", [4 * 128, 512], F32)
    gd_out = nc.dram_tensor("gd_out", [8 * 128, 512], F32)
    ck_in = [nc.dram_tensor("ck_in%d" % g, [128, LP // 2], F32) for g in range(4)]
    ck_mid = [nc.dram_tensor("ck_mid%d" % g, [4 * 128, LP // 2], F32) for g in range(4)]
    ck_out = [nc.dram_tensor("ck_out%d" % g, [8 * 128, LP // 2], F32) for g in range(4)]
    cv_in = [nc.dram_tensor("cv_in%d" % g, [128, 512], F32) for g in range(4)]
    cv_mid = [nc.dram_tensor("cv_mid%d" % g, [4 * 128, 512], F32) for g in range(4)]
    cv_out = [nc.dram_tensor("cv_out%d" % g, [8 * 128, 512], F32) for g in range(4)]
    cm_in = nc.dram_tensor("cm_in", [128, 512], F32)
    cm_mid = nc.dram_tensor("cm_mid", [4 * 128, 512], F32)
    cm_out = nc.dram_tensor("cm_out", [8 * 128, 512], F32)

    with ExitStack() as st:
        kb = KB(nc, st)
        S = kb.S
        sb, ps, buf = kb.sb, kb.ps, kb.buf

        def arena(name, nbytes):
            return sb(name, [128, nbytes // 4], F32)

        def carve(ar, off, shape, dt):
            free = 1
            for d_ in shape[1:]:
                free *= d_
            nb = free * (2 if dt == BF16 else 4)
            assert off % 4 == 0 and nb % 4 == 0
            ap = ar[0:shape[0], off // 4:(off + nb) // 4]
            if dt != F32:
                ap = ap.bitcast(dt)
            if len(shape) == 3:
                ap = ap.rearrange("p (a b) -> p a b", b=shape[2])
            return ap

        A_h_t = arena("A_h", KC * NT * 4)
        A_h = carve(A_h_t, 0, [128, KC, NT], F32)
        xT_t = arena("xT", KC * NT * 2)
        xT = carve(xT_t, 0, [128, KC, NT], BF16)
        oT_t = arena("oT", KC * NT * 2)
        oT = carve(oT_t, 0, [128, KC, NT], BF16)
        NW = 2
        WR_t = [arena("wr%d" % i, 16384) for i in range(NW)]
        WR = [carve(WR_t[i], 0, [128, 8192], BF16) for i in range(NW)]
        A_m = arena("A_m", 19456)
        b_hT = [buf("hT%d" % c) for c in range(KC)]
        b_xT = buf("xT")
        b_oT = [buf("oT%d" % c) for c in range(KC)]
        b_WR = [buf("wr%d" % i) for i in range(NW)]
        d_WR = [S.dma_sem("wr%d" % i) for i in range(NW)]
        TSZ = NT * 4
        b_T = [buf("T%d" % i) for i in range(9)]

        ident_bf = sb("ident_bf", [128, 128], BF16)
        ident_f = sb("ident_f", [128, 128], F32)
        ones_bf = sb("ones_bf", [128, 128], BF16)
        triu = sb("triu", [64, 8, 64], BF16)
        smask = sb("smask", [16, 16], F32)
        rowsel = sb("rowsel", [16, NS], F32)
        vecsT = sb("vecsT", [128, 128], F32)
        vecs_sb = sb("vecs_sb", [128, 128], F32)
        lbt = sb("lbt", [128, 2, H], F32)
        onehot = sb("onehot", [128, 8], F32)
        scanmask = sb("scanmask", [128, NT], F32)
        b_scan = buf("scanmask")
        onecol = sb("onecol", [128, 1], F32)
        b_const = buf("const")
        b_vecs = buf("vecs")
        fence_t = sb("fence_t", [128, 1], F32)

        PS = [ps("ps%d" % i, [128, 512], F32) for i in range(6)]
        PT = [ps("pt%d" % i, [128, 1024], BF16) for i in range(2)]
        b_PS = [buf("ps%d" % i) for i in range(6)]
        b_PT = [buf("pt%d" % i) for i in range(2)]

        d_ld = S.dma_sem("ld")
        d_c = S.dma_sem("const")
        d_out = S.dma_sem("out")
        d_cc = S.dma_sem("cc")
        out_bufs = []

        g = "gpsimd"
        kb.ve("memset", (ident_bf[:], 1.0), [], [b_const], eng=g)
        kb.ve("affine_select", (ident_bf[:], ident_bf[:], [[-1, 128]], ALU.is_equal, 0.0), [b_const], [b_const], eng=g,
              base=0, channel_multiplier=1)
        kb.ve("memset", (ident_f[:], 1.0), [], [b_const], eng=g)
        kb.ve("affine_select", (ident_f[:], ident_f[:], [[-1, 128]], ALU.is_equal, 0.0), [b_const], [b_const], eng=g,
              base=0, channel_multiplier=1)
        kb.ve("memset", (ones_bf[:], 1.0), [], [b_const], eng=g)
        kb.ve("memset", (triu[:], 1.0), [], [b_const], eng=g)
        kb.ve("affine_select", (triu[:], triu[:], [[0, 8], [1, 64]], ALU.is_ge, 0.0), [b_const], [b_const], eng=g,
              base=0, channel_multiplier=-1)
        kb.ve("memset", (smask[:], 1.0), [], [b_const], eng=g)
        kb.ve("affine_select", (smask[:], smask[:], [[1, 16]], ALU.is_ge, 0.0), [b_const], [b_const], eng=g,
              base=0, channel_multiplier=-1)
        for j in range(NS):
            kb.ve("affine_select", (smask[:, 4 * j:4 * j + 4], smask[:, 4 * j:4 * j + 4], [[0, 4]], ALU.is_ge, 0.0),
                  [b_const], [b_const], eng=g, base=-4 * j, channel_multiplier=1)
        kb.ve("memset", (rowsel[:], 1.0), [], [b_const], eng=g)
        kb.ve("affine_select", (rowsel[:], rowsel[:], [[-4, NS]], ALU.is_ge, 0.0), [b_const], [b_const], eng=g,
              base=0, channel_multiplier=1)
        kb.ve("affine_select", (rowsel[:], rowsel[:], [[4, NS]], ALU.is_ge, 0.0), [b_const], [b_const], eng=g,
              base=3, channel_multiplier=-1)
        kb.ve("memset", (scanmask[:], 1.0), [], [b_scan], eng=g)
        kb.ve("memset", (scanmask[:, 0:LP].rearrange("p (c t) -> p c t", t=64)[:, :, 0:1], 0.0), [], [b_scan], eng=g)
        kb.ve("memset", (scanmask[:, LP:NT].rearrange("p (c t) -> p c t", t=ST)[:, :, 0:1], 0.0), [], [b_scan], eng=g)

        kb.dma("sync", vecs_sb[:], vecs_h.ap(), [], [b_vecs], d_c)
        d_c2 = S.dma_sem("const2")
        b_oh = buf("onehot")
        kb.dma("sync", onehot[:], onehot_h.ap(), [], [b_oh], d_c2)
        kb.tr(PS[0][:, 0:128], vecs_sb[:], ident_f[:], [b_vecs, b_const], [b_PS[0]])
        kb.ve("tensor_copy", (vecsT[:], PS[0][:, 0:128]), [b_PS[0]], [b_vecs])
        kb.ve("tensor_sub", (lbt[:, 0, :], vecsT[:, 16:32], vecsT[:, 0:16]), [b_vecs], [b_vecs])
        kb.act(lbt[:, 0, :], lbt[:, 0, :], AF.Exp, [b_vecs], [b_vecs])
        kb.ve("tensor_scalar_add", (lbt[:, 0, :], lbt[:, 0, :], 1.0), [b_vecs], [b_vecs])
        kb.ve("reciprocal", (lbt[:, 0, :], lbt[:, 0, :]), [b_vecs], [b_vecs])
        kb.ve("tensor_scalar", (lbt[:, 1, :], lbt[:, 0, :], -1.0, 1.0, ALU.mult, ALU.add), [b_vecs], [b_vecs])
        V_NMA, V_NFFN, V_NKV, V_NMB, V_ON, V_KN, V_QN = 32, 48, 80, 96, 112, 113, 114

        wstate = {"n": 0}

        def wload(pieces):
            s = wstate["n"] % NW
            wstate["n"] += 1
            for dst_fn, src in pieces:
                kb.dma("gpsimd", dst_fn(WR[s]), src, [], [b_WR[s]], d_WR[s])
            return s

        jobs = []

        def run_jobs():
            loaded = {}
            order = [i for i, j in enumerate(jobs) if j[0] is not None and j[0] != "NOPREFETCH"]
            issued = 0
            for i, (ld, run) in enumerate(jobs):
                p = next((n for n, ii in enumerate(order) if ii >= i), len(order))
                tgt = min(p + NW, len(order))
                nxt_np = next((ii for ii in range(i, len(jobs)) if jobs[ii][0] == "NOPREFETCH"), None)
                if nxt_np is not None:
                    tgt = min(tgt, sum(1 for ii in order if ii < nxt_np))
                while issued < tgt:
                    loaded[order[issued]] = jobs[order[issued]][0]()
                    issued += 1
                run(loaded.get(i))

        w_in_v = w_in_h.ap().rearrange("(k p) n -> p k n", p=128)
        w_out_v = w_out_h.ap().rearrange("(k p) n -> p k n", p=128)

        xst = [carve(A_h_t, 0, [128, D], F32), carve(A_h_t, 2 * TSZ, [128, D], F32)]
        xsb = [carve(A_h_t, 4 * TSZ, [128, D], BF16), carve(A_h_t, 5 * TSZ, [128, D], BF16)]
        bl_xst = [[b_T[0], b_T[1]], [b_T[2], b_T[3]]]
        bl_xsb = [[b_T[4]], [b_T[5]]]
        d_xst = [S.dma_sem("xst%d" % i) for i in range(2)]
        ssq = sb("ssq", [128, 16], F32)
        b_ssq = buf()

        def phase1a(_):
            for t in range(9):
                rows = 128 if t < 8 else NS * ST
                i = t % 2
                src = xp_h[t * 128:(t + 1) * 128, :] if t < 8 else xs_h[:, :]
                kb.dma("sync", xst[i][0:rows, :], src, [], bl_xst[i], d_xst[i])
                kb.act(xsb[i][0:rows, :], xst[i][0:rows, :], AF.Square, bl_xst[i], bl_xsb[i] + [b_ssq],
                       accum_out=ssq[0:rows, t:t + 1])
                kb.act(ssq[0:rows, t:t + 1], ssq[0:rows, t:t + 1], AF.Ln, [b_ssq], [b_ssq], bias=EPS, scale=1.0 / D)
                kb.act(ssq[0:rows, t:t + 1], ssq[0:rows, t:t + 1], AF.Exp, [b_ssq], [b_ssq], scale=-0.5)
                kb.ve("tensor_scalar_mul", (xsb[i][0:rows, :], xst[i][0:rows, :], ssq[0:rows, t:t + 1]),
                      bl_xst[i] + [b_ssq], bl_xsb[i])
                col0 = t * 128
                for q4 in range(4):
                    pt = PT[q4 % 2]
                    bpt = b_PT[q4 % 2]
                    for cc in range(4):
                        c = q4 * 4 + cc
                        kb.tr(pt[:, cc * 128:cc * 128 + rows], xsb[i][0:rows, c * 128:(c + 1) * 128],
                              ident_bf[0:rows, 0:rows], bl_xsb[i] + [b_const], [bpt])
                    gv = vecsT[:, V_NMA + q4 * 4:V_NMA + q4 * 4 + 4]
                    kb.ve("tensor_tensor", (xT[:, q4 * 4:q4 * 4 + 4, col0:col0 + rows],
                                            pt[:, 0:512].rearrange("p (c t) -> p c t", t=128)[:, :, 0:rows],
                                            gv.unsqueeze(2).to_broadcast([128, 4, rows]), ALU.mult),
                          [bpt, b_vecs], [b_xT])

        jobs.append((None, phase1a))

        def tmp(i):
            return carve(A_h_t, i * TSZ, [128, NT], F32)
        T_ZQ, T_O = 0, 0
        T_EQ, T_QS = 1, 1
        T_F, T_OMF, T_CUM = 2, 3, 4
        T_ZG, T_A1 = 5, 5
        T_EG, T_A4 = 6, 6
        T_E = 7
        T_LF, T_RS = 8, 8
        ob = 9 * TSZ
        qt_bf = carve(A_h_t, ob + 0 * 2080, [128, NT], BF16)
        kt_bf = carve(A_h_t, ob + 1 * 2080, [128, NT], BF16)
        qi_bf = carve(A_h_t, ob + 2 * 2080, [128, NT], BF16)
        ko_bf = carve(A_h_t, ob + 3 * 2080, [128, NT], BF16)
        vt_bf = carve(A_h_t, ob + 4 * 2080, [128, NT], BF16)
        sq_bf = carve(A_h_t, ob + 5 * 2080, [128, NT], BF16)
        gate_bf = carve(A_h_t, ob + 6 * 2080, [128, NT], BF16)
        ob += 7 * 2080
        b_qt, b_kt, b_qi, b_ko, b_vt, b_sq, b_gate = [buf() for _ in range(7)]
        vtok = carve(A_h_t, ob, [64, 16, 128], BF16)
        kotok = carve(A_h_t, ob + 4096, [64, 16, 128], BF16)
        vgtok = carve(A_h_t, ob, [128, 8, 128], BF16)
        kgtok = carve(A_h_t, ob + 4096, [128, 8, 128], BF16)
        attm = carve(A_h_t, ob + 8192, [64, 16, 64], BF16)
        Sbf = carve(A_h_t, ob + 10240, [128, 16, 128], BF16)
        assert ob + 10240 + 4096 <= KC * NT * 4
        b_vtok, b_kotok, b_toks = buf(), buf(), buf()
        b_vgtok, b_kgtok = b_vtok, b_kotok
        b_attm = buf()
        b_Sbf = buf()
        Sin = carve(A_m, 0, [128, H, 128], F32)
        s0_sb = carve(A_m, 8192, [128, NS, 128], F32)
        s0_bf = carve(A_m, 10240, [128, NS, 128], BF16)
        sfin = carve(A_m, 11264, [128, NS, 128], F32)
        sloc = carve(A_m, 13312, [128, 4, 128], F32)
        mo = 15360
        Dall = carve(A_m, mo, [128, 8, H], F32)
        dtot = carve(A_m, mo + 512, [128, H], F32)
        dec = carve(A_m, mo + 576, [128, 20], F32)
        vtok_s = carve(A_m, mo + 656, [16, 128], BF16)
        kotok_s = carve(A_m, mo + 912, [16, 128], BF16)
        vm_s = carve(A_m, mo + 1168, [16, NS, 128], BF16)
        attm_s = carve(A_m, mo + 2192, [16, 16], BF16)
        Sst = carve(A_m, mo + 2224, [128, 2, 128], F32)
        assert mo + 2224 + 1024 <= 19456
        b_dec = buf()
        b_Sst = [buf(), buf()]
        b_s0, b_s0bf, b_sfin = buf(), buf(), buf()
        d_s0 = S.dma_sem("s0")
        d_sfin = S.dma_sem("sfin")
        b_Sin = [buf() for _ in range(4)]
        b_sloc, b_dtot = buf(), buf()
        d_sloc = S.dma_sem("sloc")
        d_dtot = S.dma_sem("dtot")
        b_gs = [buf() for _ in range(4)]
        b_gd = buf()

        def proj(slot, wcol0, dst_parts, kparts=PARTS):
            wv = WR[slot][:, :].rearrange("p (k n) -> p k n", k=KC)
            for (c0, c1), (pap, pb) in zip(kparts, dst_parts):
                for k in range(KC):
                    kb.mm(pap, wv[:, k, wcol0:wcol0 + 128], xT[:, k, c0:c1], k == 0, k == KC - 1,
                          [b_WR[slot], b_xT], [pb])

        HALF = [(0, 512), (512, 1024)]

        def make_prepass(h):
            def load():
                return wload([
                    (lambda w: w[:, :].rearrange("p (k n) -> p k n", k=KC)[:, :, 0:128], w_in_v[:, :, D + h * 128:D + (h + 1) * 128]),
                    (lambda w: w[:, :].rearrange("p (k n) -> p k n", k=KC)[:, :, 128:256], w_in_v[:, :, 2 * D + h * 128:2 * D + (h + 1) * 128]),
                ])

            def run(slot):
                wv = WR[slot][:, :].rearrange("p (k n) -> p k n", k=KC)
                for pi, (c0, c1) in enumerate(HALF):
                    for k in range(KC):
                        kb.mm(PS[pi][:, :], wv[:, k, 0:128], xT[:, k, c0:c1], k == 0, k == KC - 1, [b_WR[slot], b_xT], [b_PS[pi]])
                for pi, (c0, c1) in enumerate(HALF):
                    for k in range(KC):
                        kb.mm(PS[2 + pi][:, :], wv[:, k, 128:256], xT[:, k, c0:c1], k == 0, k == KC - 1, [b_WR[slot], b_xT], [b_PS[2 + pi]])
                F_, LF_, OMF_, CUM_, E_ = tmp(T_F), tmp(T_LF), tmp(T_OMF), tmp(T_CUM), tmp(T_E)
                for pi, (c0, c1) in enumerate(HALF):
                    kb.act(F_[:, c0:c1], PS[pi][:, :], AF.Exp, [b_PS[pi]], [b_T[T_F]], scale=-1.0)
                    kb.act(vt_bf[:, c0:c1], PS[2 + pi][:, :], AF.Copy, [b_PS[2 + pi]], [b_vt])
                kb.ve("tensor_scalar_add", (F_[:, 0:LP], F_[:, 0:LP], 1.0), [b_T[T_F]], [b_T[T_F]])
                kb.ve("reciprocal", (F_[:, 0:LP], F_[:, 0:LP]), [b_T[T_F]], [b_T[T_F]])
                kb.ve("tensor_scalar", (F_[:, 0:LP], F_[:, 0:LP], lbt[:, 1, h:h + 1], lbt[:, 0, h:h + 1], ALU.mult, ALU.add),
                      [b_T[T_F], b_vecs], [b_T[T_F]])
                kb.act(LF_[:, 0:LP], F_[:, 0:LP], AF.Ln, [b_T[T_F]], [b_T[T_LF]])
                kb.ve("tensor_scalar", (OMF_[:, 0:LP], F_[:, 0:LP], -1.0, 1.0, ALU.mult, ALU.add), [b_T[T_F]], [b_T[T_OMF]])
                kb.ve("tensor_tensor_scan", (CUM_[:, 0:LP], onecol[:, 0:1].to_broadcast([128, LP]), LF_[:, 0:LP], 0.0, ALU.mult, ALU.add),
                      [b_T[T_LF], b_const], [b_T[T_CUM]])
                kb.act(E_[:, 0:LP], CUM_[:, 0:LP], AF.Exp, [b_T[T_CUM]], [b_T[T_E]], bias=CUM_[:, LP - 1:LP], scale=-1.0)
                kb.ve("tensor_tensor", (ko_bf[:, 0:LP], OMF_[:, 0:LP], E_[:, 0:LP], ALU.mult), [b_T[T_OMF], b_T[T_E]], [b_ko])
                kb.act(dtot[:, h:h + 1], CUM_[:, LP - 1:LP], AF.Exp, [b_T[T_CUM]], [b_dtot])
                for t in range(8):
                    kb.tr(PT[0][:, t * 128:(t + 1) * 128], ko_bf[:, t * 128:(t + 1) * 128], ident_bf[:], [b_ko, b_const], [b_PT[0]])
                kb.ve("tensor_copy", (kgtok[:, :, :], PT[0][:, :].rearrange("p (t k) -> p t k", k=128)), [b_PT[0]], [b_kgtok])
                for t in range(8):
                    kb.tr(PT[1][:, t * 128:(t + 1) * 128], vt_bf[:, t * 128:(t + 1) * 128], ident_bf[:], [b_vt, b_const], [b_PT[1]])
                kb.act(vgtok[:, :, :], PT[1][:, :].rearrange("p (t k) -> p t k", k=128), AF.Copy, [b_PT[1]], [b_vgtok])
                for t in range(8):
                    kb.mm(PS[4][:, 0:128], kgtok[:, t, :], vgtok[:, t, :], t == 0, t == 7, [b_kgtok, b_vgtok], [b_PS[4]])
                hh = h % 4
                kb.ve("tensor_copy", (sloc[:, hh, :], PS[4][:, 0:128]), [b_PS[4]], [b_sloc])
                if hh == 3:
                    gq = h // 4
                    kb.dma("sync", gs_in[gq].ap(), sloc[:, :, :].rearrange("p a b -> p (a b)"), [b_sloc], [b_gs[gq]], d_sloc)
                    kb.cc(GRP4, gs_in[gq].ap().opt(), gs_mid[gq].ap().opt(), [b_gs[gq]], [b_gs[gq]], d_cc)
                    kb.cc(PAIRS, gs_mid[gq].ap().opt(), gs_out[gq].ap().opt(), [b_gs[gq]], [b_gs[gq]], d_cc)
                if h == H - 1:
                    kb.dma("sync", gd_in.ap(), carve(A_m, mo + 512, [128, 512], F32), [b_dtot], [b_gd], d_dtot)
                    kb.cc(GRP4, gd_in.ap().opt(), gd_mid.ap().opt(), [b_gd], [b_gd], d_cc)
                    kb.cc(PAIRS, gd_mid.ap().opt(), gd_out.ap().opt(), [b_gd], [b_gd], d_cc)
            return load, run

        kb.ve("memset", (onecol[:], 1.0), [], [b_const], eng=g)
        for h in range(H):
            jobs.append(make_prepass(h))

        G_sb = carve(A_h_t, 0, [128, 8, 512], F32)
        Pch = [carve(A_h_t, 4 * TSZ, [128, 512], F32), carve(A_h_t, 5 * TSZ, [128, 512], F32)]
        b_Dall = buf()
        bl_G = [b_T[0], b_T[1], b_T[2], b_T[3]]
        b_Pch = [b_T[4], b_T[5]]
        d_G = S.dma_sem("G")
        d_Dall = S.dma_sem("Dall")
        d_sp = S.dma_sem("sp")

        def chain(_):
            kb.dma("sync", Dall[:, :, :], gd_out.ap().rearrange("(r p) h -> p r h", p=128)[:, :, 0:H], [b_gd], [b_Dall], d_Dall)
            for gq in range(4):
                kb.dma("sync", G_sb[:, :, :], gs_out[gq].ap().rearrange("(r p) n -> p r n", p=128), [b_gs[gq]], bl_G, d_G)
                Sg = Sin[:, gq * 4:(gq + 1) * 4, :].rearrange("p a b -> p (a b)")
                kb.ve("memset", (Sg, 0.0), [], [b_Sin[gq]])
                kb.ve("tensor_copy", (Pch[0][:, :], G_sb[:, 0, :]), bl_G, [b_Pch[0]])
                cur = 0
                for j in range(1, 8):
                    kb.ve("scalar_tensor_tensor", (Sg, Pch[cur][:, :], onehot[:, j:j + 1], Sg, ALU.mult, ALU.add),
                          [b_Pch[cur], b_oh, b_Sin[gq]], [b_Sin[gq]])
                    nx = 1 - cur
                    kb.ve("tensor_tensor", (Pch[nx][:, :].rearrange("p (a b) -> p a b", a=4),
                                            Pch[cur][:, :].rearrange("p (a b) -> p a b", a=4),
                                            Dall[:, j, gq * 4:(gq + 1) * 4].unsqueeze(2).to_broadcast([128, 4, 128]), ALU.mult),
                          [b_Pch[cur], b_Dall], [b_Pch[nx]])
                    kb.ve("tensor_tensor", (Pch[nx][:, :], Pch[nx][:, :], G_sb[:, j, :], ALU.add), [b_Pch[nx]] + bl_G, [b_Pch[nx]])
                    cur = nx
                kb.dma("sync", o_sp[gq * 4:(gq + 1) * 4, :, :].rearrange("h k v -> k h v"),
                       Pch[cur][:, :].rearrange("p (a b) -> p a b", a=4), [b_Pch[cur]], [], d_sp)
                out_bufs.append(b_Pch[cur])

        jobs.append((None, chain))

        def make_main(h):
            def load():
                rr = lambda j: (lambda w: w[:, :].rearrange("p (k n) -> p k n", k=KC)[:, :, j * 128:(j + 1) * 128])
                return wload([(rr(j), w_in_v[:, :, j * D + h * 128:j * D + (h + 1) * 128]) for j in range(4)])

            def run(slot):
                gq = h // 4
                kb.dma("sync", s0_sb[:, :, :], s0_h[:, h, :, :].rearrange("j k v -> k j v"), [], [b_s0], d_s0)
                kb.act(s0_bf[:, :, :], s0_sb[:, :, :], AF.Copy, [b_s0], [b_s0bf])
                ZQ, EQ, QS, F_, LF_, OMF_, CUM_, A1, A4, E_, O_, RS, ZG, EG = [tmp(i) for i in (T_ZQ, T_EQ, T_QS, T_F, T_LF, T_OMF, T_CUM, T_A1, T_A4, T_E, T_O, T_RS, T_ZG, T_EG)]
                pq = [(PS[i][:, 0:c1 - c0], b_PS[i]) for i, (c0, c1) in enumerate(PARTS)]
                pf = [(PS[3 + i][:, 0:c1 - c0], b_PS[3 + i]) for i, (c0, c1) in enumerate(PARTS)]
                proj(slot, 0, pq)
                for i, (c0, c1) in enumerate(PARTS):
                    kb.act(ZQ[:, c0:c1], pq[i][0], AF.Copy, [pq[i][1]], [b_T[T_ZQ]])
                proj(slot, 128, pf)
                for i, (c0, c1) in enumerate(PARTS):
                    kb.act(F_[:, c0:c1], pf[i][0], AF.Exp, [pf[i][1]], [b_T[T_F]], scale=-1.0)
                proj(slot, 256, pq)
                for i, (c0, c1) in enumerate(PARTS):
                    kb.act(vt_bf[:, c0:c1], pq[i][0], AF.Copy, [pq[i][1]], [b_vt])
                proj(slot, 384, pf)
                for i, (c0, c1) in enumerate(PARTS):
                    kb.act(ZG[:, c0:c1], pf[i][0], AF.Copy, [pf[i][1]], [b_T[T_ZG]])
                kb.act(EQ, ZQ, AF.Exp, [b_T[T_ZQ]], [b_T[T_EQ]], scale=-1.0)
                kb.ve("tensor_scalar_add", (EQ, EQ, 1.0), [b_T[T_EQ]], [b_T[T_EQ]])
                kb.ve("reciprocal", (EQ, EQ), [b_T[T_EQ]], [b_T[T_EQ]])
                kb.ve("tensor_tensor", (QS, ZQ, EQ, ALU.mult), [b_T[T_ZQ], b_T[T_EQ]], [b_T[T_QS]])
                kb.ve("tensor_scalar_add", (F_, F_, 1.0), [b_T[T_F]], [b_T[T_F]])
                kb.ve("reciprocal", (F_, F_), [b_T[T_F]], [b_T[T_F]])
                kb.ve("tensor_scalar", (F_, F_, lbt[:, 1, h:h + 1], lbt[:, 0, h:h + 1], ALU.mult, ALU.add),
                      [b_T[T_F], b_vecs], [b_T[T_F]])
                kb.act(LF_, F_, AF.Ln, [b_T[T_F]], [b_T[T_LF]])
                kb.ve("tensor_scalar", (OMF_, F_, -1.0, 1.0, ALU.mult, ALU.add), [b_T[T_F]], [b_T[T_OMF]])
                kb.act(EG, ZG, AF.Exp, [b_T[T_ZG]], [b_T[T_EG]], scale=-1.0)
                kb.ve("tensor_scalar_add", (EG, EG, 1.0), [b_T[T_EG]], [b_T[T_EG]])
                kb.ve("reciprocal", (EG, EG), [b_T[T_EG]], [b_T[T_EG]])
                kb.ve("tensor_tensor", (gate_bf[:, :], ZG, EG, ALU.mult), [b_T[T_ZG], b_T[T_EG]], [b_gate])
                kb.ve("tensor_tensor_scan", (CUM_, scanmask[:, :], LF_, 0.0, ALU.mult, ALU.add), [b_T[T_LF], b_scan], [b_T[T_CUM]])
                cp = CUM_[:, 0:LP].rearrange("p (c t) -> p c t", t=64)
                cs = CUM_[:, LP:NT].rearrange("p (c t) -> p c t", t=ST)
                for (dst, mid_p, mid_s) in ((A1, 31, 1), (A4, 63, 3)):
                    kb.ve("tensor_tensor", (dst[:, 0:LP].rearrange("p (c t) -> p c t", t=64), cp,
                                            cp[:, :, mid_p:mid_p + 1].to_broadcast([128, 16, 64]), ALU.subtract),
                          [b_T[T_CUM]], [b_T[T_A1 if dst is A1 else T_A4]])
                    kb.ve("tensor_tensor", (dst[:, LP:NT].rearrange("p (c t) -> p c t", t=ST), cs,
                                            cs[:, :, mid_s:mid_s + 1].to_broadcast([128, NS, ST]), ALU.subtract),
                          [b_T[T_CUM]], [b_T[T_A1 if dst is A1 else T_A4]])
                kb.act(dec[:, 0:16], cp[:, :, 63], AF.Exp, [b_T[T_CUM]], [b_dec])
                kb.act(dec[:, 16:20], cs[:, :, 3], AF.Exp, [b_T[T_CUM]], [b_dec])
                kb.act(E_, A1, AF.Exp, [b_T[T_A1]], [b_T[T_E]])
                kb.ve("tensor_tensor", (qt_bf[:, :], QS, E_, ALU.mult), [b_T[T_QS], b_T[T_E]], [b_qt])
                kb.act(E_, A1, AF.Exp, [b_T[T_A1], b_T[T_E]], [b_T[T_E]], scale=-1.0)
                kb.ve("tensor_tensor", (kt_bf[:, :], OMF_, E_, ALU.mult), [b_T[T_OMF], b_T[T_E]], [b_kt])
                kb.act(E_, CUM_, AF.Exp, [b_T[T_CUM], b_T[T_E]], [b_T[T_E]])
                kb.ve("tensor_tensor", (qi_bf[:, :], QS, E_, ALU.mult), [b_T[T_QS], b_T[T_E]], [b_qi])
                kb.act(E_, A4, AF.Exp, [b_T[T_A4], b_T[T_E]], [b_T[T_E]], scale=-1.0)
                kb.ve("tensor_tensor", (ko_bf[:, :], OMF_, E_, ALU.mult), [b_T[T_OMF], b_T[T_E]], [b_ko])
                for half in range(2):
                    for cc in range(8):
                        c = half * 8 + cc
                        kb.tr(PT[0][0:64, cc * 128:(cc + 1) * 128], vt_bf[:, c * 64:(c + 1) * 64], ident_bf[:], [b_vt, b_const], [b_PT[0]])
                    kb.ve("tensor_copy", (vtok[:, half * 8:half * 8 + 8, :], PT[0][0:64, :].rearrange("p (c k) -> p c k", k=128)),
                          [b_PT[0]], [b_vtok])
                    for cc in range(8):
                        c = half * 8 + cc
                        kb.tr(PT[1][0:64, cc * 128:(cc + 1) * 128], ko_bf[:, c * 64:(c + 1) * 64], ident_bf[:], [b_ko, b_const], [b_PT[1]])
                    kb.act(kotok[:, half * 8:half * 8 + 8, :], PT[1][0:64, :].rearrange("p (c k) -> p c k", k=128), AF.Copy,
                           [b_PT[1]], [b_kotok])
                kb.tr(PT[0][0:16, 0:128], vt_bf[:, LP:NT], ident_bf[:], [b_vt, b_const], [b_PT[0]])
                kb.tr(PT[0][0:16, 128:256], ko_bf[:, LP:NT], ident_bf[:], [b_ko, b_const], [b_PT[0]])
                kb.ve("tensor_copy", (vtok_s[:, :], PT[0][0:16, 0:128]), [b_PT[0]], [b_toks])
                kb.ve("tensor_copy", (kotok_s[:, :], PT[0][0:16, 128:256]), [b_PT[0]], [b_toks])
                for j in range(NS):
                    kb.ve("tensor_scalar_mul", (vm_s[:, j, :], vtok_s[:, :], rowsel[:, j:j + 1]), [b_toks, b_const], [b_toks])
                for half in range(2):
                    for cc in range(8):
                        c = half * 8 + cc
                        kb.mm(PS[half][0:64, cc * 64:(cc + 1) * 64], kt_bf[:, c * 64:(c + 1) * 64], qt_bf[:, c * 64:(c + 1) * 64],
                              True, True, [b_kt, b_qt], [b_PS[half]])
                    kb.ve("tensor_tensor", (attm[:, half * 8:half * 8 + 8, :], PS[half][0:64, :].rearrange("p (c t) -> p c t", t=64),
                                            triu[:, :, :], ALU.mult), [b_PS[half], b_const], [b_attm])
                kb.mm(PS[2][0:16, 0:16], kt_bf[:, LP:NT], qt_bf[:, LP:NT], True, True, [b_kt, b_qt], [b_PS[2]])
                kb.ve("tensor_tensor", (attm_s[:, :], PS[2][0:16, 0:16], smask[:, :], ALU.mult), [b_PS[2], b_const], [b_attm])
                for q4 in range(4):
                    for cc in range(4):
                        c = q4 * 4 + cc
                        kb.mm(PS[2 + (q4 % 2)][:, cc * 128:(cc + 1) * 128], kotok[:, c, :], vtok[:, c, :], True, True,
                              [b_kotok, b_vtok], [b_PS[2 + (q4 % 2)]])
                    for cc in range(4):
                        c = q4 * 4 + cc
                        if c == 0:
                            kb.ve("tensor_copy", (Sbf[:, 0, :], Sin[:, h, :]), [b_Sin[gq]], [b_Sbf])
                            prev = Sin[:, h, :]
                            prev_b = b_Sin[gq]
                        else:
                            prev = Sst[:, (c - 1) % 2, :]
                            prev_b = b_Sst[(c - 1) % 2]
                        kb.ve("scalar_tensor_tensor", (Sst[:, c % 2, :], prev, dec[:, c:c + 1], PS[2 + (q4 % 2)][:, cc * 128:(cc + 1) * 128],
                                                       ALU.mult, ALU.add), [prev_b, b_dec, b_PS[2 + (q4 % 2)]], [b_Sst[c % 2]])
                        if c < 15:
                            kb.act(Sbf[:, c + 1, :], Sst[:, c % 2, :], AF.Copy, [b_Sst[c % 2]], [b_Sbf])
                for half in range(2):
                    for cc in range(8):
                        c = half * 8 + cc
                        dst = PS[4 + half][:, cc * 64:(cc + 1) * 64]
                        kb.mm(dst, vtok[:, c, :], attm[:, c, :], True, False, [b_vtok, b_attm], [b_PS[4 + half]])
                        kb.mm(dst, Sbf[:, c, :], qi_bf[:, c * 64:(c + 1) * 64], False, True, [b_Sbf, b_qi], [b_PS[4 + half]])
                    kb.act(O_[:, half * 512:(half + 1) * 512], PS[4 + half][:, :], AF.Copy, [b_PS[4 + half]], [b_T[T_O]])
                kb.mm(PS[0][:, 0:16], vtok_s[:, :], attm_s[:, :], True, False, [b_toks, b_attm], [b_PS[0]])
                for j in range(NS):
                    kb.mm(PS[0][:, 4 * j:4 * j + 4], s0_bf[:, j, :], qi_bf[:, LP + 4 * j:LP + 4 * j + 4], False, j == NS - 1,
                          [b_s0bf, b_qi], [b_PS[0]])
                kb.act(O_[:, LP:NT], PS[0][:, 0:16], AF.Copy, [b_PS[0]], [b_T[T_O]])
                for j in range(NS):
                    kb.mm(PS[1][:, j * 128:(j + 1) * 128], kotok_s[:, :], vm_s[:, j, :], True, True, [b_toks], [b_PS[1]])
                for j in range(NS):
                    kb.ve("scalar_tensor_tensor", (sfin[:, j, :], s0_sb[:, j, :], dec[:, 16 + j:17 + j], PS[1][:, j * 128:(j + 1) * 128],
                                                   ALU.mult, ALU.add), [b_s0, b_dec, b_PS[1]], [b_sfin])
                kb.dma("sync", o_ss[:, h, :, :].rearrange("j k v -> k j v"), sfin[:, :, :], [b_sfin], [], d_sfin)
                kb.act(sq_bf[:, :], O_, AF.Square, [b_T[T_O]], [b_sq])
                for i, (c0, c1) in enumerate(PARTS):
                    kb.mm(PS[1 + i][:, 0:c1 - c0], ones_bf[:, :], sq_bf[:, c0:c1], True, True, [b_const, b_sq], [b_PS[1 + i]])
                    kb.act(RS[:, c0:c1], PS[1 + i][:, 0:c1 - c0], AF.Ln, [b_PS[1 + i]], [b_T[T_RS]], bias=EPS, scale=1.0 / 128)
                kb.act(RS, RS, AF.Exp, [b_T[T_RS]], [b_T[T_RS]], scale=-0.5)
                kb.ve("scalar_tensor_tensor", (O_, O_, vecsT[:, V_ON:V_ON + 1], RS, ALU.mult, ALU.mult),
                      [b_T[T_O], b_vecs, b_T[T_RS]], [b_T[T_O]])
                kb.ve("tensor_tensor", (oT[:, h, :], O_, gate_bf[:, :], ALU.mult), [b_T[T_O], b_gate], [b_oT[h]])
            return load, run

        if upto >= 2:
            for h in range(H):
                jobs.append(make_main(h))

        PARTS_R = [(0, 512), (512, 1024), (1024, NT)]
        ALL_AH = b_T + [b_qt, b_kt, b_qi, b_ko, b_vt, b_sq, b_gate, b_vtok, b_kotok, b_attm, b_Sbf]
        ALL_AM = b_Sin + [b_s0, b_s0bf, b_sfin, b_sloc, b_dtot, b_dec, b_toks, b_attm, b_Dall] + b_Sst
        xblk = [carve(A_m, i * 4608, [128, 8, 128], F32) for i in range(2)]
        xblk_s = [carve(A_m, i * 4608 + 4096, [16, 128], F32) for i in range(2)]
        b_xblk = [buf(), buf()]
        d_xblk = [S.dma_sem("xblk0"), S.dma_sem("xblk1")]
        sg = carve(A_m, 9216, [128, NT], F32)
        b_sg = buf()
        rstd = scanmask
        b_rstd = b_scan
        xp_blk = xp_h.ap().rearrange("(t p) d -> p t d", p=128)
        fence = {"ah": True, "am": True}

        def psparts(sel, parts):
            return [(PS[3 * sel + i][:, 0:c1 - c0], b_PS[3 * sel + i]) for i, (c0, c1) in enumerate(parts)]

        def make_wout(gq):
            def load():
                return wload([(lambda w: w[:, :].rearrange("p (k n) -> p k n", k=KC), w_out_v[:, :, gq * 512:(gq + 1) * 512])])

            def run(slot):
                wv = WR[slot][:, :].rearrange("p (k n) -> p k n", k=KC)
                for oo in range(4):
                    o = gq * 4 + oo
                    i = o % 2
                    extra = ALL_AM if fence["am"] else []
                    kb.dma("sync", xblk[i][:, :, :], xp_blk[:, :, o * 128:(o + 1) * 128], [], [b_xblk[i]] + (extra if o < 2 else []), d_xblk[i])
                    kb.dma("sync", xblk_s[i][:, :], xs_h[:, o * 128:(o + 1) * 128], [], [b_xblk[i]], d_xblk[i])
                    pp = psparts(o % 2, PARTS_R)
                    for pi, (c0, c1) in enumerate(PARTS_R):
                        pap, pb = pp[pi]
                        for k in range(KC):
                            kb.mm(pap, wv[:, k, oo * 128:(oo + 1) * 128], oT[:, k, c0:c1], k == 0, False, [b_WR[slot]] + b_oT, [pb])
                        if pi < 2:
                            for tt in range(4):
                                t = pi * 4 + tt
                                kb.mm(pap[:, tt * 128:(tt + 1) * 128], xblk[i][:, t, :], ident_f[:, :], False, tt == 3,
                                      [b_xblk[i], b_const], [pb])
                        else:
                            kb.mm(pap, xblk_s[i][:, :], ident_f[0:16, 0:16], False, True, [b_xblk[i], b_const], [pb])
                        kb.act(A_h[:, o, c0:c1], pap, AF.Copy, [pb], [b_hT[o]] + ALL_AH)
            return load, run

        def norm_to_xT(gcol):
            for c in range(KC):
                kb.act(xT[:, c, :], A_h[:, c, :], AF.Square, [b_hT[c]], [b_xT])
            for pi, (c0, c1) in enumerate(PARTS):
                for c in range(KC):
                    kb.mm(PS[pi][:, 0:c1 - c0], ones_bf[:, :], xT[:, c, c0:c1], c == 0, c == KC - 1, [b_const, b_xT], [b_PS[pi]])
                kb.act(rstd[:, c0:c1], PS[pi][:, 0:c1 - c0], AF.Ln, [b_PS[pi]], [b_rstd], bias=EPS, scale=1.0 / D)
            kb.act(rstd[:, :], rstd[:, :], AF.Exp, [b_rstd], [b_rstd], scale=-0.5)
            for c in range(KC):
                kb.ve("scalar_tensor_tensor", (xT[:, c, :], A_h[:, c, :], vecsT[:, gcol + c:gcol + c + 1], rstd[:, :], ALU.mult, ALU.mult),
                      [b_hT[c], b_vecs, b_rstd], [b_xT])

        def add_ffn(l):
            w_gu_v = w_gu_h[l].rearrange("(k p) n -> p k n", p=128)
            w_dn_v = w_dn_h[l].rearrange("(k p) n -> p k n", p=128)
            def pre(_, l=l):
                if l == 1:
                    S.op("vector", lambda e: e.memset(fence_t[:, 0:1], 0.0), [], [b_sg, b_Vloc, b_Kloc, b_pm])
                norm_to_xT(V_NFFN + 16 * l)
            jobs.append((None, pre))
            for gi in range(4):
                tiles = list(range(gi * 11, gi * 11 + 11))
                pairs = [tiles[i:i + 2] for i in range(0, 11, 2)]
                for pr in pairs:
                    def load(pr=pr):
                        pcs = []
                        for n_, j in enumerate(pr):
                            pcs.append((lambda w, n_=n_: w[:, :].rearrange("p (k n) -> p k n", k=KC)[:, :, n_ * 256:n_ * 256 + 128],
                                        w_gu_v[:, :, j * 128:(j + 1) * 128]))
                            pcs.append((lambda w, n_=n_: w[:, :].rearrange("p (k n) -> p k n", k=KC)[:, :, n_ * 256 + 128:n_ * 256 + 256],
                                        w_gu_v[:, :, DFF + j * 128:DFF + (j + 1) * 128]))
                        return wload(pcs)

                    def run(slot, pr=pr, gi=gi):
                        for n_, j in enumerate(pr):
                            jj = j - gi * 11
                            pg = psparts(0, PARTS)
                            pu = psparts(1, PARTS)
                            proj(slot, n_ * 256, pg)
                            for i, (c0, c1) in enumerate(PARTS):
                                kb.act(sg[:, c0:c1], pg[i][0], AF.Silu, [pg[i][1]], [b_sg])
                            proj(slot, n_ * 256 + 128, pu)
                            for i, (c0, c1) in enumerate(PARTS):
                                kb.ve("tensor_tensor", (oT[:, jj, c0:c1], sg[:, c0:c1], pu[i][0], ALU.mult), [b_sg, pu[i][1]], [b_oT[jj]])
                    jobs.append((load, run))
                for oq in range(4):
                    def load(oq=oq, gi=gi):
                        return wload([(lambda w: w[:, 0:11 * 512].rearrange("p (k n) -> p k n", k=11),
                                       w_dn_v[:, gi * 11:gi * 11 + 11, oq * 512:(oq + 1) * 512])])

                    def run(slot, oq=oq, gi=gi):
                        wv = WR[slot][:, 0:11 * 512].rearrange("p (k n) -> p k n", k=11)
                        for oo in range(4):
                            o = oq * 4 + oo
                            pp = psparts(o % 2, PARTS)
                            for pi, (c0, c1) in enumerate(PARTS):
                                pap, pb = pp[pi]
                                for kk in range(11):
                                    kb.mm(pap, wv[:, kk, oo * 128:(oo + 1) * 128], oT[:, kk, c0:c1], kk == 0, kk == 10,
                                          [b_WR[slot], b_oT[kk]], [pb])
                                kb.ve("tensor_tensor", (A_h[:, o, c0:c1], A_h[:, o, c0:c1], pap, ALU.add), [b_hT[o], pb], [b_hT[o]])
                    jobs.append((load, run))

        kT = carve(oT_t, 0, [128, 4, NT], F32)
        vT = carve(oT_t, 4 * NT * 4, [128, 4, NT], F32)
        Kloc = carve(A_m, 0, [128, 4, NT], BF16)
        Vloc = carve(A_m, 8320, [128, 9, 512], BF16)
        Kloc32 = carve(A_m, 0, [128, 4, NT // 2], F32)
        Vloc32 = carve(A_m, 8320, [128, 9, 256], F32)
        b_Kloc, b_Vloc = buf(), buf()
        kstage = [sb("kstage%d" % i, [128, 512], F32) for i in range(2)] + [sb("kstage2", [16, 512], F32)]
        b_kstage = [buf(), buf(), buf()]
        d_kst = [S.dma_sem("kst0"), S.dma_sem("kst1"), S.dma_sem("kst2")]
        kmean_own = sb("kmean_own", [128, 4, 4], F32)
        b_kmo = buf()
        d_ckx = [S.dma_sem("ckx%d" % i) for i in range(4)]
        d_cvx = [S.dma_sem("cvx%d" % i) for i in range(4)]
        d_cmx = S.dma_sem("cmx")
        b_ck = [buf() for _ in range(4)]
        b_cv = [buf() for _ in range(4)]
        b_cm = buf()
        w_kv_v = w_kv_h.ap().rearrange("(k p) n -> p k n", p=128)

        def make_kv(which):
            def load():
                return wload([(lambda w: w[:, :].rearrange("p (k n) -> p k n", k=KC), w_kv_v[:, :, which * 512:(which + 1) * 512])])

            def run(slot):
                dstT = kT if which == 0 else vT
                for kvh in range(4):
                    pp = psparts(kvh % 2, PARTS)
                    proj(slot, kvh * 128, pp)
                    for i, (c0, c1) in enumerate(PARTS):
                        kb.act(dstT[:, kvh, c0:c1], pp[i][0], AF.Copy, [pp[i][1]], [b_oT[kvh + 4 * which]])
                if which == 0:
                    for kvh in range(4):
                        bk = b_oT[kvh]
                        kb.act(Kloc[:, kvh, :], kT[:, kvh, :], AF.Square, [bk], [b_Kloc] + ((b_xblk + [b_sg]) if kvh == 0 else []))
                        for pi, (c0, c1) in enumerate(PARTS):
                            kb.mm(PS[pi][:, 0:c1 - c0], ones_bf[:, :], Kloc[:, kvh, c0:c1], True, True, [b_const, b_Kloc], [b_PS[pi]])
                            kb.act(rstd[:, c0:c1], PS[pi][:, 0:c1 - c0], AF.Ln, [b_PS[pi]], [b_rstd], bias=EPS, scale=1.0 / 128)
                        kb.act(rstd[:, :], rstd[:, :], AF.Exp, [b_rstd], [b_rstd], scale=-0.5)
                        kb.ve("scalar_tensor_tensor", (kT[:, kvh, :], kT[:, kvh, :], vecsT[:, V_KN:V_KN + 1], rstd[:, :], ALU.mult, ALU.mult),
                              [bk, b_vecs, b_rstd], [bk])
                        kb.act(Kloc[:, kvh, :], kT[:, kvh, :], AF.Copy, [bk], [b_Kloc])
                        kb.ve("tensor_reduce", (kmean_own[:, kvh, :], kT[:, kvh, 0:LP].rearrange("p (b t) -> p b t", t=256), AX.X, ALU.add),
                              [bk], [b_kmo])
                        kb.dma("sync", ck_in[kvh].ap(), Kloc32[:, kvh, 0:LP // 2], [b_Kloc], [b_ck[kvh]], d_ckx[kvh])
                        if not cfg.get("nocc_k"):
                            kb.cc(GRP4, ck_in[kvh].ap().opt(), ck_mid[kvh].ap().opt(), [b_ck[kvh]], [b_ck[kvh]], d_cc)
                            kb.cc(PAIRS, ck_mid[kvh].ap().opt(), ck_out[kvh].ap().opt(), [b_ck[kvh]], [b_ck[kvh]], d_cc)
                    kb.ve("tensor_scalar_mul", (rstd[:, 0:16], kmean_own[:, :, :].rearrange("p a b -> p (a b)"), 1.0 / 256), [b_kmo], [b_rstd])
                    kb.dma("sync", cm_in.ap(), rstd[:, 0:512], [b_rstd], [b_cm], d_cmx)
                    if not cfg.get("nocc_m"):
                        kb.cc(GRP4, cm_in.ap().opt(), cm_mid.ap().opt(), [b_cm], [b_cm], d_cc)
                        kb.cc(PAIRS, cm_mid.ap().opt(), cm_out.ap().opt(), [b_cm], [b_cm], d_cc)
                o_p, o_s = (o_kp, o_ks) if which == 0 else (o_vp, o_vs)
                for t in range(0 if cfg.get("kv_noout") else 9):
                    rows = 128 if t < 8 else NS * ST
                    i = t % 2 if t < 8 else 2
                    for kvh in range(4):
                        kb.tr(PS[i][0:rows, kvh * 128:(kvh + 1) * 128], dstT[:, kvh, t * 128:t * 128 + rows], ident_f[:, :],
                              [b_oT[kvh + 4 * which], b_const], [b_PS[i]])
                    kb.act(kstage[i][0:rows, :], PS[i][0:rows, :], AF.Copy, [b_PS[i]], [b_kstage[i]])
                    if which == 1:
                        kb.ve("tensor_copy", (Vloc[0:rows, t, :], kstage[i][0:rows, :]), [b_kstage[i]], [b_Vloc] + ((b_xblk + [b_sg]) if t == 0 else []))
                    dst = o_p[t * 128:(t + 1) * 128, :] if t < 8 else o_s[:, :]
                    kb.dma("sync", dst, kstage[i][0:rows, :], [b_kstage[i]], [], d_kst[i])
                if which == 1 and not cfg.get("kv_noout"):
                    for kvh in range(4):
                        kb.dma("sync", cv_in[kvh].ap().rearrange("p (t d) -> p t d", d=64), Vloc32[:, 0:8, kvh * 64:(kvh + 1) * 64],
                               [b_Vloc], [b_cv[kvh]], d_cvx[kvh])
                        if not cfg.get("nocc_v"):
                            kb.cc(GRP4, cv_in[kvh].ap().opt(), cv_mid[kvh].ap().opt(), [b_cv[kvh]], [b_cv[kvh]], d_cc)
                            kb.cc(PAIRS, cv_mid[kvh].ap().opt(), cv_out[kvh].ap().opt(), [b_cv[kvh]], [b_cv[kvh]], d_cc)
                    out_bufs.extend(b_kstage)
            return load, run

        if upto >= 3:
            for gq in range(4):
                jobs.append(make_wout(gq))
        if upto >= 4:
            add_ffn(0)
        if upto >= 5:
            jobs.append((None, lambda _: norm_to_xT(V_NKV)))
            jobs.append(make_kv(0))
            jobs.append(make_kv(1))

        SCALE = 128.0 ** -0.5
        NEG = -30000.0
        QF = sb("QF", [128, NT], F32)
        biasS = sb("biasS", [128, 8, H, 32], BF16)
        b_QF, b_biasS = buf(), buf()
        GM = carve(kstage[0], 0, [128, 8, 32], F32)
        SEL = carve(kstage[0], 1024, [128, 8, 32], F32)
        kmean_all = carve(kstage[1], 0, [128, 4, 32], F32)
        max8 = carve(kstage[1], 512, [128, 8, 8], F32)
        thr = carve(kstage[1], 768, [128, 8], F32)
        kml = carve(kstage[1], 1024, [128, 8, 16], F32)
        b_gm, b_km = b_kstage[0], b_kstage[1]
        d_att = S.dma_sem("att")
        d_pm = S.dma_sem("pm")
        w_q_v = w_q_h.ap().rearrange("(k p) n -> p k n", p=128)
        w_o_v = w_o_h.ap().rearrange("(k p) n -> p k n", p=128)

        def load_kmean(_):
            kb.dma("sync", kml[:, :, :], cm_out.ap().rearrange("(r p) n -> p r n", p=128)[:, :, 0:16], [b_cm], [b_km], d_att)
            for kvh in range(4):
                kb.ve("tensor_copy", (kmean_all[:, kvh, :].rearrange("p (r b) -> p r b", b=4), kml[:, :, kvh * 4:(kvh + 1) * 4]),
                      [b_km], [b_km])
            kb.dma("sync", pm_sb[:, :, :], pastmask_h.ap().rearrange("p (t n) -> p t n", n=32), [], [b_pm] + ALL_AM + b_xblk + [b_sg], d_pm)

        pm_sb = carve(A_m, 17536, [128, 8, 32], F32)
        b_pm = buf()

        def gate_tiles(hd, src, rows, ntile, gm, sel, mx, th, pm, ps_ap, ps_b):
            kvh = hd // 4
            for t in range(ntile):
                kb.mm(ps_ap[0:rows, t * 32:(t + 1) * 32], src(t), kmean_all[:, kvh, :], True, True, [b_QF, b_km], [ps_b])
            g3 = ps_ap[0:rows, 0:ntile * 32].rearrange("p (t n) -> p t n", n=32)
            if pm is not None:
                kb.ve("tensor_tensor", (gm, g3, pm, ALU.add), [ps_b, b_pm], [b_gm])
            else:
                kb.ve("tensor_copy", (gm, g3), [ps_b], [b_gm])
            for t in range(ntile):
                kb.ve("max", (mx[:, t, :], gm[:, t, :]), [b_gm], [b_km])
            kb.ve("tensor_scalar_max", (th, mx[:, :, 2], -1e29), [b_km], [b_km])
            kb.ve("tensor_tensor", (sel, gm, th.unsqueeze(2).to_broadcast([rows, ntile, 32]), ALU.is_ge), [b_gm, b_km], [b_gm])
            return sel

        def make_q(gq):
            def load():
                return wload([(lambda w: w[:, :].rearrange("p (k n) -> p k n", k=KC), w_q_v[:, :, gq * 512:(gq + 1) * 512])])

            def run(slot):
                for hh in range(4):
                    hd = gq * 4 + hh
                    pp = psparts(hh % 2, PARTS)
                    proj(slot, hh * 128, pp)
                    for i, (c0, c1) in enumerate(PARTS):
                        kb.act(QF[:, c0:c1], pp[i][0], AF.Copy, [pp[i][1]], [b_QF])
                    fence_w = b_oT if hd == 0 else [b_oT[hd]]
                    kb.act(oT[:, hd, :], QF[:, :], AF.Square, [b_QF], fence_w)
                    for pi, (c0, c1) in enumerate(PARTS):
                        kb.mm(PS[pi][:, 0:c1 - c0], ones_bf[:, :], oT[:, hd, c0:c1], True, True, [b_const, b_oT[hd]], [b_PS[pi]])
                        kb.act(rstd[:, c0:c1], PS[pi][:, 0:c1 - c0], AF.Ln, [b_PS[pi]], [b_rstd], bias=EPS, scale=1.0 / 128)
                    kb.act(rstd[:, :], rstd[:, :], AF.Exp, [b_rstd], [b_rstd], scale=-0.5)
                    kb.ve("scalar_tensor_tensor", (QF[:, :], QF[:, :], vecsT[:, V_QN:V_QN + 1], rstd[:, :], ALU.mult, ALU.mult),
                          [b_QF, b_vecs, b_rstd], [b_QF])
                    kb.act(oT[:, hd, :], QF[:, :], AF.Copy, [b_QF], [b_oT[hd]])
                    sel = gate_tiles(hd, lambda t: QF[:, t * 128:(t + 1) * 128], 128, 8, GM, SEL, max8, thr, pm_sb[:, :, :], PS[4][:, 0:256], b_PS[4])
                    kb.ve("tensor_scalar", (biasS[:, 0:8, hd, :], sel, -1.0, -NEG, ALU.add, ALU.mult), [b_gm], [b_biasS])
                    kb.ve("tensor_copy", (qs_f32[:, :, hd, :], QF[:, LP:NT].rearrange("p (j t) -> p j t", t=ST)), [b_QF], [b_qs])
            return load, run

        NPT = 2
        Pt = [carve(WR_t[1], i * 1024, [128, 512], BF16) for i in range(NPT)]
        biasT = [carve(WR_t[1], 2048 + i * 1024, [32, 512], BF16) for i in range(2)]
        causalT = carve(WR_t[1], 4096, [128, 4, 128], BF16)
        rden = carve(WR_t[1], 5120, [128, 512], F32)
        cm_s = carve(WR_t[1], 7168, [16, NS, 16], BF16)
        Esel = carve(WR_t[1], 7680, [32, 32, 128], BF16)
        b_Pt = [buf() for _ in range(NPT)]
        b_biasT = [buf(), buf()]
        b_causal, b_rden = buf(), buf()
        KA = carve(xT_t, 0, [128, 8, LP], BF16)
        VA = carve(xT_t, 16384, [128, 8, 8, 128], BF16) if False else None
        VA32 = carve(xT_t, 16384, [128, 8, 512], F32)
        VAb = xT_t[:, 4096:8192].bitcast(BF16).rearrange("p (r t d) -> p r t d", r=8, t=8)
        KA32 = carve(xT_t, 0, [128, 8, LP // 2], F32)
        b_KA, b_VA = buf(), buf()
        d_KA = S.dma_sem("KA")

        ring_bufs = []

        def attention(_):
            if cfg.get("serial_att", False):
                S.serial_buf = Buf("serial")
            W0, W1 = b_WR[0], b_WR[1]
            fme = S.op("gpsimd", lambda e: e.memset(fence_t[:, 0:1], 0.0), [], [W0, W1])
            for b_ in b_Pt + b_biasT + [b_causal, b_rden]:
                b_.w = fme
                ring_bufs.append(b_)
            attention.fme = fme

            def fw():
                return []
            kb.ve("memset", (causalT, 0.0), [], [b_causal] + fw(), eng="gpsimd")
            kb.ve("affine_select", (causalT, causalT, [[0, 4], [1, 128]], ALU.is_ge, NEG), [b_causal], [b_causal], eng="gpsimd",
                  base=0, channel_multiplier=-1)
            kb.ve("memset", (Esel, 1.0), [], [b_causal], eng="gpsimd")
            kb.ve("affine_select", (Esel, Esel, [[-1, 32], [0, 128]], ALU.is_equal, 0.0), [b_causal], [b_causal], eng="gpsimd",
                  base=0, channel_multiplier=1)
            kb.ve("memset", (cm_s, 0.0), [], [b_causal], eng="gpsimd")
            for j in range(NS):
                kb.ve("affine_select", (cm_s[:, j, :].rearrange("p (h t) -> p h t", t=4), cm_s[:, j, :].rearrange("p (h t) -> p h t", t=4),
                                        [[0, 4], [1, 4]], ALU.is_ge, NEG), [b_causal], [b_causal], eng="gpsimd", base=4 * j, channel_multiplier=-1)
                kb.ve("affine_select", (cm_s[:, j, :], cm_s[:, j, :], [[0, 16]], ALU.is_ge, NEG), [b_causal], [b_causal], eng="gpsimd",
                      base=-4 * j, channel_multiplier=1)
            pcount = {"n": 0, "sc": 0}

            def key_tile(qrhs, ncol, kT_ap, v_ap, bias_mm, acc, den, accb, denb, first_t, last_t, kr, vr, kparts=128, scbank=None):
                si = pcount["sc"] % 2 if scbank is None else scbank
                pcount["sc"] += 1
                sc, scb = PS[si][0:kparts, 0:ncol], b_PS[si]
                kb.mm(sc, kT_ap, qrhs, True, bias_mm is None, kr + b_oT, [scb])
                if bias_mm is not None:
                    l_, r_, rb = bias_mm
                    kb.mm(sc, l_, r_, False, True, rb, [scb])
                pi = pcount["n"] % NPT
                pcount["n"] += 1
                kb.act(Pt[pi][0:kparts, 0:ncol], sc, AF.Exp, [scb], [b_Pt[pi]], scale=SCALE)
                kb.mm(acc, v_ap, Pt[pi][0:kparts, 0:ncol], first_t, last_t, vr + [b_Pt[pi]], [accb])
                kb.mm(den, ones_bf[0:kparts, :], Pt[pi][0:kparts, 0:ncol], first_t, last_t, [b_const, b_Pt[pi]], [denb])

            for kvh in range(4):
                kb.dma("sync", KA32[:, :, :], ck_out[kvh].ap().rearrange("(r p) n -> p r n", p=128), [b_ck[kvh]], [b_KA, b_xT], d_KA)
                kb.dma("sync", VA32[:, :, :], cv_out[kvh].ap().rearrange("(r p) n -> p r n", p=128), [b_cv[kvh]], [b_VA, b_xT], d_KA)
                for i in range(8):
                    qrhs = oT[:, 4 * kvh:4 * kvh + 4, i * 128:(i + 1) * 128]
                    bt = biasT[i % 2]
                    for h4 in range(4):
                        kb.tr(PT[0][0:32, h4 * 128:(h4 + 1) * 128], biasS[:, i, 4 * kvh + h4, :], ident_bf[:, :], [b_biasS, b_const], [b_PT[0]])
                    kb.ve("tensor_copy", (bt, PT[0][0:32, 0:512]), [b_PT[0]], [b_biasT[i % 2]])
                    nb = 28 + i // 2
                    acc, den = PS[2][:, :], PS[3][:, :]
                    for kt in range(2 * nb):
                        n = kt // 2
                        r, lc = kt // 8, (kt % 8) * 128
                        key_tile(qrhs, 512, KA[:, r, lc:lc + 128], VAb[:, r, kt % 8, :],
                                 (Esel[:, n, :], bt, [b_causal, b_biasT[i % 2]]),
                                 acc, den, b_PS[2], b_PS[3], kt == 0, False, [b_KA], [b_VA])
                    if i % 2 == 1:
                        key_tile(qrhs, 512, Kloc[:, kvh, (i - 1) * 128:i * 128], Vloc[:, i - 1, kvh * 128:(kvh + 1) * 128], None,
                                 acc, den, b_PS[2], b_PS[3], False, False, [b_Kloc], [b_Vloc])
                    key_tile(qrhs, 512, Kloc[:, kvh, i * 128:(i + 1) * 128], Vloc[:, i, kvh * 128:(kvh + 1) * 128],
                             (ident_bf[:, :], causalT.rearrange("p h t -> p (h t)"), [b_const, b_causal]),
                             acc, den, b_PS[2], b_PS[3], False, True, [b_Kloc], [b_Vloc])
                    kb.ve("reciprocal", (rden[:, :], den), [b_PS[3]], [b_rden])
                    kb.ve("tensor_tensor", (qrhs, acc.rearrange("p (h t) -> p h t", t=128), rden[:, :].rearrange("p (h t) -> p h t", t=128),
                                            ALU.mult), [b_PS[2], b_rden], [b_oT[4 * kvh + x] for x in range(4)])
            if with_cache:
                sample_attention(key_tile, fme)
            S.op("gpsimd", lambda e: e.memset(fence_t[:, 0:1], 0.0), [], [W0, W1] + ring_bufs)
            S.serial_buf = None

        qs_f32 = sb("qs_f32", [128, NS, H, ST], F32)
        b_qs = buf()

        def sample_attention(key_tile, fme):
            kpg = [carve(WR_t[0], i * 2048, [128, 512], F32) for i in range(2)]
            vpg = [carve(WR_t[0], 4096 + i * 2048, [128, 512], F32) for i in range(2)]
            kpb = [carve(WR_t[0], 8192 + i * 1024, [128, 512], BF16) for i in range(2)]
            vpb = [carve(WR_t[0], 10240 + i * 1024, [128, 512], BF16) for i in range(2)]
            KTs = [carve(WR_t[0], 12288 + i * 1024, [128, 4, 128], BF16) for i in range(2)]
            idx = carve(WR_t[0], 14336, [128, 64], I32)
            ptb = carve(WR_t[0], 14592, [128, 64], I32)
            iota_p = carve(WR_t[0], 14848, [128, 1], I32)
            km_s = carve(WR_t[0], 14852, [128, 4, 32], F32)
            bT_s = carve(WR_t[0], 15364, [32, 64], BF16)
            gm_s = carve(WR_t[0], 15492, [16, 4, 32], F32)
            sel_s = carve(WR_t[0], 16004, [16, 4, 32], BF16)
            mx_s = carve(WR_t[0], 16260, [16, 8], F32)
            th_s = carve(WR_t[0], 16292, [16, 1], F32)
            b_kpg, b_vpg, b_kpb, b_vpb, b_KTs = [[buf(), buf()] for _ in range(5)]
            b_idx, b_kms, b_bTs, b_gs_ = buf(), buf(), buf(), buf()
            for b_ in b_kpg + b_vpg + b_kpb + b_vpb + b_KTs + [b_idx, b_kms, b_bTs, b_gs_]:
                b_.w = fme
                ring_bufs.append(b_)
            d_kg = [S.dma_sem("kg0"), S.dma_sem("kg1")]
            d_vg = [S.dma_sem("vg0"), S.dma_sem("vg1")]
            d_pt = S.dma_sem("pt")
            kb.ve("iota", (iota_p, [[0, 1]]), [], [b_idx], eng="gpsimd", base=0, channel_multiplier=1)

            def gather(dst, bdst, dsem, src_h, pg):
                S.op("gpsimd", lambda e: e.indirect_dma_start(out=dst, out_offset=None, in_=src_h.ap(),
                                                              in_offset=bass.IndirectOffsetOnAxis(ap=idx[:, pg:pg + 1], axis=0)),
                     [b_idx], [bdst], dsem=dsem)

            for j in range(NS):
                kb.dma("sync", ptb[:, :], pt_h[j:j + 1, :].partition_broadcast(128), [], [b_idx], d_pt)
                kb.ve("tensor_scalar", (idx[:, :], ptb[:, :], 128, iota_p[:, 0:1], ALU.mult, ALU.add), [b_idx], [b_idx])
                for pg in range(64):
                    i = pg % 2
                    gather(kpg[i][:, :], b_kpg[i], d_kg[i], ck_h, pg)
                    kb.act(kpb[i][:, :], kpg[i][:, :], AF.Copy, [b_kpg[i]], [b_kpb[i]])
                    for kvh in range(4):
                        col = kvh * 32 + pg // 2
                        kb.mm(PS[4][:, col:col + 1], kpb[i][:, kvh * 128:(kvh + 1) * 128], ones_bf[:, 0:1],
                              pg == 0 and kvh == 0, pg == 63 and kvh == 3, [b_kpb[i], b_const], [b_PS[4]])
                kb.ve("tensor_scalar_mul", (km_s.rearrange("p a b -> p (a b)"), PS[4][:, 0:128], 1.0 / 256), [b_PS[4]], [b_kms])
                for kvh in range(4):
                    kb.mm(PS[5][0:16, kvh * 32:(kvh + 1) * 32], qs_f32[:, j, 4 * kvh:4 * kvh + 4, :].rearrange("p h t -> p (h t)"),
                          km_s[:, kvh, :], True, True, [b_qs, b_kms], [b_PS[5]])
                kb.ve("tensor_copy", (gm_s, PS[5][0:16, 0:128].rearrange("p (a b) -> p a b", b=32)), [b_PS[5]], [b_gs_])
                for kvh in range(4):
                    kb.ve("max", (mx_s[:, :], gm_s[:, kvh, :]), [b_gs_], [b_gs_])
                    kb.ve("tensor_scalar", (sel_s[:, kvh, :], gm_s[:, kvh, :], mx_s[:, 2:3], None, ALU.is_ge), [b_gs_], [b_gs_])
                kb.ve("tensor_scalar", (sel_s[:, :, :], sel_s[:, :, :], -1.0, -NEG, ALU.add, ALU.mult), [b_gs_], [b_gs_])
                for kvh in range(4):
                    kb.tr(PT[0][0:32, kvh * 16:(kvh + 1) * 16], sel_s[:, kvh, :], ident_bf[0:16, 0:16], [b_gs_, b_const], [b_PT[0]])
                kb.ve("tensor_copy", (bT_s, PT[0][0:32, 0:64]), [b_PT[0]], [b_bTs])
                for pg in range(64):
                    i = pg % 2
                    n = pg // 2
                    gather(kpg[i][:, :], b_kpg[i], d_kg[i], ck_h, pg)
                    gather(vpg[i][:, :], b_vpg[i], d_vg[i], cvv_h, pg)
                    kb.act(kpb[i][:, :], kpg[i][:, :], AF.Copy, [b_kpg[i]], [b_kpb[i]])
                    kb.ve("tensor_copy", (vpb[i][:, :], vpg[i][:, :]), [b_vpg[i]], [b_vpb[i]])
                    for kvh in range(4):
                        kb.tr(PT[1][:, kvh * 128:(kvh + 1) * 128], kpb[i][:, kvh * 128:(kvh + 1) * 128], ident_bf[:, :],
                              [b_kpb[i], b_const], [b_PT[1]])
                    kb.ve("tensor_copy", (KTs[i], PT[1][:, 0:512].rearrange("p (a b) -> p a b", b=128)), [b_PT[1]], [b_KTs[i]])
                    si = pg % 2
                    sc, scb = PS[si][:, 0:64], b_PS[si]
                    for kvh in range(4):
                        qr = oT[:, 4 * kvh:4 * kvh + 4, LP + 4 * j:LP + 4 * j + 4]
                        kb.mm(sc[:, kvh * 16:(kvh + 1) * 16], KTs[i][:, kvh, :], qr, True, False, [b_KTs[i]] + b_oT, [scb])
                        kb.mm(sc[:, kvh * 16:(kvh + 1) * 16], Esel[:, n, :], bT_s[:, kvh * 16:(kvh + 1) * 16], False, True,
                              [b_causal, b_bTs], [scb])
                    pi = pg % NPT
                    kb.act(Pt[pi][:, 0:64], sc, AF.Exp, [scb], [b_Pt[pi]], scale=SCALE)
                    for kvh in range(4):
                        kb.mm(PS[2][:, kvh * 16:(kvh + 1) * 16], vpb[i][:, kvh * 128:(kvh + 1) * 128], Pt[pi][:, kvh * 16:(kvh + 1) * 16],
                              pg == 0 and kvh == 0, False, [b_vpb[i], b_Pt[pi]], [b_PS[2]])
                    kb.mm(PS[3][:, 0:64], ones_bf[:, :], Pt[pi][:, 0:64], pg == 0, False, [b_const, b_Pt[pi]], [b_PS[3]])
                for kvh in range(4):
                    qr = oT[:, 4 * kvh:4 * kvh + 4, LP + 4 * j:LP + 4 * j + 4]
                    key_tile(qr, 16, Kloc[:, kvh, LP:NT], Vloc[0:16, 8, kvh * 128:(kvh + 1) * 128],
                             (ident_bf[0:16, 0:16], cm_s[:, j, :], [b_const, b_causal]),
                             PS[2][:, kvh * 16:(kvh + 1) * 16], PS[3][:, kvh * 16:(kvh + 1) * 16], b_PS[2], b_PS[3],
                             False, kvh == 3, [b_Kloc], [b_Vloc], kparts=16, scbank=5)
                kb.ve("reciprocal", (rden[:, 0:64], PS[3][:, 0:64]), [b_PS[3]], [b_rden])
                for kvh in range(4):
                    qr = oT[:, 4 * kvh:4 * kvh + 4, LP + 4 * j:LP + 4 * j + 4]
                    kb.ve("tensor_tensor", (qr, PS[2][:, kvh * 16:(kvh + 1) * 16].rearrange("p (h t) -> p h t", t=4),
                                            rden[:, kvh * 16:(kvh + 1) * 16].rearrange("p (h t) -> p h t", t=4), ALU.mult),
                          [b_PS[2], b_rden], [b_oT[4 * kvh + x] for x in range(4)])

        RING_SCRATCH = []

        def make_wo(gq):
            def load():
                return wload([(lambda w: w[:, :].rearrange("p (k n) -> p k n", k=KC), w_o_v[:, :, gq * 512:(gq + 1) * 512])])

            def run(slot):
                wv = WR[slot][:, :].rearrange("p (k n) -> p k n", k=KC)
                for oo in range(4):
                    o = gq * 4 + oo
                    pp = psparts(o % 2, PARTS)
                    for pi, (c0, c1) in enumerate(PARTS):
                        pap, pb = pp[pi]
                        for k in range(KC):
                            kb.mm(pap, wv[:, k, oo * 128:(oo + 1) * 128], oT[:, k, c0:c1], k == 0, k == KC - 1, [b_WR[slot], b_oT[k]], [pb])
                        kb.ve("tensor_tensor", (A_h[:, o, c0:c1], A_h[:, o, c0:c1], pap, ALU.add), [b_hT[o], pb], [b_hT[o]])
            return load, run

        def write_y(_):
            for t in range(9):
                rows = 128 if t < 8 else NS * ST
                i = t % 2 if t < 8 else 2
                for cg in range(4):
                    for cc in range(4):
                        c = cg * 4 + cc
                        kb.tr(PS[i][0:rows, cc * 128:(cc + 1) * 128], A_h[:, c, t * 128:t * 128 + rows], ident_f[:, :],
                              [b_hT[c], b_const], [b_PS[i]])
                    kb.act(kstage[i][0:rows, :], PS[i][0:rows, :], AF.Copy, [b_PS[i]], [b_kstage[i]])
                    dst = o_yp[t * 128:(t + 1) * 128, cg * 512:(cg + 1) * 512] if t < 8 else o_ys[:, cg * 512:(cg + 1) * 512]
                    kb.dma("sync", dst, kstage[i][0:rows, :], [b_kstage[i]], [], d_kst[i])
            out_bufs.extend(b_kstage)

        if upto >= 6:
            jobs.append((None, lambda _: norm_to_xT(V_NMB)))
            jobs.append((None, load_kmean))
            for gq in range(4):
                jobs.append(make_q(gq))
            if cfg.get("dbg2"):
                def dbg2(_):
                    d_d2 = S.dma_sem("dbg2")
                    kb.dma("sync", o_km.ap(), kmean_all.rearrange("p a b -> p (a b)"), [b_km], [], d_d2)
                    kb.dma("gpsimd", o_bias.ap(), biasS[:, :, :, :].rearrange("p a b c -> p (a b c)"), [b_biasS], [], d_d2)
                    out_bufs.extend([b_km, b_biasS])
                jobs.append((None, dbg2))
            jobs.append(("NOPREFETCH", attention))
            for gq in range(4):
                jobs.append(make_wo(gq))
        if upto >= 7:
            add_ffn(1)
            jobs.append((None, write_y))

        def dbg_out(_):
            d_dbg = S.dma_sem("dbg")
            which = cfg.get("dbg", "xT")
            if which == "xT":
                kb.ve("tensor_copy", (A_h[:, :, :], xT[:, :, :]), [b_xT] + b_T + [b_vtok, b_kotok, b_attm, b_Sbf, b_qt, b_kt, b_qi, b_ko, b_vt, b_sq, b_gate], b_T)
            elif which == "hT":
                pass
            elif which == "oT":
                kb.ve("tensor_copy", (A_h[:, :, :], oT[:, :, :]), b_oT + b_T + [b_vtok, b_kotok, b_attm, b_Sbf, b_qt, b_kt, b_qi, b_ko, b_vt, b_sq, b_gate], b_T)
            kb.dma("sync", o_dbg.ap(), A_h[:, :, :], b_T + b_hT, [], d_dbg)
            out_bufs.extend(b_T + b_hT)

        if cfg.get("dbg"):
            jobs.append((None, dbg_out))

        run_jobs()
        out_bufs.extend([b_sfin])
        S.final_wait("sync", out_bufs)
        S.replay()
    return nc


def pack_vecs(inp):
    v = np.zeros((128, 128), np.float32)
    v[0:32] = np.asarray(inp["lb_logits"], np.float32).reshape(32, 128)
    v[32:48] = np.asarray(inp["norm_mix_a"], np.float32).reshape(16, 128)
    v[48:80] = np.asarray(inp["norm_ffn"], np.float32).reshape(32, 128)
    v[80:96] = np.asarray(inp["norm_kv"], np.float32).reshape(16, 128)
    v[96:112] = np.asarray(inp["norm_mix_b"], np.float32).reshape(16, 128)
    v[112] = np.asarray(inp["onorm_a"], np.float32).reshape(128)
    v[113] = np.asarray(inp["k_norm"], np.float32).reshape(128)
    v[114] = np.asarray(inp["q_norm"], np.float32).reshape(128)
    return v


def make_in_maps(inp, cfg=None):
    vecs = pack_vecs(inp)
    maps = []
    for c in range(NCORES):
        oh = np.zeros((128, 8), np.float32)
        oh[:, c] = 1.0
        m = {
            "xp": np.ascontiguousarray(inp["x_prompt"][0, c * LP:(c + 1) * LP]),
            "xs": np.ascontiguousarray(inp["x_sample"][c * NS:(c + 1) * NS].reshape(NS * ST, D)),
            "s0": np.ascontiguousarray(inp["state_hgrn"][0, c * NS:(c + 1) * NS]),
            "vecs": vecs,
            "onehot": oh,
            "w_in": np.ascontiguousarray(inp["w_in_a"][0]),
            "w_out": np.ascontiguousarray(inp["w_out_a"][0]),
            "w_gu": np.asarray(inp["w_gate_up"]),
            "w_dn": np.asarray(inp["w_down"]),
            "w_kv": np.asarray(inp["w_kv"]),
            "w_q": np.ascontiguousarray(inp["w_q_b"][0]),
            "w_o": np.ascontiguousarray(inp["w_o_b"][0]),
        }
        pm = np.full((128, 8, 32), -1e30, np.float32)
        for i in range(8):
            pm[:, i, :4 * c + i // 2] = 0.0
        m["pastmask"] = pm.reshape(128, 256)
        if (cfg or {}).get("npages"):
            ptc = np.asarray(inp["page_table"][c * NS:(c + 1) * NS]).reshape(-1)
            m["cache_k"] = np.ascontiguousarray(np.asarray(inp["cache_k"])[ptc]).reshape(256 * 128, 512)
            m["cache_v"] = np.ascontiguousarray(np.asarray(inp["cache_v"])[ptc]).reshape(256 * 128, 512)
            m["pt"] = np.arange(256, dtype=np.int32).reshape(NS, 64)
        elif not (cfg or {}).get("nocache"):
            m["cache_k"] = np.asarray(inp["cache_k"]).reshape(2560 * 128, 512)
            m["cache_v"] = np.asarray(inp["cache_v"]).reshape(2560 * 128, 512)
            m["pt"] = np.ascontiguousarray(inp["page_table"][c * NS:(c + 1) * NS]).astype(np.int32)
        maps.append(m)
    return maps


def kernel(**inputs):
    inp = {k: np.asarray(v) for k, v in inputs.items()}
    nc = build()
    res = run_bass_kernel_spmd(nc, make_in_maps(inp), core_ids=list(range(NCORES)))
    r = res.results
    cat = lambda name: np.concatenate([np.asarray(r[c][name]) for c in range(NCORES)], axis=0)
    y_prompt = cat("o_yp").reshape(1, NCORES * LP, D).astype(np.float32)
    y_sample = cat("o_ys").reshape(NCORES * NS, ST, D).astype(np.float32)
    s_prompt = np.asarray(r[NCORES - 1]["o_sp"]).reshape(1, 1, H, 128, 128).astype(np.float32)
    s_sample = cat("o_ss").reshape(1, NCORES * NS, H, 128, 128).astype(np.float32)
    k_prompt = cat("o_kp").reshape(1, NCORES * LP, 4, 128).astype(np.float32)
    v_prompt = cat("o_vp").reshape(1, NCORES * LP, 4, 128).astype(np.float32)
    k_sample = cat("o_ks").reshape(NCORES * NS, ST, 4, 128).astype(np.float32)
    v_sample = cat("o_vs").reshape(NCORES * NS, ST, 4, 128).astype(np.float32)
    return (y_prompt, y_sample, s_prompt, s_sample, k_prompt, v_prompt, k_sample, v_sample)
```
